# Optimizing a Trainium2 kernel written in Bass

```python
import math
import jax
import jax.numpy as jnp
from jax import lax
import numpy as np

D_MODEL = 4096
BATCH = 1
SEQ = 8192
DEPTH = 4

GRID_W = 64
CTX_LEN = 256
N_MIXERS = 3
N_SSD_LAYERS = (DEPTH + 2) // 3
N_S5_LAYERS = (DEPTH + 1) // 3
N_DIFF_LAYERS = DEPTH // 3
RMS_EPS = 1e-6

FFN_DIM = D_MODEL
FFN_CONV = 3

SSD_D_INNER = 2 * D_MODEL
SSD_HEAD_DIM = 64
SSD_HEADS = SSD_D_INNER // SSD_HEAD_DIM
SSD_GROUPS = 8
SSD_HEADS_PER_GROUP = SSD_HEADS // SSD_GROUPS
SSD_STATE = 128
SSD_GN = SSD_GROUPS * SSD_STATE
SSD_CONV = 5
SSD_CONV_CH = SSD_D_INNER + 2 * SSD_GN
SSD_IN_COLS = 2 * SSD_D_INNER + 2 * SSD_GN + 2 * SSD_HEADS
SSD_CHUNK = 128

S5_GROUP_CH = 16
S5_GROUPS = D_MODEL // S5_GROUP_CH
S5_STATE = 64
S5_BLOCK_GROUPS = 32
S5_N_BLOCKS = S5_GROUPS // S5_BLOCK_GROUPS

ATT_HEAD_DIM = 128
ATT_HEADS = D_MODEL // (2 * ATT_HEAD_DIM)
ATT_SCALE = ATT_HEAD_DIM ** -0.5
ATT_Q_BLOCK = 128
ROPE_BASE = 10000.0

kernel_name = "hybrid_ssd_s5_diffattn_prefix_dit"


def rmsnorm(x, gain):
    xf = x.astype(jnp.float32)
    y = xf * lax.rsqrt(jnp.mean(xf * xf, axis=-1, keepdims=True) + RMS_EPS)
    return (y * gain.astype(jnp.float32)).astype(x.dtype)


def group_rmsnorm(y, gain, groups):
    shp = y.shape
    yg = y.astype(jnp.float32).reshape(*shp[:-1], groups, shp[-1] // groups)
    yg = yg * lax.rsqrt(jnp.mean(yg * yg, axis=-1, keepdims=True) + RMS_EPS)
    return yg.reshape(shp) * gain.astype(jnp.float32)


def modulate(h, shift, scale):
    return h * (1.0 + scale) + shift


def orient(t, reverse):
    return t[:, ::-1] if reverse else t


def dwconv_centred(u, w, b):
    k = w.shape[0]
    r = k // 2
    n = u.shape[1]
    up = jnp.pad(u, ((0, 0), (r, r), (0, 0)))
    out = b + up[:, 0:n] * w[0]
    for j in range(1, k):
        out = out + up[:, j:j + n] * w[j]
    return out


def conv_ffn(u, w_up, conv_w, conv_b, w_down):
    a = dwconv_centred(u @ w_up, conv_w, conv_b)
    gate, val = jnp.split(a, 2, axis=-1)
    return (jax.nn.silu(gate) * val) @ w_down


def ssd_chunked_scan(xs, dt, a, bm, cm, h0):
    f32 = jnp.float32
    bsz, n = xs.shape[:2]
    nc = n // SSD_CHUNK
    xs, bm, cm = xs.astype(f32), bm.astype(f32), cm.astype(f32)

    def chunks(t):
        return t.reshape(bsz, nc, SSD_CHUNK, *t.shape[2:])

    xdt = chunks(xs * dt[..., None])
    bc, cc = chunks(bm), chunks(cm)
    da = jnp.moveaxis(chunks(dt * a), 2, -1)
    cum = jnp.cumsum(da, axis=-1)
    lower = jnp.tril(jnp.ones((SSD_CHUNK, SSD_CHUNK), dtype=bool))
    decay_in = jnp.exp(jnp.where(lower, cum[..., :, None] - cum[..., None, :], -jnp.inf))
    cb = jnp.einsum('bclgn,bcsgn->bcgls', cc, bc)
    y_diag = jnp.einsum('bcgkls,bcsgkp->bclgkp', cb[:, :, :, None] * decay_in, xdt)
    decay_to_end = jnp.exp(cum[..., -1:] - cum)
    chunk_states = jnp.einsum('bclgn,bcgkl,bclgkp->bcgkpn', bc, decay_to_end, xdt)
    chunk_decay = jnp.exp(cum[..., -1])

    def carry_state(h, inp):
        s, d = inp
        return h * d[..., None, None] + s, h

    h_last, h_enter = lax.scan(carry_state, h0.astype(f32),
                               (jnp.moveaxis(chunk_states, 1, 0), jnp.moveaxis(chunk_decay, 1, 0)))
    h_enter = jnp.moveaxis(h_enter, 0, 1)
    y_off = jnp.einsum('bclgn,bcgkpn->bclgkp', cc, h_enter) * jnp.moveaxis(jnp.exp(cum), -1, 2)[..., None]
    y = (y_diag + y_off).reshape(bsz, n, *xs.shape[2:])
    return y, h_last


def ssd_mixer(u_lat, u_ctx, w_in, conv_w, conv_b, a_log, dt_bias, d_skip, norm_w, w_out, want_ctx):
    f32 = jnp.float32
    a = -jnp.exp(a_log.astype(f32)).reshape(2, SSD_GROUPS, SSD_HEADS_PER_GROUP)
    dt_b = dt_bias.astype(f32).reshape(2, SSD_GROUPS, SSD_HEADS_PER_GROUP)
    d_h = d_skip.astype(f32).reshape(SSD_GROUPS, SSD_HEADS_PER_GROUP, 1)

    def project(u):
        bsz, n, _ = u.shape
        zxbcdt = u @ w_in
        z = zxbcdt[..., :SSD_D_INNER]
        xbc = jax.nn.silu(dwconv_centred(zxbcdt[..., SSD_D_INNER:SSD_D_INNER + SSD_CONV_CH], conv_w, conv_b))
        xs = xbc[..., :SSD_D_INNER].reshape(bsz, n, SSD_GROUPS, SSD_HEADS_PER_GROUP, SSD_HEAD_DIM)
        bm = xbc[..., SSD_D_INNER:SSD_D_INNER + SSD_GN].reshape(bsz, n, SSD_GROUPS, SSD_STATE)
        cm = xbc[..., SSD_D_INNER + SSD_GN:].reshape(bsz, n, SSD_GROUPS, SSD_STATE)
        dt_raw = zxbcdt[..., SSD_D_INNER + SSD_CONV_CH:].reshape(bsz, n, 2, SSD_GROUPS, SSD_HEADS_PER_GROUP)
        dt = jax.nn.softplus(dt_raw.astype(f32) + dt_b)
        return z, xs, bm, cm, dt

    def scan_both(xs, bm, cm, dt, h0_f, h0_b):
        y_f, hf = ssd_chunked_scan(xs, dt[:, :, 0], a[0], bm, cm, h0_f)
        y_b, hb = ssd_chunked_scan(xs[:, ::-1], dt[:, ::-1, 1], a[1], bm[:, ::-1], cm[:, ::-1], h0_b)
        return y_f + y_b[:, ::-1] + d_h * xs.astype(f32), hf, hb

    def finish(y, z):
        bsz, n = z.shape[:2]
        y = y.reshape(bsz, n, SSD_D_INNER) * jax.nn.silu(z.astype(f32))
        y = group_rmsnorm(y, norm_w, SSD_GROUPS)
        return y.astype(z.dtype) @ w_out

    zc, xc, bc, cc, dtc = project(u_ctx)
    zl, xl, bl, cl, dtl = project(u_lat)
    h0 = jnp.zeros((u_lat.shape[0], SSD_GROUPS, SSD_HEADS_PER_GROUP, SSD_HEAD_DIM, SSD_STATE), f32)
    yc, st_f, st_b = scan_both(xc, bc, cc, dtc, h0, h0)
    yl, _, _ = scan_both(xl, bl, cl, dtl, st_f, st_b)
    out_l = finish(yl, zl)
    out_c = finish(yc, zc) if want_ctx else None
    return out_l, out_c


def complex_affine_combine(e1, e2):
    a1r, a1i, b1r, b1i = e1
    a2r, a2i, b2r, b2i = e2
    return (a2r * a1r - a2i * a1i,
            a2r * a1i + a2i * a1r,
            a2r * b1r - a2i * b1i + b2r,
            a2r * b1i + a2i * b1r + b2i)


def diag_ssm_scan(u, ar, ai, bbr, bbi, cr, ci, h0r, h0i):
    uf = u.astype(jnp.float32)
    bur = jnp.einsum('btgc,gpc->btgp', uf, bbr)
    bui = jnp.einsum('btgc,gpc->btgp', uf, bbi)
    bur = bur.at[:, 0].add(ar * h0r - ai * h0i)
    bui = bui.at[:, 0].add(ar * h0i + ai * h0r)
    shape = bur.shape
    _, _, sr, si = lax.associative_scan(
        complex_affine_combine,
        (jnp.broadcast_to(ar, shape), jnp.broadcast_to(ai, shape), bur, bui), axis=1)
    y = jnp.einsum('gcp,btgp->btgc', cr, sr) - jnp.einsum('gcp,btgp->btgc', ci, si)
    return y, sr[:, -1], si[:, -1]


def s5_mixer(u_lat, u_ctx, lam_re, lam_im, log_step, b_re, b_im, c_re, c_im, d_skip, glu_w, glu_b, want_ctx):
    f32 = jnp.float32
    step = jnp.exp(log_step.astype(f32))[..., None]
    lr, li = lam_re.astype(f32), lam_im.astype(f32)
    mag = jnp.exp(lr * step)
    ar, ai = mag * jnp.cos(li * step), mag * jnp.sin(li * step)
    den = lr * lr + li * li
    kr = ((ar - 1.0) * lr + ai * li) / den
    ki = (ai * lr - (ar - 1.0) * li) / den
    br, bi = b_re.astype(f32), b_im.astype(f32)
    bbr = kr[..., None] * br - ki[..., None] * bi
    bbi = kr[..., None] * bi + ki[..., None] * br
    cr, ci = c_re.astype(f32), c_im.astype(f32)

    def u_blocks(u):
        bsz, n, _ = u.shape
        return jnp.moveaxis(u.reshape(bsz, n, S5_N_BLOCKS, S5_BLOCK_GROUPS, S5_GROUP_CH), 2, 0)

    def p_blocks(p):
        return jnp.moveaxis(p.reshape(2, S5_N_BLOCKS, S5_BLOCK_GROUPS, *p.shape[2:]), 1, 0)

    def run_block(args):
        ul, uc, ar_b, ai_b, bbr_b, bbi_b, cr_b, ci_b = args
        zero = jnp.zeros((ul.shape[0], S5_BLOCK_GROUPS, S5_STATE), f32)
        ys_l, ys_c = [], []
        for d in range(2):
            rev = d == 1
            pars = (ar_b[d], ai_b[d], bbr_b[d], bbi_b[d], cr_b[d], ci_b[d])
            yc_d, sr, si = diag_ssm_scan(orient(uc, rev), *pars, zero, zero)
            yl_d, _, _ = diag_ssm_scan(orient(ul, rev), *pars, sr, si)
            ys_l.append(orient(yl_d, rev))
            ys_c.append(orient(yc_d, rev))
        return ys_l[0] + ys_l[1], ys_c[0] + ys_c[1]

    yl_b, yc_b = lax.map(run_block, (u_blocks(u_lat), u_blocks(u_ctx), p_blocks(ar), p_blocks(ai),
                                     p_blocks(bbr), p_blocks(bbi), p_blocks(cr), p_blocks(ci)))

    def finish(yb, u):
        bsz, n, _ = u.shape
        y = jnp.moveaxis(yb, 0, 2).reshape(bsz, n, D_MODEL) + d_skip.astype(f32) * u.astype(f32)
        y = jax.nn.gelu(y).astype(u.dtype)
        z = y @ glu_w + glu_b
        return (z[..., :D_MODEL] * jax.nn.sigmoid(z[..., D_MODEL:])).astype(u.dtype)

    out_l = finish(yl_b, u_lat)
    out_c = finish(yc_b, u_ctx) if want_ctx else None
    return out_l, out_c


def axial_rope_tables(n):
    f32 = jnp.float32
    rows = n // GRID_W
    row = jnp.repeat(jnp.arange(rows, dtype=f32), GRID_W)
    col = jnp.tile(jnp.arange(GRID_W, dtype=f32), rows)
    n_freq = ATT_HEAD_DIM // 4
    inv_freq = ROPE_BASE ** (-jnp.arange(n_freq, dtype=f32) / n_freq)
    ang_r = row[:, None] * inv_freq
    ang_c = col[:, None] * inv_freq
    return jnp.cos(ang_r), jnp.sin(ang_r), jnp.cos(ang_c), jnp.sin(ang_c)


def rotate_pairs(x, cos, sin):
    x1, x2 = jnp.split(x, 2, axis=-1)
    cos, sin = cos[None, :, None, :], sin[None, :, None, :]
    return jnp.concatenate([x1 * cos - x2 * sin, x1 * sin + x2 * cos], axis=-1)


def apply_axial_rope(x, cos_r, sin_r, cos_c, sin_c):
    xr, xc = jnp.split(x, 2, axis=-1)
    return jnp.concatenate([rotate_pairs(xr, cos_r, sin_r), rotate_pairs(xc, cos_c, sin_c)], axis=-1).astype(x.dtype)


def diff_softmax_attend(q, k, v, lam):
    bsz, nq = q.shape[:2]
    nk = k.shape[1]
    s = jnp.einsum('bqhd,bkhd->bhqk', q, k).astype(jnp.float32) * ATT_SCALE
    p = jax.nn.softmax(s, axis=-1).reshape(bsz, ATT_HEADS, 2, nq, nk)
    a = p[:, :, 0] - lam * p[:, :, 1]
    return jnp.einsum('bhqk,bkhe->bqhe', a, v.astype(jnp.float32))


def diff_attn_mixer(u_lat, u_ctx, w_q, w_k, w_v, w_o, q_gain, k_gain, lam_q1, lam_k1, lam_q2, lam_k2,
                    sub_gain, lambda_init, want_ctx):
    f32 = jnp.float32
    bsz, seq, _ = u_lat.shape

    def qkv(u):
        n = u.shape[1]
        q = rmsnorm((u @ w_q).reshape(bsz, n, 2 * ATT_HEADS, ATT_HEAD_DIM), q_gain)
        k = rmsnorm((u @ w_k).reshape(bsz, n, 2 * ATT_HEADS, ATT_HEAD_DIM), k_gain)
        v = (u @ w_v).reshape(bsz, n, ATT_HEADS, 2 * ATT_HEAD_DIM)
        return q, k, v

    ql, kl, vl = qkv(u_lat)
    qc, kc, vc = qkv(u_ctx)
    rope = axial_rope_tables(seq)
    ql, kl = apply_axial_rope(ql, *rope), apply_axial_rope(kl, *rope)
    lam = (jnp.exp(jnp.sum(lam_q1.astype(f32) * lam_k1.astype(f32)))
           - jnp.exp(jnp.sum(lam_q2.astype(f32) * lam_k2.astype(f32))) + lambda_init)

    k_all = jnp.concatenate([kc, kl], axis=1)
    v_all = jnp.concatenate([vc, vl], axis=1)
    nb = seq // ATT_Q_BLOCK
    q_blocks = jnp.moveaxis(ql.reshape(bsz, nb, ATT_Q_BLOCK, 2 * ATT_HEADS, ATT_HEAD_DIM), 1, 0)
    o_l = lax.map(lambda qb: diff_softmax_attend(qb, k_all, v_all, lam), q_blocks)
    o_l = jnp.moveaxis(o_l, 0, 1).reshape(bsz, seq, ATT_HEADS, 2 * ATT_HEAD_DIM)

    def finish(o):
        o = rmsnorm(o, sub_gain) * (1.0 - lambda_init)
        return o.reshape(bsz, o.shape[1], D_MODEL).astype(u_lat.dtype) @ w_o

    out_l = finish(o_l)
    out_c = finish(diff_softmax_attend(qc, kc, vc, lam)) if want_ctx else None
    return out_l, out_c


def setup_inputs(seed: int = 0) -> dict:
    key = jax.random.key(seed)
    keys = jax.random.split(key, 48)
    counter = [0]
    f32 = jnp.float32

    def nk():
        k = keys[counter[0]]
        counter[0] += 1
        return k

    def normal(shape, scale):
        return scale * jax.random.normal(nk(), shape, f32)

    def gain(shape):
        return 1.0 + 0.1 * jax.random.normal(nk(), shape, f32)

    def log_uniform(shape, lo, hi):
        return jax.random.uniform(nk(), shape, f32, math.log(lo), math.log(hi))

    D, F = D_MODEL, FFN_DIM
    inp = {}
    inp['x'] = normal((BATCH, SEQ, D), 1.0)
    inp['c'] = normal((BATCH, D), 1.0)
    inp['ctx'] = normal((BATCH, CTX_LEN, D), 1.0)
    inp['c_ctx'] = normal((D,), 1.0)
    inp['mod_w'] = normal((DEPTH, D, 6 * D), 0.5 * D ** -0.5)
    inp['mod_b'] = normal((DEPTH, 6 * D), 0.02)
    inp['norm_mix'] = gain((DEPTH, D))
    inp['norm_ffn'] = gain((DEPTH, D))
    inp['ffn_up'] = normal((DEPTH, D, 2 * F), D ** -0.5)
    inp['ffn_conv_w'] = normal((DEPTH, FFN_CONV, 2 * F), FFN_CONV ** -0.5)
    inp['ffn_conv_b'] = normal((DEPTH, 2 * F), 0.02)
    inp['ffn_down'] = normal((DEPTH, F, D), F ** -0.5)
    na = N_SSD_LAYERS
    inp['ssd_w_in'] = normal((na, D, SSD_IN_COLS), D ** -0.5)
    inp['ssd_conv_w'] = normal((na, SSD_CONV, SSD_CONV_CH), SSD_CONV ** -0.5)
    inp['ssd_conv_b'] = normal((na, SSD_CONV_CH), 0.02)
    inp['ssd_a_log'] = jnp.log(jax.random.uniform(nk(), (na, 2, SSD_HEADS), f32, 1.0, 16.0))
    dt0 = jnp.exp(log_uniform((na, 2, SSD_HEADS), 0.001, 0.1))
    inp['ssd_dt_bias'] = dt0 + jnp.log(-jnp.expm1(-dt0))
    inp['ssd_d'] = gain((na, SSD_HEADS))
    inp['ssd_norm'] = gain((na, SSD_D_INNER))
    inp['ssd_w_out'] = normal((na, SSD_D_INNER, D), SSD_D_INNER ** -0.5)
    nb = N_S5_LAYERS
    inp['s5_lam_re'] = -0.5 + normal((nb, 2, S5_GROUPS, S5_STATE), 0.01)
    inp['s5_lam_im'] = (jnp.broadcast_to(math.pi * jnp.arange(S5_STATE, dtype=f32), (nb, 2, S5_GROUPS, S5_STATE))
                        + normal((nb, 2, S5_GROUPS, S5_STATE), 0.01))
    inp['s5_log_step'] = log_uniform((nb, 2, S5_GROUPS), 0.001, 0.1)
    inp['s5_b_re'] = normal((nb, S5_GROUPS, S5_STATE, S5_GROUP_CH), (2 * S5_GROUP_CH) ** -0.5)
    inp['s5_b_im'] = normal((nb, S5_GROUPS, S5_STATE, S5_GROUP_CH), (2 * S5_GROUP_CH) ** -0.5)
    inp['s5_c_re'] = normal((nb, 2, S5_GROUPS, S5_GROUP_CH, S5_STATE), S5_STATE ** -0.5)
    inp['s5_c_im'] = normal((nb, 2, S5_GROUPS, S5_GROUP_CH, S5_STATE), S5_STATE ** -0.5)
    inp['s5_d'] = gain((nb, D))
    inp['s5_glu_w'] = normal((nb, D, 2 * D), D ** -0.5)
    inp['s5_glu_b'] = normal((nb, 2 * D), 0.02)
    nc = N_DIFF_LAYERS
    inp['da_w_q'] = normal((nc, D, D), D ** -0.5)
    inp['da_w_k'] = normal((nc, D, D), D ** -0.5)
    inp['da_w_v'] = normal((nc, D, D), D ** -0.5)
    inp['da_w_o'] = normal((nc, D, D), D ** -0.5)
    inp['da_q_norm'] = gain((nc, ATT_HEAD_DIM))
    inp['da_k_norm'] = gain((nc, ATT_HEAD_DIM))
    inp['da_lam_q1'] = normal((nc, ATT_HEAD_DIM), 0.1)
    inp['da_lam_k1'] = normal((nc, ATT_HEAD_DIM), 0.1)
    inp['da_lam_q2'] = normal((nc, ATT_HEAD_DIM), 0.1)
    inp['da_lam_k2'] = normal((nc, ATT_HEAD_DIM), 0.1)
    inp['da_sub_norm'] = gain((nc, 2 * ATT_HEAD_DIM))
    return inp


def reference(x, c, ctx, c_ctx, mod_w, mod_b, norm_mix, norm_ffn, ffn_up, ffn_conv_w, ffn_conv_b, ffn_down,
              ssd_w_in, ssd_conv_w, ssd_conv_b, ssd_a_log, ssd_dt_bias, ssd_d, ssd_norm, ssd_w_out,
              s5_lam_re, s5_lam_im, s5_log_step, s5_b_re, s5_b_im, s5_c_re, s5_c_im, s5_d, s5_glu_w, s5_glu_b,
              da_w_q, da_w_k, da_w_v, da_w_o, da_q_norm, da_k_norm, da_lam_q1, da_lam_k1, da_lam_q2, da_lam_k2,
              da_sub_norm):
    cond_lat = jax.nn.silu(c)
    cond_ctx = jax.nn.silu(c_ctx)
    for i in range(DEPTH):
        want_ctx = i < DEPTH - 1
        sh1, sc1, g1, sh2, sc2, g2 = jnp.split((cond_lat @ mod_w[i] + mod_b[i])[:, None, :], 6, axis=-1)
        csh1, csc1, cg1, csh2, csc2, cg2 = jnp.split((cond_ctx @ mod_w[i] + mod_b[i])[None, None, :], 6, axis=-1)
        h = modulate(rmsnorm(x, norm_mix[i]), sh1, sc1)
        hc = modulate(rmsnorm(ctx, norm_mix[i]), csh1, csc1)
        kind, j = i % N_MIXERS, i // N_MIXERS
        if kind == 0:
            o, oc = ssd_mixer(h, hc, ssd_w_in[j], ssd_conv_w[j], ssd_conv_b[j], ssd_a_log[j], ssd_dt_bias[j],
                              ssd_d[j], ssd_norm[j], ssd_w_out[j], want_ctx)
        elif kind == 1:
            o, oc = s5_mixer(h, hc, s5_lam_re[j], s5_lam_im[j], s5_log_step[j], s5_b_re[j], s5_b_im[j],
                             s5_c_re[j], s5_c_im[j], s5_d[j], s5_glu_w[j], s5_glu_b[j], want_ctx)
        else:
            lambda_init = 0.8 - 0.6 * math.exp(-0.3 * i)
            o, oc = diff_attn_mixer(h, hc, da_w_q[j], da_w_k[j], da_w_v[j], da_w_o[j], da_q_norm[j], da_k_norm[j],
                                    da_lam_q1[j], da_lam_k1[j], da_lam_q2[j], da_lam_k2[j], da_sub_norm[j],
                                    lambda_init, want_ctx)
        x = x + g1 * o
        hf = modulate(rmsnorm(x, norm_ffn[i]), sh2, sc2)
        x = x + g2 * conv_ffn(hf, ffn_up[i], ffn_conv_w[i], ffn_conv_b[i], ffn_down[i])
        if want_ctx:
            ctx = ctx + cg1 * oc
            hfc = modulate(rmsnorm(ctx, norm_ffn[i]), csh2, csc2)
            ctx = ctx + cg2 * conv_ffn(hfc, ffn_up[i], ffn_conv_w[i], ffn_conv_b[i], ffn_down[i])
    return x
```

```python
import contextlib
import math
import numpy as np
import ml_dtypes
import concourse.bass as bass
import concourse.mybir as mybir
from concourse.bass_utils import run_bass_kernel_spmd

F32 = mybir.dt.float32
BF16 = mybir.dt.bfloat16
AF = mybir.ActivationFunctionType
ALU = mybir.AluOpType
AX = mybir.AxisListType

NR = 8
D = 4096
KD = D // 128
TCTX = 32
CTXN = NR * TCTX
EPS = 1e-6
TLAT = 1024


def dims(TL):
    TT = TCTX + TL
    NT = CTXN + NR * TL
    return TT, NT


class Buf:
    __slots__ = ("name", "w", "r")

    def __init__(self, name=""):
        self.name = name
        self.w = None
        self.r = {}


class T:
    def __init__(self, h, name=""):
        self.h = h
        self.b = Buf(name)

    def __getitem__(self, idx):
        return self.h[idx]


class Sched:
    SEM_ROT = 30000

    def __init__(self, nc, n_dma_sems=16):
        self.nc = nc
        self.eng = {"pe": nc.tensor, "act": nc.scalar, "dve": nc.vector,
                    "pool": nc.gpsimd, "sp": nc.sync}
        self.sem, self.cnt, self.semgen = {}, {}, {}
        for k in self.eng:
            self.semgen[k] = 0
            self.sem[k] = nc.alloc_semaphore(f"s_{k}_0")
            self.cnt[k] = 0
        self.seen = {k: {} for k in self.eng}
        self.dsem = [nc.alloc_semaphore(f"d_{i}") for i in range(n_dma_sems)]
        self.dcnt = [0] * n_dma_sems
        self.dnext = 0
        self.n_inst = 0
        self.n_wait = 0
        self.stack = contextlib.ExitStack()

    def _wait(self, e, dep):
        if dep is None:
            return
        key, val = dep
        if key[0] == "e" and key[1] == e and e in ("pe", "sp"):
            return
        if self.seen[e].get(key, 0) >= val:
            return
        self.seen[e][key] = val
        self.eng[e].wait_ge(key[2], val)
        self.n_wait += 1

    def _deps(self, e, reads, writes):
        for b in reads:
            self._wait(e, b.w)
        for b in writes:
            self._wait(e, b.w)
            for k, v in b.r.items():
                self._wait(e, (k, v))

    def _mark(self, me, reads, writes):
        k, v = me
        for b in reads:
            if b.r.get(k, 0) < v:
                b.r[k] = v
        for b in writes:
            b.w = me
            b.r = {}

    def op(self, e, fn, reads=(), writes=()):
        reads = [t.b if isinstance(t, T) else t for t in reads]
        writes = [t.b if isinstance(t, T) else t for t in writes]
        self._deps(e, reads, writes)
        if self.cnt[e] >= self.SEM_ROT:
            self.semgen[e] += 1
            self.sem[e] = self.nc.alloc_semaphore(f"s_{e}_{self.semgen[e]}")
            self.cnt[e] = 0
        inst = fn(self.eng[e])
        self.cnt[e] += 1
        inst.then_inc(self.sem[e], 1)
        me = (("e", e, self.sem[e]), self.cnt[e])
        self._mark(me, reads, writes)
        self.n_inst += 1
        return inst

    def dma(self, q, out, in_, reads=(), writes=(), **kw):
        reads = [t.b if isinstance(t, T) else t for t in reads]
        writes = [t.b if isinstance(t, T) else t for t in writes]
        i = self.dnext
        self.dnext = (self.dnext + 1) % len(self.dsem)
        key = ("d", i, self.dsem[i])
        if self.dcnt[i] > 0:
            self._wait(q, (key, self.dcnt[i]))
        self._deps(q, reads, writes)
        inst = self.eng[q].dma_start(out=out, in_=in_, **kw)
        self.dcnt[i] += 16
        inst.then_inc(self.dsem[i], 16)
        me = (key, self.dcnt[i])
        self._mark(me, reads, writes)
        self.n_inst += 1
        return inst

    def barrier(self):
        for e in self.eng:
            for e2 in self.eng:
                if e2 != e and self.cnt[e2] > 0:
                    self._wait(e, (("e", e2, self.sem[e2]), self.cnt[e2]))
            for i, s in enumerate(self.dsem):
                if self.dcnt[i] > 0:
                    self._wait(e, (("d", i, s), self.dcnt[i]))

    def sb(self, name, shape, dt, stack=None):
        st = stack if stack is not None else self.stack
        self.uid = getattr(self, "uid", 0) + 1
        h = st.enter_context(self.nc.sbuf_tensor(f"sb{self.uid}_{name}", list(shape), dt))
        return T(h, name)

    def ps(self, name, shape=(128, 512), dt=F32, stack=None):
        st = stack if stack is not None else self.stack
        self.uid = getattr(self, "uid", 0) + 1
        h = st.enter_context(self.nc.psum_tensor(f"ps{self.uid}_{name}", list(shape), dt))
        return T(h, name)


def new_prog():
    nc = bass.Bass("TRN2", target_bir_lowering=False)
    return nc, Sched(nc)


def din(nc, name, shape, dt=F32):
    return T(nc.dram_tensor(name, list(shape), dt, kind="ExternalInput"), name)


def dout(nc, name, shape, dt=F32):
    return T(nc.dram_tensor(name, list(shape), dt, kind="ExternalOutput"), name)


def dscr(nc, name, shape, dt=F32):
    return T(nc.dram_tensor(name, list(shape), dt, kind="Internal"), name)


def tok_tiles(n, maxn=512):
    k = (n + maxn - 1) // maxn
    assert n % k == 0, (n, k)
    return k, n // k


MODC = 6 * D // NR
MODB = MODC // 128


def build_mod(nlayers=4):
    nc, S = new_prog()
    cvec = din(nc, "cvec", [128, KD, 2])
    modw = din(nc, "modw", [nlayers, D, MODC])
    modb = din(nc, "modb", [128, nlayers, MODB])
    out = dout(nc, "modo", [128, nlayers, MODB, 2])
    cs = S.sb("cs", [128, KD, 2], F32)
    cb = S.sb("cb", [128, KD, 2], BF16)
    bs = S.sb("bs", [128, nlayers, MODB], F32)
    os_ = S.sb("os", [128, nlayers, MODB, 2], F32)
    wt = [S.sb(f"wt{i}", [128, KD, 512], BF16) for i in range(2)]
    pt = [S.ps(f"pt{i}") for i in range(2)]
    S.dma("sp", cs[:, :, :], cvec[:, :, :], reads=[cvec], writes=[cs])
    S.dma("sp", bs[:, :, :], modb[:, :, :], reads=[modb], writes=[bs])
    S.op("act", lambda e: e.activation(cb[:, :, :], cs[:, :, :], AF.Silu), reads=[cs], writes=[cb])
    it = 0
    for l in range(nlayers):
        wv = modw.h.ap()[l].rearrange("(k p) c -> p k c", p=128)
        for g in range(MODB // 4):
            w = wt[it % 2]
            p = pt[it % 2]
            it += 1
            S.dma("pool", w[:, :, :], wv[:, :, g * 512:(g + 1) * 512], reads=[modw], writes=[w])
            for j in range(4):
                for k in range(KD):
                    S.op("pe", lambda e: e.matmul(p[:, 2 * j:2 * j + 2], w[:, k, j * 128:(j + 1) * 128],
                                                  cb[:, k, :], start=(k == 0), stop=(k == KD - 1)),
                         reads=[w, cb], writes=[p])
            for j in range(4):
                b = g * 4 + j
                S.op("dve", lambda e: e.tensor_scalar(os_[:, l, b, :], p[:, 2 * j:2 * j + 2],
                                                      bs[:, l, b:b + 1], None, ALU.add),
                     reads=[p, bs], writes=[os_])
    S.dma("sp", out[:, :, :, :], os_[:, :, :, :], reads=[os_], writes=[out])
    S.barrier()
    return nc


NVEC = 9


def emit_norm_mod(S, xn, ss_ps, vec, hout, tsl, ntok, c0n, st):
    rstd = S.sb("rstd", [128, ntok], F32, st)
    s1 = S.sb("s1", [128, 2, KD], F32, st)
    hb = [S.sb(f"hb{i}", [128, ntok], BF16, st) for i in range(2)]
    tmp = [S.sb(f"ntmp{i}", [128, ntok], F32, st) for i in range(2)]
    S.op("dve", lambda e: e.tensor_scalar(rstd[:, :], ss_ps[:, 0:ntok], 1.0 / D, EPS, ALU.mult, ALU.add),
         reads=[ss_ps], writes=[rstd])
    S.op("act", lambda e: e.activation(rstd[:, :], rstd[:, :], AF.Sqrt), reads=[rstd], writes=[rstd])
    S.op("dve", lambda e: e.reciprocal(rstd[:, :], rstd[:, :]), reads=[rstd], writes=[rstd])
    for j in range(2):
        S.op("dve", lambda e: e.scalar_tensor_tensor(s1[:, j, :], vec[:, 3 + j, :], 1.0, vec[:, 2, :],
                                                     ALU.add, ALU.mult), reads=[vec], writes=[s1])
    for ob in range(KD):
        t = tmp[ob % 2]
        h = hb[ob % 2]
        S.op("pool", lambda e: e.tensor_tensor(t[:, :], xn[:, ob, :], rstd[:, :], ALU.mult),
             reads=[xn, rstd], writes=[t])
        if c0n > 0:
            S.op("dve", lambda e: e.tensor_scalar(h[:, 0:c0n], t[:, 0:c0n], s1[:, 0, ob:ob + 1],
                                                  vec[:, 5, ob:ob + 1], ALU.mult, ALU.add),
                 reads=[t, s1, vec], writes=[h])
        S.op("dve", lambda e: e.tensor_scalar(h[:, c0n:ntok], t[:, c0n:ntok], s1[:, 1, ob:ob + 1],
                                              vec[:, 6, ob:ob + 1], ALU.mult, ALU.add),
             reads=[t, s1, vec], writes=[h])
        S.dma("sp", hout.h.ap()[ob * 128:(ob + 1) * 128, tsl], h[:, :], reads=[h], writes=[hout])


def build_B(K, glu, TL, gemm=True, do_norm=True):
    TT, NT = dims(TL)
    nc, S = new_prog()
    KC = K // 128
    Cout = 2 * D if glu else D
    xres = din(nc, "xres", [D, TT])
    vecd = din(nc, "vec", [128, NVEC, KD])
    if gemm:
        yin = din(nc, "yin", [K, TT], BF16)
        w = din(nc, "w", [K, Cout])
    xnew = dout(nc, "xnew", [D, TT])
    hnext = dout(nc, "hnext", [D, TT], BF16) if do_norm else None
    ntile, ntok = tok_tiles(TT, 512)
    CG = 256
    vec = S.sb("vec", [128, NVEC, KD], F32)
    ones = S.sb("ones", [128, 128], F32)
    S.dma("sp", vec[:, :, :], vecd[:, :, :], reads=[vecd], writes=[vec])
    S.op("pool", lambda e: e.memset(ones[:, :], 1.0), writes=[ones])
    xn = S.sb("xn", [128, KD, ntok], F32)
    ss = S.ps("ss")
    if gemm:
        yt = S.sb("yt", [128, KC, ntok], BF16)
        ngrp = 2 if glu else 1
        wt = [S.sb(f"wt{i}", [128, KC, ngrp, CG], BF16) for i in range(2)]
        pa = [S.ps(f"pa{i}") for i in range(2)]
        pb = [S.ps(f"pb{i}") for i in range(2)] if glu else None
        sg = [S.sb(f"sg{i}", [128, ntok], F32) for i in range(2)] if glu else None
        za = [S.sb(f"za{i}", [128, ntok], F32) for i in range(2)] if glu else None
    xr = [S.sb(f"xr{i}", [128, ntok], F32) for i in range(2)]
    sq = [S.sb(f"sq{i}", [128, ntok], F32) for i in range(2)]
    wit = 0
    for tt in range(ntile):
        tsl = slice(tt * ntok, (tt + 1) * ntok)
        c0n = TCTX if tt == 0 else 0
        with contextlib.ExitStack() as st:
            if gemm:
                S.dma("sp", yt[:, :, :], yin.h.ap().rearrange("(k p) t -> p k t", p=128)[:, :, tsl],
                      reads=[yin], writes=[yt])
            for cg in range(D // CG):
                if gemm:
                    wtile = wt[wit % 2]
                    wit += 1
                    wv = w.h.ap().rearrange("(k p) c -> p k c", p=128)
                    S.dma("pool", wtile[:, :, 0, :], wv[:, :, cg * CG:(cg + 1) * CG], reads=[w], writes=[wtile])
                    if glu:
                        S.dma("pool", wtile[:, :, 1, :], wv[:, :, D + cg * CG:D + (cg + 1) * CG],
                              reads=[w], writes=[wtile])
                for j in range(CG // 128):
                    ob = cg * (CG // 128) + j
                    x_ = xr[ob % 2]
                    S.dma("sp", x_[:, :], xres.h.ap()[ob * 128:(ob + 1) * 128, tsl], reads=[xres], writes=[x_])
                    if gemm:
                        p = pa[ob % 2]
                        for k in range(KC):
                            S.op("pe", lambda e: e.matmul(p[:, 0:ntok], wtile[:, k, 0, j * 128:(j + 1) * 128],
                                                          yt[:, k, :], start=(k == 0), stop=(k == KC - 1)),
                                 reads=[wtile, yt], writes=[p])
                        src = p
                        if glu:
                            q = pb[ob % 2]
                            for k in range(KC):
                                S.op("pe", lambda e: e.matmul(q[:, 0:ntok], wtile[:, k, 1, j * 128:(j + 1) * 128],
                                                              yt[:, k, :], start=(k == 0), stop=(k == KC - 1)),
                                     reads=[wtile, yt], writes=[q])
                            s_ = sg[ob % 2]
                            z_ = za[ob % 2]
                            S.op("act", lambda e: e.activation(s_[:, :], q[:, 0:ntok], AF.Sigmoid,
                                                               bias=vec[:, 8, ob:ob + 1]),
                                 reads=[q, vec], writes=[s_])
                            S.op("dve", lambda e: e.scalar_tensor_tensor(z_[:, :], p[:, 0:ntok], vec[:, 7, ob:ob + 1],
                                                                         s_[:, :], ALU.add, ALU.mult),
                                 reads=[p, vec, s_], writes=[z_])
                            src = z_
                        if c0n > 0:
                            S.op("dve", lambda e: e.scalar_tensor_tensor(
                                xn[:, ob, 0:c0n], src[:, 0:c0n], vec[:, 0, ob:ob + 1], x_[:, 0:c0n],
                                ALU.mult, ALU.add), reads=[src, vec, x_], writes=[xn])
                        S.op("dve", lambda e: e.scalar_tensor_tensor(
                            xn[:, ob, c0n:ntok], src[:, c0n:ntok], vec[:, 1, ob:ob + 1], x_[:, c0n:ntok],
                            ALU.mult, ALU.add), reads=[src, vec, x_], writes=[xn])
                    else:
                        S.op("dve", lambda e: e.tensor_copy(xn[:, ob, :], x_[:, :]), reads=[x_], writes=[xn])
                    S.dma("sp", xnew.h.ap()[ob * 128:(ob + 1) * 128, tsl], xn[:, ob, :], reads=[xn], writes=[xnew])
                    if do_norm:
                        s2 = sq[ob % 2]
                        S.op("act", lambda e: e.activation(s2[:, :], xn[:, ob, :], AF.Square),
                             reads=[xn], writes=[s2])
                        S.op("pe", lambda e: e.matmul(ss[:, 0:ntok], ones[:, :], s2[:, :],
                                                      start=(ob == 0), stop=(ob == KD - 1)),
                             reads=[ones, s2], writes=[ss])
            if do_norm:
                emit_norm_mod(S, xn, ss, vec, hnext, tsl, ntok, c0n, st)
            S.barrier()
    S.barrier()
    return nc


def seq_ranges(t0, n, TL):
    out = []
    t = t0
    end = t0 + n
    while t < end:
        if t < CTXN:
            r, o = divmod(t, TCTX)
            ln = min(TCTX - o, end - t)
            out.append((r, o, t, ln))
        else:
            r, o = divmod(t - CTXN, TL)
            ln = min(TL - o, end - t)
            out.append((r, TCTX + o, t, ln))
        t += ln
    return out


def seq_tiles(TL, maxn=512):
    NT = CTXN + NR * TL
    tiles = [(0, CTXN)]
    t = CTXN
    while t < NT:
        n = min(maxn, NT - t)
        tiles.append((t, n))
        t += n
    return tiles


def proj_all(S, hfull, w, C, pre, TL, evac_hook=None):
    NB = C // 128
    tiles = seq_tiles(TL)
    with contextlib.ExitStack() as st:
        wt = [S.sb(f"pw{i}", [128, KD, 512], BF16, st) for i in range(2)]
        ht = [S.sb(f"ph{i}", [128, KD, 512], BF16, st) for i in range(2)]
        og = [S.sb(f"po{i}", [128, 512], F32, st) for i in range(4)]
        pp = [S.ps(f"pp{i}", stack=st) for i in range(4)]
        wv = w.h.ap().rearrange("(k p) c -> p k c", p=128)
        hit = 0
        oit = 0
        for g in range((NB + 3) // 4):
            nb = min(4, NB - g * 4)
            wtile = wt[g % 2]
            S.dma("pool", wtile[:, :, 0:nb * 128], wv[:, :, g * 512:g * 512 + nb * 128], reads=[w], writes=[wtile])
            for (t0, n) in tiles:
                h = ht[hit % 2]
                hit += 1
                for (r, o, ts, ln) in seq_ranges(t0, n, TL):
                    S.dma("sp", h[:, :, ts - t0:ts - t0 + ln],
                          hfull.h.ap()[r].rearrange("(k p) t -> p k t", p=128)[:, :, o:o + ln],
                          reads=[hfull], writes=[h])
                for j in range(nb):
                    p = pp[oit % 4]
                    o_ = og[oit % 4]
                    oit += 1
                    for k in range(KD):
                        S.op("pe", lambda e: e.matmul(p[:, 0:n], wtile[:, k, j * 128:(j + 1) * 128], h[:, k, 0:n],
                                                      start=(k == 0), stop=(k == KD - 1)),
                             reads=[wtile, h], writes=[p])
                    if oit % 2 == 0:
                        S.op("dve", lambda e: e.tensor_copy(o_[:, 0:n], p[:, 0:n]), reads=[p], writes=[o_])
                    else:
                        S.op("act", lambda e: e.activation(o_[:, 0:n], p[:, 0:n], AF.Copy), reads=[p], writes=[o_])
                    cb = g * 4 + j
                    S.dma("sp", pre.h.ap()[cb * 128:(cb + 1) * 128, t0:t0 + n], o_[:, 0:n], reads=[o_], writes=[pre])
        S.barrier()


def zlayout(TL, K):
    P = K // 2
    NT = CTXN + NR * TL
    return P, P, 3 * P + CTXN, NT + 4 * P


def load_padded(S, Z, pre, row0, nrows, TL, K):
    P, c0, l0, tot = zlayout(TL, K)
    NT = CTXN + NR * TL
    S.dma("sp", Z[0:nrows, c0:c0 + CTXN], pre.h.ap()[row0:row0 + nrows, 0:CTXN], reads=[pre], writes=[Z])
    S.dma("sp", Z[0:nrows, l0:l0 + NT - CTXN], pre.h.ap()[row0:row0 + nrows, CTXN:NT], reads=[pre], writes=[Z])


def conv_seq(S, Z, pT, wc, bias, acc, TL, K, nrows=128, eng="dve"):
    P, c0, l0, tot = zlayout(TL, K)
    NT = CTXN + NR * TL
    for (zs, os_, n) in ((c0, 0, CTXN), (l0, CTXN, NT - CTXN)):
        S.op(eng, lambda e: e.tensor_scalar(acc[0:nrows, os_:os_ + n], Z[0:nrows, zs - P:zs - P + n],
                                            wc[0:nrows, 0:1], bias[0:nrows, 0:1], ALU.mult, ALU.add),
             reads=[Z, pT], writes=[acc])
        for j in range(1, K):
            S.op(eng, lambda e: e.scalar_tensor_tensor(acc[0:nrows, os_:os_ + n], Z[0:nrows, zs - P + j:zs - P + j + n],
                                                       wc[0:nrows, j:j + 1], acc[0:nrows, os_:os_ + n],
                                                       ALU.mult, ALU.add),
                 reads=[Z, pT, acc], writes=[acc])


def store_yout(S, yout, ob, row0, TL, nrows=128):
    yv = yout.h.ap().rearrange("r c t -> c r t")
    S.dma("sp", yv[row0:row0 + nrows, :, 0:TCTX], ob[0:nrows, 0:CTXN].rearrange("p (r t) -> p r t", t=TCTX),
          reads=[ob], writes=[yout])
    S.dma("sp", yv[row0:row0 + nrows, :, TCTX:TCTX + TL], ob[0:nrows, CTXN:CTXN + NR * TL].rearrange("p (r t) -> p r t", t=TL),
          reads=[ob], writes=[yout])


def build_ffnA(TL):
    TT, NT = dims(TL)
    nc, S = new_prog()
    hfull = din(nc, "hfull", [NR, D, TT], BF16)
    w = din(nc, "w", [D, 1024])
    convp = din(nc, "convp", [128, 8, 4])
    yout = dout(nc, "yout", [NR, 512, TT], BF16)
    pre = dscr(nc, "pre", [1024, NT])
    proj_all(S, hfull, w, 1024, pre, TL)
    P, c0, l0, tot = zlayout(TL, 3)
    cp = S.sb("cp", [128, 8, 4], F32)
    S.dma("sp", cp[:, :, :], convp[:, :, :], reads=[convp], writes=[cp])
    Zg = S.sb("Zg", [128, tot], F32)
    Zv = S.sb("Zv", [128, tot], F32)
    ag = S.sb("ag", [128, NT], F32)
    av = S.sb("av", [128, NT], F32)
    ob = [S.sb(f"ob{i}", [128, NT], BF16) for i in range(2)]
    S.op("pool", lambda e: e.memset(Zg[:, :], 0.0), writes=[Zg])
    S.op("pool", lambda e: e.memset(Zv[:, :], 0.0), writes=[Zv])
    for i in range(4):
        load_padded(S, Zg, pre, i * 128, 128, TL, 3)
        load_padded(S, Zv, pre, 512 + i * 128, 128, TL, 3)
        conv_seq(S, Zg, cp, cp[:, i, 0:3], cp[:, i, 3:4], ag, TL, 3)
        conv_seq(S, Zv, cp, cp[:, 4 + i, 0:3], cp[:, 4 + i, 3:4], av, TL, 3)
        S.op("act", lambda e: e.activation(ag[:, :], ag[:, :], AF.Silu), reads=[ag], writes=[ag])
        o_ = ob[i % 2]
        S.op("dve", lambda e: e.tensor_tensor(o_[:, :], ag[:, :], av[:, :], ALU.mult), reads=[ag, av], writes=[o_])
        store_yout(S, yout, o_, i * 128, TL)
    S.barrier()
    return nc


SSD_C = 2432
NEG = -30000.0


def ssd_consts():
    c = np.zeros((128, 6, 128), np.float32)
    i = np.arange(128)
    c[:, 0, :] = np.eye(128)
    c[:, 1, :] = (i[:, None] <= i[None, :])
    c[:, 2, :] = (i[:, None] >= i[None, :])
    c[:, 3, :] = np.where(i[None, :] >= i[:, None], 0.0, NEG)
    c[:, 4, :] = np.where(i[None, :] <= i[:, None], 0.0, NEG)
    c[:, 5, :] = 1.0
    return c


def build_ssdA(TL):
    TT, NT = dims(TL)
    nc, S = new_prog()
    hfull = din(nc, "hfull", [NR, D, TT], BF16)
    w = din(nc, "w", [D, SSD_C])
    convp = din(nc, "convp", [128, 10, 6])
    hpd = din(nc, "hp", [128, 8, 2])
    dtpd = din(nc, "dtp", [32, 2])
    constd = din(nc, "consts", [128, 6, 128])
    yout = dout(nc, "yout", [NR, 1024, TT], BF16)
    pre = dscr(nc, "pre", [SSD_C, NT])
    xs = dscr(nc, "xs", [1024, NT], BF16)
    Bs = dscr(nc, "Bs", [128, NT], BF16)
    Cs = dscr(nc, "Cs", [128, NT], BF16)
    dts = dscr(nc, "dts", [32, NT])
    das = dscr(nc, "das", [32, NT])
    yf = dscr(nc, "yf", [1024, NT])

    proj_all(S, hfull, w, SSD_C, pre, TL)

    P, c0, l0, tot = zlayout(TL, 5)
    with contextlib.ExitStack() as st:
        cp = S.sb("cp", [128, 10, 6], F32, st)
        dtp = S.sb("dtp", [32, 2], F32, st)
        S.dma("sp", cp[:, :, :], convp[:, :, :], reads=[convp], writes=[cp])
        S.dma("sp", dtp[:, :], dtpd[:, :], reads=[dtpd], writes=[dtp])
        Z = S.sb("Z", [128, tot], F32, st)
        acc = S.sb("acc", [128, NT], F32, st)
        ob = S.sb("ob", [128, NT], BF16, st)
        S.op("pool", lambda e: e.memset(Z[:, :], 0.0), writes=[Z])
        for i in range(10):
            load_padded(S, Z, pre, 1024 + i * 128, 128, TL, 5)
            conv_seq(S, Z, cp, cp[:, i, 0:5], cp[:, i, 5:6], acc, TL, 5)
            S.op("act", lambda e: e.activation(ob[:, :], acc[:, :], AF.Silu), reads=[acc], writes=[ob])
            if i < 8:
                S.dma("sp", xs.h.ap()[i * 128:(i + 1) * 128, :], ob[:, :], reads=[ob], writes=[xs])
            else:
                dst = Bs if i == 8 else Cs
                S.dma("sp", dst.h.ap()[:, :], ob[:, :], reads=[ob], writes=[dst])
        dtt = S.sb("dtt", [32, NT], F32, st)
        dat = S.sb("dat", [32, NT], F32, st)
        av = S.sb("av", [32, 1], F32, st)
        S.dma("sp", dtt[:, :], pre.h.ap()[2304:2336, :], reads=[pre], writes=[dtt])
        S.op("act", lambda e: e.activation(dtt[:, :], dtt[:, :], AF.Exp, bias=dtp[:, 0:1]), reads=[dtt, dtp], writes=[dtt])
        S.op("act", lambda e: e.activation(av[:, :], dtp[:, 1:2], AF.Exp), reads=[dtp], writes=[av])
        S.op("act", lambda e: e.activation(dtt[:, :], dtt[:, :], AF.Ln, bias=1.0), reads=[dtt], writes=[dtt])
        S.op("dve", lambda e: e.tensor_scalar(dat[:, :], dtt[:, :], av[:, 0:1], -1.0, ALU.mult, ALU.mult),
             reads=[dtt, av], writes=[dat])
        S.dma("sp", dts.h.ap()[:, :], dtt[:, :], reads=[dtt], writes=[dts])
        S.dma("sp", das.h.ap()[:, :], dat[:, :], reads=[dat], writes=[das])
        S.barrier()

    nchunk_c = CTXN // 128
    nchunk_l = NR * TL // 128
    ctx_chunks = list(range(nchunk_c))
    lat_chunks = list(range(nchunk_c, nchunk_c + nchunk_l))
    with contextlib.ExitStack() as st:
        cst = S.sb("cst", [128, 6, 128], F32, st)
        cstb = S.sb("cstb", [128, 128], BF16, st)
        hp = S.sb("hp", [128, 8, 2], F32, st)
        S.dma("sp", cst[:, :, :], constd[:, :, :], reads=[constd], writes=[cst])
        S.dma("sp", hp[:, :, :], hpd[:, :, :], reads=[hpd], writes=[hp])
        S.op("dve", lambda e: e.tensor_copy(cstb[:, :], cst[:, 0, :]), reads=[cst], writes=[cstb])
        H = S.sb("H", [128, 16, 64], F32, st)
        Hb = S.sb("Hb", [128, 16, 64], BF16, st)
        Ht = S.sb("Ht", [128, 16, 64], F32, st)
        xT = S.sb("xT", [128, 8, 128], BF16, st)
        BT = S.sb("BT", [128, 128], BF16, st)
        CT = S.sb("CT", [128, 128], BF16, st)
        dd = S.sb("dd", [32, 2, 128], F32, st)
        xtok = S.sb("xtok", [128, 16, 64], BF16, st)
        xdt = S.sb("xdt", [128, 16, 64], BF16, st)
        xdtw = S.sb("xdtw", [128, 16, 64], BF16, st)
        Btok = S.sb("Btok", [128, 128], BF16, st)
        dtok = S.sb("dtok", [128, 2, 32], F32, st)
        cum = S.sb("cum", [128, 16], F32, st)
        ncum = S.sb("ncum", [128, 16], F32, st)
        totb = S.sb("totb", [128, 16], F32, st)
        cd = S.sb("cd", [128, 16], F32, st)
        dte = S.sb("dte", [128, 16], F32, st)
        cbT = S.sb("cbT", [128, 128], F32, st)
        Ecum = S.sb("Ecum", [128, 4, 128], F32, st)
        Ck = S.sb("Ck", [128, 4, 128], BF16, st)
        Ek = [S.sb(f"Ek{i}", [128, 128], F32, st) for i in range(2)]
        Mk = [S.sb(f"Mk{i}", [128, 128], BF16, st) for i in range(2)]
        yfs = S.sb("yfs", [128, 8, 128], F32, st)
        zc = S.sb("zc", [128, 8, 128], F32, st)
        ych = S.sb("ych", [128, 8, 128], F32, st)
        ysq = S.sb("ysq", [128, 128], F32, st)
        rs = S.sb("rs", [128, 128], F32, st)
        yob = S.sb("yob", [128, 8, 128], BF16, st)
        p_misc = S.ps("p_misc", stack=st)
        p_xt = S.ps("p_xt", [128, 1024], BF16, st)
        p_s = S.ps("p_s", stack=st)
        p_d = S.ps("p_d", stack=st)
        p_m = S.ps("p_m", stack=st)
        p_y = [S.ps(f"p_y{i}", stack=st) for i in range(2)]

        for d in range(2):
            tri = cst[:, 1 + d, :]
            negm = cst[:, 3 + d, :]
            S.op("pool", lambda e: e.memset(H[:, :, :], 0.0), writes=[H])
            S.op("pool", lambda e: e.memset(Hb[:, :, :], 0.0), writes=[Hb])
            order = (ctx_chunks + lat_chunks) if d == 0 else (ctx_chunks[::-1] + lat_chunks[::-1])
            for c in order:
                t0 = c * 128
                tsl = slice(t0, t0 + 128)
                S.dma("sp", xT[:, :, :], xs.h.ap().rearrange("(b p) t -> p b t", p=128)[:, :, tsl], reads=[xs], writes=[xT])
                S.dma("sp", BT[:, :], Bs.h.ap()[:, tsl], reads=[Bs], writes=[BT])
                S.dma("sp", CT[:, :], Cs.h.ap()[:, tsl], reads=[Cs], writes=[CT])
                S.dma("sp", dd[:, 0, :], dts.h.ap()[:, tsl], reads=[dts], writes=[dd])
                S.dma("sp", dd[:, 1, :], das.h.ap()[:, tsl], reads=[das], writes=[dd])
                if d == 1:
                    S.dma("sp", yfs[:, :, :], yf.h.ap().rearrange("(b p) t -> p b t", p=128)[:, :, tsl], reads=[yf], writes=[yfs])
                    S.dma("sp", zc[:, :, :], pre.h.ap()[0:1024, :].rearrange("(b p) t -> p b t", p=128)[:, :, tsl],
                          reads=[pre], writes=[zc])
                for b in range(8):
                    S.op("pe", lambda e: e.transpose(p_xt[:, b * 128:(b + 1) * 128], xT[:, b, :], cstb[:, :]),
                         reads=[xT, cstb], writes=[p_xt])
                S.op("act", lambda e: e.activation(xtok[:, :, :].rearrange("p k q -> p (k q)"), p_xt[:, :], AF.Copy),
                     reads=[p_xt], writes=[xtok])
                S.op("pe", lambda e: e.matmul(p_misc[:, 384:512], BT[:, :], cstb[:, :], start=True, stop=True),
                     reads=[BT, cstb], writes=[p_misc])
                S.op("dve", lambda e: e.tensor_copy(Btok[:, :], p_misc[:, 384:512]), reads=[p_misc], writes=[Btok])
                for j in range(2):
                    S.op("pe", lambda e: e.transpose(p_misc[:, 32 + 32 * j:64 + 32 * j], dd[:, j, :], cst[0:32, 0, 0:32]),
                         reads=[dd, cst], writes=[p_misc])
                S.op("dve", lambda e: e.tensor_copy(dtok[:, :, :].rearrange("p a b -> p (a b)"), p_misc[:, 32:96]),
                     reads=[p_misc], writes=[dtok])
                dk = 16 * d
                S.op("pe", lambda e: e.matmul(p_misc[:, 0:16], tri, dtok[:, 1, dk:dk + 16], start=True, stop=True),
                     reads=[cst, dtok], writes=[p_misc])
                S.op("pe", lambda e: e.matmul(p_misc[:, 16:32], cst[:, 5, :], dtok[:, 1, dk:dk + 16], start=True, stop=True),
                     reads=[cst, dtok], writes=[p_misc])
                S.op("pe", lambda e: e.matmul(p_misc[:, 128:256], BT[:, :], CT[:, :], start=True, stop=True),
                     reads=[BT, CT], writes=[p_misc])
                S.op("dve", lambda e: e.tensor_copy(cum[:, :], p_misc[:, 0:16]), reads=[p_misc], writes=[cum])
                S.op("dve", lambda e: e.tensor_scalar(ncum[:, :], p_misc[:, 0:16], -1.0, None, ALU.mult),
                     reads=[p_misc], writes=[ncum])
                S.op("dve", lambda e: e.tensor_copy(totb[:, :], p_misc[:, 16:32]), reads=[p_misc], writes=[totb])
                S.op("dve", lambda e: e.tensor_copy(cbT[:, :], p_misc[:, 128:256]), reads=[p_misc], writes=[cbT])
                S.op("act", lambda e: e.activation(cd[:, :], totb[:, :], AF.Exp), reads=[totb], writes=[cd])
                S.op("dve", lambda e: e.tensor_tensor(dte[:, :], totb[:, :], cum[:, :], ALU.subtract),
                     reads=[totb, cum], writes=[dte])
                S.op("act", lambda e: e.activation(dte[:, :], dte[:, :], AF.Exp), reads=[dte], writes=[dte])
                S.op("dve", lambda e: e.tensor_tensor(xdt[:, :, :], xtok[:, :, :],
                                                      dtok[:, 0, dk:dk + 16].unsqueeze(2).to_broadcast([128, 16, 64]), ALU.mult),
                     reads=[xtok, dtok], writes=[xdt])
                S.op("pool", lambda e: e.tensor_tensor(xdtw[:, :, :], xdt[:, :, :],
                                                       dte[:, :].unsqueeze(2).to_broadcast([128, 16, 64]), ALU.mult),
                     reads=[xdt, dte], writes=[xdtw])
                for g4 in range(4):
                    for kk in range(4):
                        k = g4 * 4 + kk
                        S.op("pe", lambda e: e.matmul(p_d[:, kk * 128:(kk + 1) * 128],
                                                      dtok[:, 1, dk + k:dk + k + 1].to_broadcast([128, 128]), tri,
                                                      start=True, stop=True), reads=[dtok, cst], writes=[p_d])
                    S.op("act", lambda e: e.activation(Ecum[:, :, :].rearrange("p a b -> p (a b)"), p_d[:, :], AF.Exp),
                         reads=[p_d], writes=[Ecum])
                    S.op("pool", lambda e: e.tensor_tensor(Ck[:, :, :], Ecum[:, :, :],
                                                           CT[:, :].unsqueeze(1).to_broadcast([128, 4, 128]), ALU.mult),
                         reads=[Ecum, CT], writes=[Ck])
                    for kk in range(4):
                        k = g4 * 4 + kk
                        pm = p_m[:, kk * 128:(kk + 1) * 128]
                        S.op("pe", lambda e: e.matmul(pm, dtok[:, 1, dk + k:dk + k + 1].to_broadcast([128, 128]), tri,
                                                      start=True, stop=False), reads=[dtok, cst], writes=[p_m])
                        S.op("pe", lambda e: e.matmul(pm, cst[:, 0, :], negm, start=False, stop=True),
                             reads=[cst], writes=[p_m])
                        E = Ek[k % 2]
                        M = Mk[k % 2]
                        S.op("act", lambda e: e.activation(E[:, :], pm, AF.Exp, bias=ncum[:, k:k + 1]),
                             reads=[p_m, ncum], writes=[E])
                        S.op("dve", lambda e: e.tensor_tensor(M[:, :], E[:, :], cbT[:, :], ALU.mult),
                             reads=[E, cbT], writes=[M])
                        py = p_y[k // 8]
                        b4 = (k // 2) % 4
                        po = (k % 2) * 64
                        yo = py[po:po + 64, b4 * 128:(b4 + 1) * 128]
                        S.op("pe", lambda e: e.matmul(yo, xdt[:, k, :], M[:, :], start=True, stop=False),
                             reads=[xdt, M], writes=[py])
                        S.op("pe", lambda e: e.matmul(yo, Hb[:, k, :], Ck[:, kk, :], start=False, stop=True),
                             reads=[Hb, Ck], writes=[py])
                S.op("dve", lambda e: e.tensor_tensor(Ht[:, :, :], H[:, :, :],
                                                      cd[:, :].unsqueeze(2).to_broadcast([128, 16, 64]), ALU.mult),
                     reads=[H, cd], writes=[Ht])
                for hh in range(2):
                    S.op("pe", lambda e: e.matmul(p_s[:, :], Btok[:, :],
                                                  xdtw[:, 8 * hh:8 * hh + 8, :].rearrange("p k q -> p (k q)"),
                                                  start=True, stop=True), reads=[Btok, xdtw], writes=[p_s])
                    S.op("dve", lambda e: e.tensor_tensor(H[:, 8 * hh:8 * hh + 8, :].rearrange("p k q -> p (k q)"),
                                                          Ht[:, 8 * hh:8 * hh + 8, :].rearrange("p k q -> p (k q)"),
                                                          p_s[:, :], ALU.add), reads=[Ht, p_s], writes=[H])
                S.op("act", lambda e: e.activation(Hb[:, :, :], H[:, :, :], AF.Copy), reads=[H], writes=[Hb])
                if d == 0:
                    for hh in range(2):
                        S.op("act", lambda e: e.activation(ych[:, 4 * hh:4 * hh + 4, :].rearrange("p a b -> p (a b)"),
                                                           p_y[hh][:, :], AF.Copy), reads=[p_y[hh]], writes=[ych])
                    S.dma("sp", yf.h.ap().rearrange("(b p) t -> p b t", p=128)[:, :, tsl], ych[:, :, :], reads=[ych], writes=[yf])
                else:
                    S.op("act", lambda e: e.activation(zc[:, :, :], zc[:, :, :], AF.Silu), reads=[zc], writes=[zc])
                    for b in range(8):
                        S.op("dve", lambda e: e.scalar_tensor_tensor(ych[:, b, :], xT[:, b, :], hp[:, b, 0:1], yfs[:, b, :],
                                                                     ALU.mult, ALU.add), reads=[xT, hp, yfs], writes=[ych])
                    for hh in range(2):
                        sl = ych[:, 4 * hh:4 * hh + 4, :].rearrange("p a b -> p (a b)")
                        S.op("dve", lambda e: e.tensor_tensor(sl, sl, p_y[hh][:, :], ALU.add), reads=[ych, p_y[hh]], writes=[ych])
                    S.op("pool", lambda e: e.tensor_tensor(ych[:, :, :], ych[:, :, :], zc[:, :, :], ALU.mult),
                         reads=[ych, zc], writes=[ych])
                    for b in range(8):
                        S.op("act", lambda e: e.activation(ysq[:, :], ych[:, b, :], AF.Square), reads=[ych], writes=[ysq])
                        S.op("pe", lambda e: e.matmul(p_misc[:, 256:384], cst[:, 5, :], ysq[:, :], start=(b == 0), stop=(b == 7)),
                             reads=[cst, ysq], writes=[p_misc])
                    S.op("dve", lambda e: e.tensor_scalar(rs[:, :], p_misc[:, 256:384], 1.0 / 1024, EPS, ALU.mult, ALU.add),
                         reads=[p_misc], writes=[rs])
                    S.op("act", lambda e: e.activation(rs[:, :], rs[:, :], AF.Sqrt), reads=[rs], writes=[rs])
                    S.op("dve", lambda e: e.reciprocal(rs[:, :], rs[:, :]), reads=[rs], writes=[rs])
                    for b in range(8):
                        S.op("dve", lambda e: e.scalar_tensor_tensor(yob[:, b, :], ych[:, b, :], hp[:, b, 1:2], rs[:, :],
                                                                     ALU.mult, ALU.mult), reads=[ych, hp, rs], writes=[yob])
                    yv = yout.h.ap().rearrange("r (b p) t -> r p b t", p=128)
                    for (r, o, ts, ln) in seq_ranges(t0, 128, TL):
                        S.dma("sp", yv[r][:, :, o:o + ln], yob[:, :, ts - t0:ts - t0 + ln], reads=[yob], writes=[yout])
        S.barrier()
    return nc


def pvec(v):
    v = np.asarray(v, np.float32)
    return np.ascontiguousarray(v.reshape(-1, 128).T)


def ssd_host(inp, j, g):
    w_in = inp["ssd_w_in"][j]
    DI, GN = 8192, 1024
    cols = np.r_[g * 1024:(g + 1) * 1024, DI + g * 1024:DI + (g + 1) * 1024,
                 2 * DI + g * 128:2 * DI + (g + 1) * 128, 2 * DI + GN + g * 128:2 * DI + GN + (g + 1) * 128,
                 2 * DI + 2 * GN + g * 16:2 * DI + 2 * GN + (g + 1) * 16,
                 2 * DI + 2 * GN + 128 + g * 16:2 * DI + 2 * GN + 128 + (g + 1) * 16]
    w = np.zeros((D, SSD_C), np.float32)
    w[:, :len(cols)] = w_in[:, cols]
    cch = np.r_[g * 1024:(g + 1) * 1024, DI + g * 128:DI + (g + 1) * 128, DI + GN + g * 128:DI + GN + (g + 1) * 128]
    cw = inp["ssd_conv_w"][j][:, cch]
    cb = inp["ssd_conv_b"][j][cch]
    cp = np.concatenate([cw, cb[None]], 0)
    convp = np.ascontiguousarray(cp.T.reshape(10, 128, 6).transpose(1, 0, 2))
    dsk = np.repeat(inp["ssd_d"][j][g * 16:(g + 1) * 16], 64)
    nw = inp["ssd_norm"][j][g * 1024:(g + 1) * 1024]
    hp = np.ascontiguousarray(np.stack([pvec(dsk), pvec(nw)], -1))
    dtb = inp["ssd_dt_bias"][j].reshape(2, 8, 16)[:, g].reshape(32)
    alog = inp["ssd_a_log"][j].reshape(2, 8, 16)[:, g].reshape(32)
    dtp = np.ascontiguousarray(np.stack([dtb, alog], -1).astype(np.float32))
    return {"w": w, "convp": convp, "hp": hp, "dtp": dtp, "consts": ssd_consts()}


def gather_tokens(outs, key):
    return [np.ascontiguousarray(np.concatenate([outs[c][key][r] for c in range(NR)], 0)) for r in range(NR)]


ATT_SCALE = 128 ** -0.5


def rope_tables(nlat):
    f32 = np.float32
    t = np.arange(nlat)
    row = (t // 64).astype(f32)
    col = (t % 64).astype(f32)
    n_freq = 32
    inv_freq = (f32(10000.0) ** (-np.arange(n_freq, dtype=f32) / f32(n_freq))).astype(f32)
    ang_r = (row[:, None] * inv_freq[None, :]).astype(f32)
    ang_c = (col[:, None] * inv_freq[None, :]).astype(f32)
    cos = np.zeros((128, nlat), f32)
    sins = np.zeros((128, nlat), f32)
    for d in range(128):
        ang = ang_r if d < 64 else ang_c
        f = d % 32
        cos[d] = np.cos(ang[:, f])
        s = np.sin(ang[:, f])
        sins[d] = -s if (d % 64) < 32 else s
    perm = np.zeros((128, 128), f32)
    for d in range(128):
        partner = d + 32 if (d % 64) < 32 else d - 32
        perm[partner, d] = 1.0
    return cos, sins, perm


def build_attA(TL, lambda_init, want_ctx=True, dbg=False):
    TT, NT = dims(TL)
    NL = NR * TL
    nc, S = new_prog()
    hfull = din(nc, "hfull", [NR, D, TT], BF16)
    wqk = din(nc, "wqk", [D, 1024])
    wv = din(nc, "wv", [D, 512])
    gains = din(nc, "gains", [128, 2])
    grow = din(nc, "grow", [128, 2, 128])
    lamv = din(nc, "lamv", [128, 4, 128])
    subg = din(nc, "subg", [128, 256])
    cosd = din(nc, "cosd", [128, NL])
    sind = din(nc, "sind", [128, NL])
    permd = din(nc, "perm", [128, 128])
    yout = dout(nc, "yout", [NR, 512, TT], BF16)
    mk = dout if dbg else dscr
    pre = mk(nc, "pre", [1024, NT])
    qk = mk(nc, "qk", [1024, NT], BF16)
    Vs = mk(nc, "Vs", [NT, 512], BF16)
    dbgo = dout(nc, "dbgo", [128, 8]) if dbg else None

    proj_all(S, hfull, wqk, 1024, pre, TL)

    with contextlib.ExitStack() as st:
        wvs = S.sb("wvs", [128, KD, 512], BF16, st)
        S.dma("pool", wvs[:, :, :], wv.h.ap().rearrange("(k p) c -> p k c", p=128), reads=[wv], writes=[wvs])
        ht = [S.sb(f"vh{i}", [128, KD, 512], BF16, st) for i in range(2)]
        vo = [S.sb(f"vo{i}", [128, 512], BF16, st) for i in range(2)]
        pv = [S.ps(f"pv{i}", stack=st) for i in range(2)]
        it = 0
        for ti, (t0, n) in enumerate(seq_tiles(TL)):
            h = ht[ti % 2]
            for (r, o, ts, ln) in seq_ranges(t0, n, TL):
                S.dma("sp", h[:, :, ts - t0:ts - t0 + ln],
                      hfull.h.ap()[r].rearrange("(k p) t -> p k t", p=128)[:, :, o:o + ln], reads=[hfull], writes=[h])
            for sub in range(n // 128):
                p = pv[it % 2]
                o_ = vo[it % 2]
                it += 1
                for k in range(KD):
                    S.op("pe", lambda e: e.matmul(p[:, :], h[:, k, sub * 128:(sub + 1) * 128], wvs[:, k, :],
                                                  start=(k == 0), stop=(k == KD - 1)), reads=[h, wvs], writes=[p])
                S.op("act", lambda e: e.activation(o_[:, :], p[:, :], AF.Copy), reads=[p], writes=[o_])
                S.dma("sp", Vs.h.ap()[t0 + sub * 128:t0 + (sub + 1) * 128, :], o_[:, :], reads=[o_], writes=[Vs])
        S.barrier()

    with contextlib.ExitStack() as st:
        gn = S.sb("gn", [128, 2], F32, st)
        pm = S.sb("pm", [128, 128], F32, st)
        ones = S.sb("ones", [128, 128], F32, st)
        S.dma("sp", gn[:, :], gains[:, :], reads=[gains], writes=[gn])
        S.dma("sp", pm[:, :], permd[:, :], reads=[permd], writes=[pm])
        S.op("pool", lambda e: e.memset(ones[:, :], 1.0), writes=[ones])
        cs = [S.sb(f"cs{i}", [128, 512], F32, st) for i in range(2)]
        sn = [S.sb(f"sn{i}", [128, 512], F32, st) for i in range(2)]
        X = [S.sb(f"X{i}", [128, 512], F32, st) for i in range(2)]
        sq = [S.sb(f"sq{i}", [128, 512], F32, st) for i in range(2)]
        rs = [S.sb(f"rs{i}", [128, 512], F32, st) for i in range(2)]
        Xn = [S.sb(f"Xn{i}", [128, 512], F32, st) for i in range(2)]
        R1 = [S.sb(f"R1{i}", [128, 512], F32, st) for i in range(2)]
        ob = [S.sb(f"ob{i}", [128, 512], BF16, st) for i in range(2)]
        pss = [S.ps(f"pss{i}", stack=st) for i in range(2)]
        ppx = [S.ps(f"ppx{i}", stack=st) for i in range(2)]
        it = 0
        for ti, (t0, n) in enumerate(seq_tiles(TL)):
            lat = t0 >= CTXN
            c_, s_ = cs[ti % 2], sn[ti % 2]
            if lat:
                S.dma("sp", c_[:, 0:n], cosd.h.ap()[:, t0 - CTXN:t0 - CTXN + n], reads=[cosd], writes=[c_])
                S.dma("sp", s_[:, 0:n], sind.h.ap()[:, t0 - CTXN:t0 - CTXN + n], reads=[sind], writes=[s_])
            for blk in range(8):
                i2 = it % 2
                it += 1
                x_, q_, r_, xn_, r1_, o_ = X[i2], sq[i2], rs[i2], Xn[i2], R1[i2], ob[i2]
                S.dma("sp", x_[:, 0:n], pre.h.ap()[blk * 128:(blk + 1) * 128, t0:t0 + n], reads=[pre], writes=[x_])
                S.op("act", lambda e: e.activation(q_[:, 0:n], x_[:, 0:n], AF.Square), reads=[x_], writes=[q_])
                S.op("pe", lambda e: e.matmul(pss[i2][:, 0:n], ones[:, :], q_[:, 0:n], start=True, stop=True),
                     reads=[ones, q_], writes=[pss[i2]])
                S.op("dve", lambda e: e.tensor_scalar(r_[:, 0:n], pss[i2][:, 0:n], 1.0 / 128, EPS, ALU.mult, ALU.add),
                     reads=[pss[i2]], writes=[r_])
                S.op("act", lambda e: e.activation(r_[:, 0:n], r_[:, 0:n], AF.Sqrt), reads=[r_], writes=[r_])
                S.op("dve", lambda e: e.reciprocal(r_[:, 0:n], r_[:, 0:n]), reads=[r_], writes=[r_])
                g_ = gn[:, 0:1] if blk < 4 else gn[:, 1:2]
                if lat:
                    S.op("dve", lambda e: e.scalar_tensor_tensor(xn_[:, 0:n], x_[:, 0:n], g_, r_[:, 0:n], ALU.mult, ALU.mult),
                         reads=[x_, gn, r_], writes=[xn_])
                    S.op("pe", lambda e: e.matmul(ppx[i2][:, 0:n], pm[:, :], xn_[:, 0:n], start=True, stop=True),
                         reads=[pm, xn_], writes=[ppx[i2]])
                    S.op("pool", lambda e: e.tensor_tensor(r1_[:, 0:n], xn_[:, 0:n], c_[:, 0:n], ALU.mult),
                         reads=[xn_, c_], writes=[r1_])
                    S.op("dve", lambda e: e.tensor_tensor(xn_[:, 0:n], ppx[i2][:, 0:n], s_[:, 0:n], ALU.mult),
                         reads=[ppx[i2], s_], writes=[xn_])
                    S.op("dve", lambda e: e.tensor_tensor(o_[:, 0:n], r1_[:, 0:n], xn_[:, 0:n], ALU.add),
                         reads=[r1_, xn_], writes=[o_])
                else:
                    S.op("dve", lambda e: e.scalar_tensor_tensor(o_[:, 0:n], x_[:, 0:n], g_, r_[:, 0:n], ALU.mult, ALU.mult),
                         reads=[x_, gn, r_], writes=[o_])
                S.dma("sp", qk.h.ap()[blk * 128:(blk + 1) * 128, t0:t0 + n], o_[:, 0:n], reads=[o_], writes=[qk])
        S.barrier()

    NKT = NT // 128
    with contextlib.ExitStack() as st:
        identb = S.sb("identb", [128, 128], BF16, st)
        identf = S.sb("identf", [128, 128], F32, st)
        S.op("pool", lambda e: e.memset(identf[:, :], 0.0), writes=[identf])
        grw = S.sb("grw", [128, 2, 128], F32, st)
        lv = S.sb("lv", [128, 4, 128], F32, st)
        sg = S.sb("sg", [128, 256], F32, st)
        S.dma("sp", grw[:, :, :], grow[:, :, :], reads=[grow], writes=[grw])
        S.dma("sp", lv[:, :, :], lamv[:, :, :], reads=[lamv], writes=[lv])
        S.dma("sp", sg[:, :], subg[:, :], reads=[subg], writes=[sg])
        pmf = S.sb("pmf", [128, 128], F32, st)
        S.dma("sp", pmf[:, :], permd[:, :], reads=[permd], writes=[pmf])
        pmisc = S.ps("pmisc", stack=st)
        S.op("pe", lambda e: e.matmul(pmisc[:, 0:128], pmf[:, :], pmf[:, :], start=True, stop=True), reads=[pmf], writes=[pmisc])
        S.op("dve", lambda e: e.tensor_copy(identb[:, :], pmisc[:, 0:128]), reads=[pmisc], writes=[identb])
        sm = S.sb("sm", [128, 8], F32, st)
        tmpv = S.sb("tmpv", [128, 128], F32, st)
        for j in range(2):
            S.op("dve", lambda e: e.tensor_reduce(sm[:, j:j + 1], grw[:, j, :], AX.X, ALU.max, apply_absolute_value=True),
                 reads=[grw], writes=[sm])
        S.op("dve", lambda e: e.scalar_tensor_tensor(sm[:, 2:3], sm[:, 0:1], -math.sqrt(128.0), sm[:, 1:2], ALU.mult, ALU.mult),
             reads=[sm], writes=[sm])
        for j in range(2):
            S.op("dve", lambda e: e.tensor_tensor(tmpv[:, :], lv[:, 2 * j, :], lv[:, 2 * j + 1, :], ALU.mult), reads=[lv], writes=[tmpv])
            S.op("dve", lambda e: e.tensor_reduce(sm[:, 3 + j:4 + j], tmpv[:, :], AX.X, ALU.add), reads=[tmpv], writes=[sm])
        S.op("act", lambda e: e.activation(sm[:, 3:5], sm[:, 3:5], AF.Exp), reads=[sm], writes=[sm])
        S.op("dve", lambda e: e.scalar_tensor_tensor(sm[:, 5:6], sm[:, 4:5], -float(lambda_init), sm[:, 3:4], ALU.add, ALU.subtract),
             reads=[sm], writes=[sm])
        S.op("dve", lambda e: e.tensor_scalar(sg[:, :], sg[:, :], 1.0 - float(lambda_init), None, ALU.mult), reads=[sg], writes=[sg])
        negB = sm[:, 2:3]
        neglam = sm[:, 5:6]
        if dbg:
            S.dma("sp", dbgo[:, :], sm[:, :], reads=[sm], writes=[dbgo])

        kT = S.sb("kT", [128, 2, NT], BF16, st)
        Vh = S.sb("Vh", [128, NKT, 257], BF16, st)
        S.op("pool", lambda e: e.memset(Vh[:, :, 256:257], 1.0), writes=[Vh])
        qT = [S.sb(f"qT{i}", [128, 512], BF16, st) for i in range(2)]
        PT = [S.sb(f"PT{i}", [128, 512], BF16, st) for i in range(3)]
        O0 = S.sb("O0", [128, 4, 256], F32, st)
        O1 = S.sb("O1", [128, 256], F32, st)
        rz = S.sb("rz", [128, 8], F32, st)
        junk = S.sb("junk", [128, 256], F32, st)
        onb = S.sb("onb", [128, 256], BF16, st)
        obT = [S.sb(f"obT{i}", [128, 2, 512], BF16, st) for i in range(2)]
        ps_s = [S.ps(f"ps_s{i}", stack=st) for i in range(2)]
        acc = [S.ps(f"acc{i}", stack=st) for i in range(4)]
        ptr = S.ps("ptr", [128, 1024], BF16, st)
        qit = 0
        pit = 0
        yv = yout.h.ap().rearrange("r (hb p) t -> r p hb t", p=128)
        for h in range(2):
            for j in range(2):
                S.dma("sp", kT[:, j, :], qk.h.ap()[512 + (2 * h + j) * 128:512 + (2 * h + j + 1) * 128, :], reads=[qk], writes=[kT])
            S.dma("sp", Vh[:, :, 0:256], Vs.h.ap().rearrange("(kt p) e -> p kt e", p=128)[:, :, h * 256:(h + 1) * 256],
                  reads=[Vs], writes=[Vh])
            qtiles = [(t0, n) for (t0, n) in seq_tiles(TL) if (t0 >= CTXN or want_ctx)]
            for (t0, n) in qtiles:
                lat = t0 >= CTXN
                kts = list(range(NKT)) if lat else list(range(CTXN // 128))
                nqg = n // 128
                obt = obT[qit % 2]
                for j in range(2):
                    q_ = qT[qit % 2]
                    qit += 1
                    S.dma("sp", q_[:, 0:n], qk.h.ap()[(2 * h + j) * 128:(2 * h + j + 1) * 128, t0:t0 + n], reads=[qk], writes=[q_])
                    for ki, kt in enumerate(kts):
                        ps_ = ps_s[pit % 2]
                        pt_ = PT[pit % 3]
                        pit += 1
                        S.op("pe", lambda e: e.matmul(ps_[:, 0:n], kT[:, j, kt * 128:(kt + 1) * 128], q_[:, 0:n], start=True, stop=True),
                             reads=[kT, q_], writes=[ps_])
                        S.op("act", lambda e: e.activation(pt_[:, 0:n], ps_[:, 0:n], AF.Exp, bias=negB, scale=ATT_SCALE),
                             reads=[ps_, sm], writes=[pt_])
                        for qg in range(nqg):
                            S.op("pe", lambda e: e.matmul(acc[qg][:, 0:257], pt_[:, qg * 128:(qg + 1) * 128], Vh[:, kt, :],
                                                          start=(ki == 0), stop=(ki == len(kts) - 1)),
                                 reads=[pt_, Vh], writes=[acc[qg]])
                    for qg in range(nqg):
                        S.op("dve", lambda e: e.reciprocal(rz[:, qg:qg + 1], acc[qg][:, 256:257]), reads=[acc[qg]], writes=[rz])
                        if j == 0:
                            S.op("dve", lambda e: e.tensor_scalar(O0[:, qg, :], acc[qg][:, 0:256], rz[:, qg:qg + 1], None, ALU.mult),
                                 reads=[acc[qg], rz], writes=[O0])
                        else:
                            S.op("dve", lambda e: e.tensor_scalar(O1[:, :], acc[qg][:, 0:256], rz[:, qg:qg + 1], None, ALU.mult),
                                 reads=[acc[qg], rz], writes=[O1])
                            S.op("dve", lambda e: e.scalar_tensor_tensor(O1[:, :], O1[:, :], neglam, O0[:, qg, :], ALU.mult, ALU.add),
                                 reads=[O1, sm, O0], writes=[O1])
                            S.op("act", lambda e: e.activation(junk[:, :], O1[:, :], AF.Square, accum_out=rz[:, 4 + qg:5 + qg]),
                                 reads=[O1], writes=[junk, rz])
                            S.op("dve", lambda e: e.tensor_scalar(rz[:, 4 + qg:5 + qg], rz[:, 4 + qg:5 + qg], 1.0 / 256, EPS, ALU.mult, ALU.add),
                                 reads=[rz], writes=[rz])
                            S.op("act", lambda e: e.activation(rz[:, 4 + qg:5 + qg], rz[:, 4 + qg:5 + qg], AF.Sqrt), reads=[rz], writes=[rz])
                            S.op("dve", lambda e: e.reciprocal(rz[:, 4 + qg:5 + qg], rz[:, 4 + qg:5 + qg]), reads=[rz], writes=[rz])
                            S.op("dve", lambda e: e.scalar_tensor_tensor(onb[:, :], O1[:, :], rz[:, 4 + qg:5 + qg], sg[:, :], ALU.mult, ALU.mult),
                                 reads=[O1, rz, sg], writes=[onb])
                            for eb in range(2):
                                S.op("pe", lambda e: e.transpose(ptr[:, eb * 512 + qg * 128:eb * 512 + (qg + 1) * 128],
                                                                 onb[:, eb * 128:(eb + 1) * 128], identb[:, :]),
                                     reads=[onb, identb], writes=[ptr])
                S.op("act", lambda e: e.activation(obt[:, :, 0:n], ptr[:, :].rearrange("p (a b) -> p a b", a=2)[:, :, 0:n], AF.Copy),
                     reads=[ptr], writes=[obt])
                for (r, o, ts, ln) in seq_ranges(t0, n, TL):
                    S.dma("sp", yv[r][:, 2 * h:2 * h + 2, o:o + ln], obt[:, :, ts - t0:ts - t0 + ln], reads=[obt], writes=[yout])
        S.barrier()
    return nc


def att_host(inp, j, c, TL, lambda_init):
    cols = np.r_[512 * c:512 * (c + 1)]
    wqk = np.ascontiguousarray(np.concatenate([inp["da_w_q"][j][:, cols], inp["da_w_k"][j][:, cols]], 1))
    wv = np.ascontiguousarray(inp["da_w_v"][j][:, cols])
    gq, gk = inp["da_q_norm"][j], inp["da_k_norm"][j]
    gains = np.ascontiguousarray(np.stack([gq, gk], -1).astype(np.float32))
    grow = np.ascontiguousarray(np.broadcast_to(np.stack([gq, gk], 0)[None], (128, 2, 128)).astype(np.float32))
    lamv = np.stack([inp["da_lam_q1"][j], inp["da_lam_k1"][j], inp["da_lam_q2"][j], inp["da_lam_k2"][j]], 0)
    lamv = np.ascontiguousarray(np.broadcast_to(lamv[None], (128, 4, 128)).astype(np.float32))
    subg = np.ascontiguousarray(np.broadcast_to(inp["da_sub_norm"][j][None], (128, 256)).astype(np.float32))
    cos, sins, perm = rope_tables(NR * TL)
    return {"wqk": wqk, "wv": wv, "gains": gains, "grow": grow, "lamv": lamv, "subg": subg,
            "cosd": cos, "sind": sins, "perm": perm}


S5W = 128
TWO_PI = 2.0 * math.pi


I32 = mybir.dt.int32
CW1 = 6.28125
CW2 = TWO_PI - 6.28125


def emit_sin(S, outT, out_ap, y, yap, ki, kiap, kf, kfap):
    S.op("dve", lambda e: e.tensor_scalar(kfap, yap, 1.0 / TWO_PI, None, ALU.mult), reads=[y], writes=[kf])
    S.op("dve", lambda e: e.tensor_copy(kiap, kfap), reads=[kf], writes=[ki])
    S.op("dve", lambda e: e.tensor_copy(kfap, kiap), reads=[ki], writes=[kf])
    S.op("dve", lambda e: e.scalar_tensor_tensor(yap, kfap, -CW1, yap, ALU.mult, ALU.add), reads=[kf, y], writes=[y])
    S.op("dve", lambda e: e.scalar_tensor_tensor(yap, kfap, -CW2, yap, ALU.mult, ALU.add), reads=[kf, y], writes=[y])
    S.op("dve", lambda e: e.tensor_scalar(kfap, yap, math.pi, None, ALU.is_gt), reads=[y], writes=[kf])
    S.op("dve", lambda e: e.scalar_tensor_tensor(yap, kfap, -TWO_PI, yap, ALU.mult, ALU.add), reads=[kf, y], writes=[y])
    S.op("dve", lambda e: e.tensor_scalar(kfap, yap, -math.pi, None, ALU.is_lt), reads=[y], writes=[kf])
    S.op("dve", lambda e: e.scalar_tensor_tensor(yap, kfap, TWO_PI, yap, ALU.mult, ALU.add), reads=[kf, y], writes=[y])
    S.op("dve", lambda e: e.tensor_scalar(yap, yap, -math.pi, math.pi, ALU.max, ALU.min), reads=[y], writes=[y])
    S.op("act", lambda e: e.activation(out_ap, yap, AF.Sin), reads=[y], writes=[outT])


def build_s5A(TL, dbg=False):
    TT, NT = dims(TL)
    W = S5W
    NWIN = NT // W
    nc, S = new_prog()
    hs5 = din(nc, "hs5", [NR, 512, TT], BF16)
    lamd = din(nc, "lam", [128, 2, 2, 16])
    stpd = din(nc, "stp", [128, 2, 16])
    bblkd = din(nc, "bblk", [128, 2, 16, 128])
    cblkd = din(nc, "cblk", [128, 2, 2, 16, 128])
    dskd = din(nc, "dsk", [128, 4])
    jrowd = din(nc, "jrow", [128, W])
    yout = dout(nc, "yout", [NR, 512, TT], BF16)
    yfd = dscr(nc, "yfd", [512, NT])
    dbgo = dout(nc, "dbgo", [128, 8, 16]) if dbg else None

    u = S.sb("u", [128, 4, NT], BF16)
    lam = S.sb("lam", [128, 2, 2, 16], F32)
    stp = S.sb("stp", [128, 2, 16], F32)
    bblkb = S.sb("bblkb", [128, 2, 16, 128], BF16)
    cblkb = S.sb("cblkb", [128, 2, 2, 16, 128], BF16)
    dsk = S.sb("dsk", [128, 4], F32)
    jrow = S.sb("jrow", [128, W], F32)
    for (dst, src) in ((lam, lamd), (stp, stpd), (dsk, dskd), (jrow, jrowd)):
        S.dma("sp", dst[tuple(slice(None) for _ in dst.h.shape)], src.h.ap(), reads=[src], writes=[dst])
    S.dma("pool", bblkb[:, :, :, :], bblkd.h.ap(), reads=[bblkd], writes=[bblkb])
    for d in range(2):
        S.dma("pool", cblkb[:, d, :, :, :], cblkd.h.ap()[:, d], reads=[cblkd], writes=[cblkb])
    uv = hs5.h.ap().rearrange("r (f p) t -> r p f t", p=128)
    for r in range(NR):
        S.dma("sp", u[:, :, TCTX * r:TCTX * (r + 1)], uv[r][:, :, 0:TCTX], reads=[hs5], writes=[u])
        S.dma("sp", u[:, :, CTXN + TL * r:CTXN + TL * (r + 1)], uv[r][:, :, TCTX:TT], reads=[hs5], writes=[u])
    for d in range(2):
        S.op("dve", lambda e: e.tensor_scalar(cblkb[:, d, 1, :, :], cblkb[:, d, 1, :, :], -1.0, None, ALU.mult),
             reads=[cblkb], writes=[cblkb])

    pki = S.sb("pki", [128, 16], I32)
    pp = S.sb("pp", [128, 2, 12, 16], F32)
    for d in range(2):
        P_ = lambda i: pp[:, d, i, :]
        lr, li = lam[:, 0, d, :], lam[:, 1, d, :]
        S.op("act", lambda e: e.activation(P_(0), stp[:, d, :], AF.Exp), reads=[stp], writes=[pp])
        S.op("dve", lambda e: e.tensor_tensor(P_(10), lr, P_(0), ALU.mult), reads=[lam, pp], writes=[pp])
        S.op("act", lambda e: e.activation(P_(1), P_(10), AF.Exp), reads=[pp], writes=[pp])
        S.op("dve", lambda e: e.tensor_tensor(P_(2), li, P_(0), ALU.mult), reads=[lam, pp], writes=[pp])
        S.op("dve", lambda e: e.tensor_copy(P_(10), P_(2)), reads=[pp], writes=[pp])
        emit_sin(S, pp, P_(4), pp, P_(10), pki, pki[:, :], pp, P_(11))
        S.op("dve", lambda e: e.tensor_scalar(P_(10), P_(2), 0.5 * math.pi, None, ALU.add), reads=[pp], writes=[pp])
        emit_sin(S, pp, P_(3), pp, P_(10), pki, pki[:, :], pp, P_(11))
        S.op("dve", lambda e: e.tensor_tensor(P_(5), P_(1), P_(3), ALU.mult), reads=[pp], writes=[pp])
        S.op("dve", lambda e: e.tensor_scalar(P_(5), P_(5), -1.0, None, ALU.add), reads=[pp], writes=[pp])
        S.op("dve", lambda e: e.tensor_tensor(P_(6), P_(1), P_(4), ALU.mult), reads=[pp], writes=[pp])
        S.op("dve", lambda e: e.tensor_tensor(P_(7), lr, lr, ALU.mult), reads=[lam], writes=[pp])
        S.op("dve", lambda e: e.tensor_tensor(P_(10), li, li, ALU.mult), reads=[lam], writes=[pp])
        S.op("dve", lambda e: e.tensor_tensor(P_(7), P_(7), P_(10), ALU.add), reads=[pp], writes=[pp])
        S.op("dve", lambda e: e.reciprocal(P_(7), P_(7)), reads=[pp], writes=[pp])
        S.op("dve", lambda e: e.tensor_tensor(P_(8), P_(5), lr, ALU.mult), reads=[pp, lam], writes=[pp])
        S.op("dve", lambda e: e.tensor_tensor(P_(10), P_(6), li, ALU.mult), reads=[pp, lam], writes=[pp])
        S.op("dve", lambda e: e.tensor_tensor(P_(8), P_(8), P_(10), ALU.add), reads=[pp], writes=[pp])
        S.op("dve", lambda e: e.tensor_tensor(P_(8), P_(8), P_(7), ALU.mult), reads=[pp], writes=[pp])
        S.op("dve", lambda e: e.tensor_tensor(P_(9), P_(6), lr, ALU.mult), reads=[pp, lam], writes=[pp])
        S.op("dve", lambda e: e.tensor_tensor(P_(10), P_(5), li, ALU.mult), reads=[pp, lam], writes=[pp])
        S.op("dve", lambda e: e.tensor_tensor(P_(9), P_(9), P_(10), ALU.subtract), reads=[pp], writes=[pp])
        S.op("dve", lambda e: e.tensor_tensor(P_(9), P_(9), P_(7), ALU.mult), reads=[pp], writes=[pp])
    if dbg:
        S.dma("sp", dbgo[:, :, :], pp[:, 0, 0:8, :], reads=[pp], writes=[dbgo])

    cosT = S.sb("cosT", [128, 16, W], F32)
    sinT = S.sb("sinT", [128, 16, W], F32)
    TrT = S.sb("TrT", [128, 16, W], F32)
    TiT = S.sb("TiT", [128, 16, W], F32)
    rB = S.sb("rB", [128, 16, W], F32)
    ang = S.sb("ang", [128, W], F32)
    angi = S.sb("angi", [128, W], I32)
    car = S.sb("car", [128, 2, 16], F32)
    t1 = [S.sb(f"t1{i}", [128, W], F32) for i in range(2)]
    t2 = [S.sb(f"t2{i}", [128, W], F32) for i in range(2)]
    Xr = [S.sb(f"Xr{i}", [128, W], F32) for i in range(2)]
    Xi = [S.sb(f"Xi{i}", [128, W], F32) for i in range(2)]
    sr = [S.sb(f"sr{i}", [128, W], F32) for i in range(2)]
    si = [S.sb(f"si{i}", [128, W], F32) for i in range(2)]
    srb = [S.sb(f"srb{i}", [128, W], BF16) for i in range(2)]
    sib = [S.sb(f"sib{i}", [128, W], BF16) for i in range(2)]
    yfs = S.sb("yfs", [128, 4, W], F32)
    ych = S.sb("ych", [128, 4, W], F32)
    g1 = S.sb("g1", [128, 4, W], F32)
    yob = S.sb("yob", [128, 4, W], BF16)
    pbr = [S.ps(f"pbr{i}") for i in range(2)]
    py = [S.ps(f"py{i}") for i in range(2)]
    yfv = yfd.h.ap().rearrange("(f p) t -> p f t", p=128)
    yv = yout.h.ap().rearrange("r (f p) t -> r p f t", p=128)
    it = 0
    for d in range(2):
        for sb in range(16):
            th = pp[:, d, 2, sb:sb + 1]
            S.op("dve", lambda e: e.tensor_scalar(ang[:, :], jrow[:, :], th, None, ALU.mult), reads=[jrow, pp], writes=[ang])
            emit_sin(S, sinT, sinT[:, sb, :], ang, ang[:, :], angi, angi[:, :], t1[0], t1[0][:, :])
            S.op("dve", lambda e: e.tensor_scalar(ang[:, :], jrow[:, :], th, 0.5 * math.pi, ALU.mult, ALU.add), reads=[jrow, pp], writes=[ang])
            emit_sin(S, cosT, cosT[:, sb, :], ang, ang[:, :], angi, angi[:, :], t1[0], t1[0][:, :])
            kr, ki = pp[:, d, 8, sb:sb + 1], pp[:, d, 9, sb:sb + 1]
            S.op("dve", lambda e: e.tensor_scalar(TrT[:, sb, :], sinT[:, sb, :], ki, None, ALU.mult), reads=[sinT, pp], writes=[TrT])
            S.op("dve", lambda e: e.scalar_tensor_tensor(TrT[:, sb, :], cosT[:, sb, :], kr, TrT[:, sb, :], ALU.mult, ALU.add),
                 reads=[cosT, pp, TrT], writes=[TrT])
            S.op("dve", lambda e: e.tensor_scalar(TiT[:, sb, :], sinT[:, sb, :], kr, -1.0, ALU.mult, ALU.mult), reads=[sinT, pp], writes=[TiT])
            S.op("dve", lambda e: e.scalar_tensor_tensor(TiT[:, sb, :], cosT[:, sb, :], ki, TiT[:, sb, :], ALU.mult, ALU.add),
                 reads=[cosT, pp, TiT], writes=[TiT])
            S.op("pool", lambda e: e.memset(rB[:, sb, :], 0.0), writes=[rB])
            S.op("dve", lambda e: e.tensor_scalar(rB[:, sb, :], rB[:, sb, :], pp[:, d, 1, sb:sb + 1], None, ALU.add), reads=[rB, pp], writes=[rB])
        S.op("pool", lambda e: e.memset(car[:, :, :], 0.0), writes=[car])
        nwc = CTXN // W
        wins = list(range(NWIN)) if d == 0 else (list(range(nwc))[::-1] + list(range(nwc, NWIN))[::-1])
        for wi in wins:
            t0 = wi * W
            if d == 1:
                S.dma("sp", yfs[:, :, :], yfv[:, :, t0:t0 + W], reads=[yfd], writes=[yfs])
            pyt = py[wi % 2]
            for sb in range(16):
                i2 = it % 2
                it += 1
                fb, po = sb // 4, 32 * (sb % 4)
                if d == 0:
                    urhs = u[:, fb, t0:t0 + W]
                else:
                    urhs = u[:, fb, t0:t0 + W][:, ::-1]
                pb = pbr[i2]
                for ri in range(2):
                    S.op("pe", lambda e: e.matmul(pb[:, ri * W:(ri + 1) * W], bblkb[:, ri, sb, :], urhs, start=True, stop=True),
                         reads=[bblkb, u], writes=[pb])
                br_, bi_ = pb[:, 0:W], pb[:, W:2 * W]
                a, b = t1[i2], t2[i2]
                xr_, xi_ = Xr[i2], Xi[i2]
                S.op("dve", lambda e: e.tensor_tensor(a[:, :], br_, TrT[:, sb, :], ALU.mult), reads=[pb, TrT], writes=[a])
                S.op("dve", lambda e: e.tensor_tensor(b[:, :], bi_, TiT[:, sb, :], ALU.mult), reads=[pb, TiT], writes=[b])
                S.op("pool", lambda e: e.tensor_tensor(xr_[:, :], a[:, :], b[:, :], ALU.subtract), reads=[a, b], writes=[xr_])
                S.op("dve", lambda e: e.tensor_tensor(a[:, :], br_, TiT[:, sb, :], ALU.mult), reads=[pb, TiT], writes=[a])
                S.op("dve", lambda e: e.tensor_tensor(b[:, :], bi_, TrT[:, sb, :], ALU.mult), reads=[pb, TrT], writes=[b])
                S.op("pool", lambda e: e.tensor_tensor(xi_[:, :], a[:, :], b[:, :], ALU.add), reads=[a, b], writes=[xi_])
                S.op("dve", lambda e: e.tensor_tensor_scan(xr_[:, :], rB[:, sb, :], xr_[:, :], car[:, 0, sb:sb + 1], ALU.mult, ALU.add),
                     reads=[rB, xr_, car], writes=[xr_])
                S.op("dve", lambda e: e.tensor_tensor_scan(xi_[:, :], rB[:, sb, :], xi_[:, :], car[:, 1, sb:sb + 1], ALU.mult, ALU.add),
                     reads=[rB, xi_, car], writes=[xi_])
                s_r, s_i = sr[i2], si[i2]
                S.op("pool", lambda e: e.tensor_tensor(a[:, :], xr_[:, :], cosT[:, sb, :], ALU.mult), reads=[xr_, cosT], writes=[a])
                S.op("pool", lambda e: e.tensor_tensor(b[:, :], xi_[:, :], sinT[:, sb, :], ALU.mult), reads=[xi_, sinT], writes=[b])
                S.op("dve", lambda e: e.tensor_tensor(s_r[:, :], a[:, :], b[:, :], ALU.subtract), reads=[a, b], writes=[s_r])
                S.op("pool", lambda e: e.tensor_tensor(a[:, :], xr_[:, :], sinT[:, sb, :], ALU.mult), reads=[xr_, sinT], writes=[a])
                S.op("pool", lambda e: e.tensor_tensor(b[:, :], xi_[:, :], cosT[:, sb, :], ALU.mult), reads=[xi_, cosT], writes=[b])
                S.op("dve", lambda e: e.tensor_tensor(s_i[:, :], a[:, :], b[:, :], ALU.add), reads=[a, b], writes=[s_i])
                S.op("act", lambda e: e.activation(srb[i2][:, :], s_r[:, :], AF.Copy), reads=[s_r], writes=[srb[i2]])
                S.op("act", lambda e: e.activation(sib[i2][:, :], s_i[:, :], AF.Copy), reads=[s_i], writes=[sib[i2]])
                S.op("act", lambda e: e.activation(car[:, 0, sb:sb + 1], s_r[:, W - 1:W], AF.Copy), reads=[s_r], writes=[car])
                S.op("act", lambda e: e.activation(car[:, 1, sb:sb + 1], s_i[:, W - 1:W], AF.Copy), reads=[s_i], writes=[car])
                yo = pyt[:, fb * W:(fb + 1) * W]
                S.op("pe", lambda e: e.matmul(yo, cblkb[:, d, 0, sb, :], srb[i2][:, :], start=(sb % 4 == 0), stop=False),
                     reads=[cblkb, srb[i2]], writes=[pyt])
                S.op("pe", lambda e: e.matmul(yo, cblkb[:, d, 1, sb, :], sib[i2][:, :], start=False, stop=(sb % 4 == 3)),
                     reads=[cblkb, sib[i2]], writes=[pyt])
            if d == 0:
                S.op("act", lambda e: e.activation(ych[:, :, :].rearrange("p f w -> p (f w)"), pyt[:, :], AF.Copy), reads=[pyt], writes=[ych])
                S.dma("sp", yfv[:, :, t0:t0 + W], ych[:, :, :], reads=[ych], writes=[yfd])
            else:
                for fb in range(4):
                    S.op("dve", lambda e: e.tensor_tensor(ych[:, fb, :], pyt[:, fb * W:(fb + 1) * W][:, ::-1], yfs[:, fb, :], ALU.add),
                         reads=[pyt, yfs], writes=[ych])
                    S.op("dve", lambda e: e.scalar_tensor_tensor(ych[:, fb, :], u[:, fb, t0:t0 + W], dsk[:, fb:fb + 1], ych[:, fb, :],
                                                                 ALU.mult, ALU.add), reads=[u, dsk, ych], writes=[ych])
                S.op("act", lambda e: e.activation(g1[:, :, :], ych[:, :, :], AF.Square), reads=[ych], writes=[g1])
                S.op("dve", lambda e: e.tensor_scalar(g1[:, :, :], g1[:, :, :], 0.044715, 1.0, ALU.mult, ALU.add), reads=[g1], writes=[g1])
                S.op("dve", lambda e: e.tensor_tensor(g1[:, :, :], g1[:, :, :], ych[:, :, :], ALU.mult), reads=[g1, ych], writes=[g1])
                S.op("act", lambda e: e.activation(g1[:, :, :], g1[:, :, :], AF.Sigmoid, scale=1.5957691216057308), reads=[g1], writes=[g1])
                S.op("dve", lambda e: e.tensor_tensor(yob[:, :, :], g1[:, :, :], ych[:, :, :], ALU.mult), reads=[g1, ych], writes=[yob])
                for (r, o, ts, ln) in seq_ranges(t0, W, TL):
                    S.dma("sp", yv[r][:, :, o:o + ln], yob[:, :, ts - t0:ts - t0 + ln], reads=[yob], writes=[yout])
    S.barrier()
    return nc


def s5_host(inp, j, b, TL):
    gs = slice(32 * b, 32 * (b + 1))
    def st_layout(a):
        return np.ascontiguousarray(a.reshape(16, 2, 64).transpose(1, 2, 0).reshape(128, 16))
    lam = np.zeros((128, 2, 2, 16), np.float32)
    stp = np.zeros((128, 2, 16), np.float32)
    for d in range(2):
        lam[:, 0, d] = st_layout(inp["s5_lam_re"][j][d, gs])
        lam[:, 1, d] = st_layout(inp["s5_lam_im"][j][d, gs])
        stp[:, d] = st_layout(np.broadcast_to(inp["s5_log_step"][j][d, gs][:, None], (32, 64)))
    bblk = np.zeros((128, 2, 16, 128), np.float32)
    for ri, key in enumerate(("s5_b_re", "s5_b_im")):
        B = inp[key][j][gs]
        for sb in range(16):
            for gg in range(2):
                g = 2 * sb + gg
                rows = 32 * (sb % 4) + 16 * gg
                bblk[rows:rows + 16, ri, sb, 64 * gg:64 * gg + 64] = B[g].T
    cblk = np.zeros((128, 2, 2, 16, 128), np.float32)
    for ri, key in enumerate(("s5_c_re", "s5_c_im")):
        for d in range(2):
            C = inp[key][j][d, gs]
            for sb in range(16):
                for gg in range(2):
                    g = 2 * sb + gg
                    co = 32 * (sb % 4) + 16 * gg
                    cblk[64 * gg:64 * gg + 64, d, ri, sb, co:co + 16] = C[g].T
    dsk = pvec(inp["s5_d"][j][512 * b:512 * (b + 1)])
    jrow = np.ascontiguousarray(np.broadcast_to(np.arange(1, S5W + 1, dtype=np.float32)[None], (128, S5W)))
    return {"lam": lam, "stp": stp, "bblk": bblk, "cblk": cblk, "dsk": dsk, "jrow": jrow}


_PROGS = {}
TIMES = []


def _prog(key, fn):
    if key not in _PROGS:
        _PROGS[key] = fn()
    return _PROGS[key]


def _run(nc, maps):
    import time
    t0 = time.time()
    res = run_bass_kernel_spmd(nc, maps, core_ids=list(range(NR)))
    TIMES.append(time.time() - t0)
    return res.results


def ffn_host(inp, i, c):
    cols = np.r_[512 * c:512 * c + 512, D + 512 * c:D + 512 * c + 512]
    wc = np.ascontiguousarray(inp["ffn_up"][i][:, cols])
    cp = np.concatenate([inp["ffn_conv_w"][i][:, cols], inp["ffn_conv_b"][i][None, cols]], 0)
    cp = np.ascontiguousarray(cp.T.reshape(8, 128, 4).transpose(1, 0, 2))
    return {"w": wc, "convp": cp}


def make_vec(g, gain, scale, shift, ba=None, bb=None):
    v = np.zeros((128, NVEC, KD), np.float32)
    if g is not None:
        v[:, 0], v[:, 1] = pvec(g[:, 0]), pvec(g[:, 1])
    if gain is not None:
        v[:, 2] = pvec(gain)
        v[:, 3], v[:, 4] = pvec(scale[:, 0]), pvec(scale[:, 1])
        v[:, 5], v[:, 6] = pvec(shift[:, 0]), pvec(shift[:, 1])
    if ba is not None:
        v[:, 7], v[:, 8] = pvec(ba), pvec(bb)
    return v


def forward(inp, TL, depth=4, x=None, ctx=None):
    TT, NT = dims(TL)
    x = inp["x"][0] if x is None else x
    ctx = inp["ctx"][0] if ctx is None else ctx
    cvec = np.stack([inp["c_ctx"], inp["c"][0]], -1).astype(np.float32)
    cvec = np.ascontiguousarray(cvec.reshape(KD, 128, 2).transpose(1, 0, 2))
    ncm = _prog(("mod", depth), lambda: build_mod(depth))
    maps = []
    for c in range(NR):
        cs = slice(MODC * c, MODC * (c + 1))
        modb = inp["mod_b"][:depth, cs]
        maps.append({"cvec": cvec, "modw": np.ascontiguousarray(inp["mod_w"][:depth, :, cs]),
                     "modb": np.ascontiguousarray(modb.reshape(depth, MODB, 128).transpose(2, 0, 1))})
    res = _run(ncm, maps)
    mod = np.zeros((depth, 6 * D, 2), np.float32)
    for c in range(NR):
        o = res[c]["modo"]
        mod[:, MODC * c:MODC * (c + 1), :] = o.transpose(1, 2, 0, 3).reshape(depth, MODC, 2)
    mods = mod.reshape(depth, 6, D, 2)

    xs = [np.ascontiguousarray(np.concatenate([ctx[TCTX * r:TCTX * (r + 1)], x[TL * r:TL * (r + 1)]], 0).T) for r in range(NR)]
    ncn = _prog(("B", 0, False, TL, False, True), lambda: build_B(128, False, TL, gemm=False, do_norm=True))
    vec = make_vec(None, inp["norm_mix"][0], mods[0, 1], mods[0, 0])
    res = _run(ncn, [{"xres": xs[r], "vec": vec} for r in range(NR)])
    hs = [res[r]["hnext"] for r in range(NR)]

    for i in range(depth):
        kind, j = i % 3, i // 3
        hfull = np.ascontiguousarray(np.stack(hs, 0))
        want_ctx = i < depth - 1
        if kind == 0:
            nca = _prog(("ssdA", TL), lambda: build_ssdA(TL))
            outs = _run(nca, [dict(ssd_host(inp, j, c), hfull=hfull) for c in range(NR)])
            Kb, glu, wB = 8192, False, inp["ssd_w_out"][j]
            ba = bb = None
        elif kind == 1:
            nca = _prog(("s5A", TL), lambda: build_s5A(TL))
            outs = _run(nca, [dict(s5_host(inp, j, c, TL), hs5=np.ascontiguousarray(hfull[:, 512 * c:512 * (c + 1), :]))
                              for c in range(NR)])
            Kb, glu, wB = 4096, True, inp["s5_glu_w"][j]
            ba, bb = inp["s5_glu_b"][j][:D], inp["s5_glu_b"][j][D:]
        else:
            li = 0.8 - 0.6 * math.exp(-0.3 * i)
            nca = _prog(("attA", TL, i, want_ctx), lambda: build_attA(TL, li, want_ctx))
            outs = _run(nca, [dict(att_host(inp, j, c, TL, li), hfull=hfull) for c in range(NR)])
            Kb, glu, wB = 4096, False, inp["da_w_o"][j]
            ba = bb = None
        yin = gather_tokens(outs, "yout")
        ncb = _prog(("B", Kb, glu, TL, True, True), lambda: build_B(Kb, glu, TL))
        vec = make_vec(mods[i, 2], inp["norm_ffn"][i], mods[i, 4], mods[i, 3], ba, bb)
        wB = np.ascontiguousarray(wB)
        res = _run(ncb, [{"xres": xs[r], "vec": vec, "yin": yin[r], "w": wB} for r in range(NR)])
        xs = [res[r]["xnew"] for r in range(NR)]
        hs = [res[r]["hnext"] for r in range(NR)]
        hfull = np.ascontiguousarray(np.stack(hs, 0))
        ncf = _prog(("ffnA", TL), lambda: build_ffnA(TL))
        outs = _run(ncf, [dict(ffn_host(inp, i, c), hfull=hfull) for c in range(NR)])
        yin = gather_tokens(outs, "yout")
        last = i == depth - 1
        ncb = _prog(("B", 4096, False, TL, True, not last), lambda: build_B(4096, False, TL, do_norm=not last))
        if last:
            vec = make_vec(mods[i, 5], None, None, None)
        else:
            vec = make_vec(mods[i, 5], inp["norm_mix"][i + 1], mods[i + 1, 1], mods[i + 1, 0])
        wB = np.ascontiguousarray(inp["ffn_down"][i])
        res = _run(ncb, [{"xres": xs[r], "vec": vec, "yin": yin[r], "w": wB} for r in range(NR)])
        xs = [res[r]["xnew"] for r in range(NR)]
        if not last:
            hs = [res[r]["hnext"] for r in range(NR)]
    out = np.concatenate([xs[r][:, TCTX:].T for r in range(NR)], 0)
    ctx_out = np.concatenate([xs[r][:, :TCTX].T for r in range(NR)], 0)
    return np.ascontiguousarray(out[None]).astype(np.float32), ctx_out


def kernel(**inputs):
    inp = {k: np.asarray(v) for k, v in inputs.items()}
    out, _ = forward(inp, TLAT)
    import sys
    print("launch wall times (s):", [round(t, 1) for t in TIMES], file=sys.stderr)
    return out
```

```python
import contextlib
import math
import numpy as np
import ml_dtypes
import concourse.bass as bass
import concourse.mybir as mybir
from concourse.bass_utils import run_bass_kernel_spmd

F32 = mybir.dt.float32
BF16 = mybir.dt.bfloat16
AF = mybir.ActivationFunctionType
ALU = mybir.AluOpType
AX = mybir.AxisListType

NR = 8
D = 4096
KD = D // 128
TCTX = 32
CTXN = NR * TCTX
EPS = 1e-6
TLAT = 1024


def dims(TL):
    TT = TCTX + TL
    NT = CTXN + NR * TL
    return TT, NT


class Buf:
    __slots__ = ("name", "w", "r")

    def __init__(self, name=""):
        self.name = name
        self.w = None
        self.r = {}


class T:
    def __init__(self, h, name=""):
        self.h = h
        self.b = Buf(name)

    def __getitem__(self, idx):
        return self.h[idx]


class Sched:
    SEM_ROT = 30000

    def __init__(self, nc, n_dma_sems=16):
        self.nc = nc
        self.eng = {"pe": nc.tensor, "act": nc.scalar, "dve": nc.vector,
                    "pool": nc.gpsimd, "sp": nc.sync}
        self.sem, self.cnt, self.semgen = {}, {}, {}
        for k in self.eng:
            self.semgen[k] = 0
            self.sem[k] = nc.alloc_semaphore(f"s_{k}_0")
            self.cnt[k] = 0
        self.seen = {k: {} for k in self.eng}
        self.dsem = [nc.alloc_semaphore(f"d_{i}") for i in range(n_dma_sems)]
        self.dcnt = [0] * n_dma_sems
        self.dnext = 0
        self.n_inst = 0
        self.n_wait = 0
        self.stack = contextlib.ExitStack()

    def _wait(self, e, dep):
        if dep is None:
            return
        key, val = dep
        if key[0] == "e" and key[1] == e and e in ("pe", "sp"):
            return
        if self.seen[e].get(key, 0) >= val:
            return
        self.seen[e][key] = val
        self.eng[e].wait_ge(key[2], val)
        self.n_wait += 1

    def _deps(self, e, reads, writes):
        for b in reads:
            self._wait(e, b.w)
        for b in writes:
            self._wait(e, b.w)
            for k, v in b.r.items():
                self._wait(e, (k, v))

    def _mark(self, me, reads, writes):
        k, v = me
        for b in reads:
            if b.r.get(k, 0) < v:
                b.r[k] = v
        for b in writes:
            b.w = me
            b.r = {}

    def op(self, e, fn, reads=(), writes=()):
        reads = [t.b if isinstance(t, T) else t for t in reads]
        writes = [t.b if isinstance(t, T) else t for t in writes]
        self._deps(e, reads, writes)
        if self.cnt[e] >= self.SEM_ROT:
            self.semgen[e] += 1
            self.sem[e] = self.nc.alloc_semaphore(f"s_{e}_{self.semgen[e]}")
            self.cnt[e] = 0
        inst = fn(self.eng[e])
        self.cnt[e] += 1
        inst.then_inc(self.sem[e], 1)
        me = (("e", e, self.sem[e]), self.cnt[e])
        self._mark(me, reads, writes)
        self.n_inst += 1
        return inst

    def dma(self, q, out, in_, reads=(), writes=(), **kw):
        reads = [t.b if isinstance(t, T) else t for t in reads]
        writes = [t.b if isinstance(t, T) else t for t in writes]
        i = self.dnext
        self.dnext = (self.dnext + 1) % len(self.dsem)
        key = ("d", i, self.dsem[i])
        if self.dcnt[i] > 0:
            self._wait(q, (key, self.dcnt[i]))
        self._deps(q, reads, writes)
        inst = self.eng[q].dma_start(out=out, in_=in_, **kw)
        self.dcnt[i] += 16
        inst.then_inc(self.dsem[i], 16)
        me = (key, self.dcnt[i])
        self._mark(me, reads, writes)
        self.n_inst += 1
        return inst

    def coll(self, kind, in_ap, out_ap, reads=(), writes=(), q="pool"):
        reads = [t.b if isinstance(t, T) else t for t in reads]
        writes = [t.b if isinstance(t, T) else t for t in writes]
        i = self.dnext
        self.dnext = (self.dnext + 1) % len(self.dsem)
        key = ("d", i, self.dsem[i])
        if self.dcnt[i] > 0:
            self._wait(q, (key, self.dcnt[i]))
        self._deps(q, reads, writes)
        inst = self.nc.gpsimd.collective_compute(kind, ALU.bypass, [list(range(NR))], [in_ap], [out_ap])
        self.dcnt[i] += 16
        inst.then_inc(self.dsem[i], 16)
        me = (key, self.dcnt[i])
        self._mark(me, reads, writes)
        self.n_inst += 1
        return inst

    def barrier(self):
        for e in self.eng:
            for e2 in self.eng:
                if e2 != e and self.cnt[e2] > 0:
                    self._wait(e, (("e", e2, self.sem[e2]), self.cnt[e2]))
            for i, s in enumerate(self.dsem):
                if self.dcnt[i] > 0:
                    self._wait(e, (("d", i, s), self.dcnt[i]))

    def sb(self, name, shape, dt, stack=None):
        st = stack if stack is not None else self.stack
        self.uid = getattr(self, "uid", 0) + 1
        h = st.enter_context(self.nc.sbuf_tensor(f"sb{self.uid}_{name}", list(shape), dt))
        return T(h, name)

    def ps(self, name, shape=(128, 512), dt=F32, stack=None):
        st = stack if stack is not None else self.stack
        self.uid = getattr(self, "uid", 0) + 1
        h = st.enter_context(self.nc.psum_tensor(f"ps{self.uid}_{name}", list(shape), dt))
        return T(h, name)


def new_prog():
    nc = bass.Bass("TRN2", target_bir_lowering=False)
    return nc, Sched(nc)


def din(nc, name, shape, dt=F32):
    return T(nc.dram_tensor(name, list(shape), dt, kind="ExternalInput"), name)


def dout(nc, name, shape, dt=F32):
    return T(nc.dram_tensor(name, list(shape), dt, kind="ExternalOutput"), name)


def dscr(nc, name, shape, dt=F32):
    return T(nc.dram_tensor(name, list(shape), dt, kind="Internal"), name)


def tok_tiles(n, maxn=512):
    k = (n + maxn - 1) // maxn
    assert n % k == 0, (n, k)
    return k, n // k


MODC = 6 * D // NR
MODB = MODC // 128


def build_mod(nlayers=4):
    nc, S = new_prog()
    cvec = din(nc, "cvec", [128, KD, 2])
    modw = din(nc, "modw", [nlayers, D, MODC])
    modb = din(nc, "modb", [128, nlayers, MODB])
    out = dout(nc, "modo", [128, nlayers, MODB, 2])
    cs = S.sb("cs", [128, KD, 2], F32)
    cb = S.sb("cb", [128, KD, 2], BF16)
    bs = S.sb("bs", [128, nlayers, MODB], F32)
    os_ = S.sb("os", [128, nlayers, MODB, 2], F32)
    wt = [S.sb(f"wt{i}", [128, KD, 512], BF16) for i in range(2)]
    pt = [S.ps(f"pt{i}") for i in range(2)]
    S.dma("sp", cs[:, :, :], cvec[:, :, :], reads=[cvec], writes=[cs])
    S.dma("sp", bs[:, :, :], modb[:, :, :], reads=[modb], writes=[bs])
    S.op("act", lambda e: e.activation(cb[:, :, :], cs[:, :, :], AF.Silu), reads=[cs], writes=[cb])
    it = 0
    for l in range(nlayers):
        wv = modw.h.ap()[l].rearrange("(k p) c -> p k c", p=128)
        for g in range(MODB // 4):
            w = wt[it % 2]
            p = pt[it % 2]
            it += 1
            S.dma("pool", w[:, :, :], wv[:, :, g * 512:(g + 1) * 512], reads=[modw], writes=[w])
            for j in range(4):
                for k in range(KD):
                    S.op("pe", lambda e: e.matmul(p[:, 2 * j:2 * j + 2], w[:, k, j * 128:(j + 1) * 128],
                                                  cb[:, k, :], start=(k == 0), stop=(k == KD - 1)),
                         reads=[w, cb], writes=[p])
            for j in range(4):
                b = g * 4 + j
                S.op("dve", lambda e: e.tensor_scalar(os_[:, l, b, :], p[:, 2 * j:2 * j + 2],
                                                      bs[:, l, b:b + 1], None, ALU.add),
                     reads=[p, bs], writes=[os_])
    S.dma("sp", out[:, :, :, :], os_[:, :, :, :], reads=[os_], writes=[out])
    S.barrier()
    return nc


NVEC = 9


def emit_norm_mod(S, xn, ss_ps, vec, hout, tsl, ntok, c0n, st):
    rstd = S.sb("rstd", [128, ntok], F32, st)
    s1 = S.sb("s1", [128, 2, KD], F32, st)
    hb = [S.sb(f"hb{i}", [128, ntok], BF16, st) for i in range(2)]
    tmp = [S.sb(f"ntmp{i}", [128, ntok], F32, st) for i in range(2)]
    S.op("dve", lambda e: e.tensor_scalar(rstd[:, :], ss_ps[:, 0:ntok], 1.0 / D, EPS, ALU.mult, ALU.add),
         reads=[ss_ps], writes=[rstd])
    S.op("act", lambda e: e.activation(rstd[:, :], rstd[:, :], AF.Sqrt), reads=[rstd], writes=[rstd])
    S.op("dve", lambda e: e.reciprocal(rstd[:, :], rstd[:, :]), reads=[rstd], writes=[rstd])
    for j in range(2):
        S.op("dve", lambda e: e.scalar_tensor_tensor(s1[:, j, :], vec[:, 3 + j, :], 1.0, vec[:, 2, :],
                                                     ALU.add, ALU.mult), reads=[vec], writes=[s1])
    for ob in range(KD):
        t = tmp[ob % 2]
        h = hb[ob % 2]
        S.op("pool", lambda e: e.tensor_tensor(t[:, :], xn[:, ob, :], rstd[:, :], ALU.mult),
             reads=[xn, rstd], writes=[t])
        if c0n > 0:
            S.op("dve", lambda e: e.tensor_scalar(h[:, 0:c0n], t[:, 0:c0n], s1[:, 0, ob:ob + 1],
                                                  vec[:, 5, ob:ob + 1], ALU.mult, ALU.add),
                 reads=[t, s1, vec], writes=[h])
        S.op("dve", lambda e: e.tensor_scalar(h[:, c0n:ntok], t[:, c0n:ntok], s1[:, 1, ob:ob + 1],
                                              vec[:, 6, ob:ob + 1], ALU.mult, ALU.add),
             reads=[t, s1, vec], writes=[h])
        S.dma("sp", hout.h.ap()[ob * 128:(ob + 1) * 128, tsl], h[:, :], reads=[h], writes=[hout])


def build_B(K, glu, TL, gemm=True, do_norm=True):
    TT, NT = dims(TL)
    nc, S = new_prog()
    KC = K // 128
    Cout = 2 * D if glu else D
    xres = din(nc, "xres", [D, TT])
    vecd = din(nc, "vec", [128, NVEC, KD])
    if gemm:
        yin = din(nc, "yin", [K, TT], BF16)
        w = din(nc, "w", [K, Cout])
    xnew = dout(nc, "xnew", [D, TT])
    hnext = dout(nc, "hnext", [D, TT], BF16) if do_norm else None
    ntile, ntok = tok_tiles(TT, 512)
    CG = 256
    vec = S.sb("vec", [128, NVEC, KD], F32)
    ones = S.sb("ones", [128, 128], F32)
    S.dma("sp", vec[:, :, :], vecd[:, :, :], reads=[vecd], writes=[vec])
    S.op("pool", lambda e: e.memset(ones[:, :], 1.0), writes=[ones])
    xn = S.sb("xn", [128, KD, ntok], F32)
    ss = S.ps("ss")
    if gemm:
        yt = S.sb("yt", [128, KC, ntok], BF16)
        ngrp = 2 if glu else 1
        wt = [S.sb(f"wt{i}", [128, KC, ngrp, CG], BF16) for i in range(2)]
        pa = [S.ps(f"pa{i}") for i in range(2)]
        pb = [S.ps(f"pb{i}") for i in range(2)] if glu else None
        sg = [S.sb(f"sg{i}", [128, ntok], F32) for i in range(2)] if glu else None
        za = [S.sb(f"za{i}", [128, ntok], F32) for i in range(2)] if glu else None
    xr = [S.sb(f"xr{i}", [128, ntok], F32) for i in range(2)]
    sq = [S.sb(f"sq{i}", [128, ntok], F32) for i in range(2)]
    wit = 0
    for tt in range(ntile):
        tsl = slice(tt * ntok, (tt + 1) * ntok)
        c0n = TCTX if tt == 0 else 0
        with contextlib.ExitStack() as st:
            if gemm:
                S.dma("sp", yt[:, :, :], yin.h.ap().rearrange("(k p) t -> p k t", p=128)[:, :, tsl],
                      reads=[yin], writes=[yt])
            for cg in range(D // CG):
                if gemm:
                    wtile = wt[wit % 2]
                    wit += 1
                    wv = w.h.ap().rearrange("(k p) c -> p k c", p=128)
                    S.dma("pool", wtile[:, :, 0, :], wv[:, :, cg * CG:(cg + 1) * CG], reads=[w], writes=[wtile])
                    if glu:
                        S.dma("pool", wtile[:, :, 1, :], wv[:, :, D + cg * CG:D + (cg + 1) * CG],
                              reads=[w], writes=[wtile])
                for j in range(CG // 128):
                    ob = cg * (CG // 128) + j
                    x_ = xr[ob % 2]
                    S.dma("sp", x_[:, :], xres.h.ap()[ob * 128:(ob + 1) * 128, tsl], reads=[xres], writes=[x_])
                    if gemm:
                        p = pa[ob % 2]
                        for k in range(KC):
                            S.op("pe", lambda e: e.matmul(p[:, 0:ntok], wtile[:, k, 0, j * 128:(j + 1) * 128],
                                                          yt[:, k, :], start=(k == 0), stop=(k == KC - 1)),
                                 reads=[wtile, yt], writes=[p])
                        src = p
                        if glu:
                            q = pb[ob % 2]
                            for k in range(KC):
                                S.op("pe", lambda e: e.matmul(q[:, 0:ntok], wtile[:, k, 1, j * 128:(j + 1) * 128],
                                                              yt[:, k, :], start=(k == 0), stop=(k == KC - 1)),
                                     reads=[wtile, yt], writes=[q])
                            s_ = sg[ob % 2]
                            z_ = za[ob % 2]
                            S.op("act", lambda e: e.activation(s_[:, :], q[:, 0:ntok], AF.Sigmoid,
                                                               bias=vec[:, 8, ob:ob + 1]),
                                 reads=[q, vec], writes=[s_])
                            S.op("dve", lambda e: e.scalar_tensor_tensor(z_[:, :], p[:, 0:ntok], vec[:, 7, ob:ob + 1],
                                                                         s_[:, :], ALU.add, ALU.mult),
                                 reads=[p, vec, s_], writes=[z_])
                            src = z_
                        if c0n > 0:
                            S.op("dve", lambda e: e.scalar_tensor_tensor(
                                xn[:, ob, 0:c0n], src[:, 0:c0n], vec[:, 0, ob:ob + 1], x_[:, 0:c0n],
                                ALU.mult, ALU.add), reads=[src, vec, x_], writes=[xn])
                        S.op("dve", lambda e: e.scalar_tensor_tensor(
                            xn[:, ob, c0n:ntok], src[:, c0n:ntok], vec[:, 1, ob:ob + 1], x_[:, c0n:ntok],
                            ALU.mult, ALU.add), reads=[src, vec, x_], writes=[xn])
                    else:
                        S.op("dve", lambda e: e.tensor_copy(xn[:, ob, :], x_[:, :]), reads=[x_], writes=[xn])
                    S.dma("sp", xnew.h.ap()[ob * 128:(ob + 1) * 128, tsl], xn[:, ob, :], reads=[xn], writes=[xnew])
                    if do_norm:
                        s2 = sq[ob % 2]
                        S.op("act", lambda e: e.activation(s2[:, :], xn[:, ob, :], AF.Square),
                             reads=[xn], writes=[s2])
                        S.op("pe", lambda e: e.matmul(ss[:, 0:ntok], ones[:, :], s2[:, :],
                                                      start=(ob == 0), stop=(ob == KD - 1)),
                             reads=[ones, s2], writes=[ss])
            if do_norm:
                emit_norm_mod(S, xn, ss, vec, hnext, tsl, ntok, c0n, st)
            S.barrier()
    S.barrier()
    return nc


def seq_ranges(t0, n, TL):
    out = []
    t = t0
    end = t0 + n
    while t < end:
        if t < CTXN:
            r, o = divmod(t, TCTX)
            ln = min(TCTX - o, end - t)
            out.append((r, o, t, ln))
        else:
            r, o = divmod(t - CTXN, TL)
            ln = min(TL - o, end - t)
            out.append((r, TCTX + o, t, ln))
        t += ln
    return out


def seq_tiles(TL, maxn=512):
    NT = CTXN + NR * TL
    tiles = [(0, CTXN)]
    t = CTXN
    while t < NT:
        n = min(maxn, NT - t)
        tiles.append((t, n))
        t += n
    return tiles


def proj_all(S, hfull, w, C, pre, TL, evac_hook=None):
    NB = C // 128
    tiles = seq_tiles(TL)
    with contextlib.ExitStack() as st:
        wt = [S.sb(f"pw{i}", [128, KD, 512], BF16, st) for i in range(2)]
        ht = [S.sb(f"ph{i}", [128, KD, 512], BF16, st) for i in range(2)]
        og = [S.sb(f"po{i}", [128, 512], F32, st) for i in range(4)]
        pp = [S.ps(f"pp{i}", stack=st) for i in range(4)]
        wv = w.h.ap().rearrange("(k p) c -> p k c", p=128)
        hit = 0
        oit = 0
        for g in range((NB + 3) // 4):
            nb = min(4, NB - g * 4)
            wtile = wt[g % 2]
            S.dma("pool", wtile[:, :, 0:nb * 128], wv[:, :, g * 512:g * 512 + nb * 128], reads=[w], writes=[wtile])
            for (t0, n) in tiles:
                h = ht[hit % 2]
                hit += 1
                for (r, o, ts, ln) in seq_ranges(t0, n, TL):
                    S.dma("sp", h[:, :, ts - t0:ts - t0 + ln],
                          hfull.h.ap()[r].rearrange("(k p) t -> p k t", p=128)[:, :, o:o + ln],
                          reads=[hfull], writes=[h])
                for j in range(nb):
                    p = pp[oit % 4]
                    o_ = og[oit % 4]
                    oit += 1
                    for k in range(KD):
                        S.op("pe", lambda e: e.matmul(p[:, 0:n], wtile[:, k, j * 128:(j + 1) * 128], h[:, k, 0:n],
                                                      start=(k == 0), stop=(k == KD - 1)),
                             reads=[wtile, h], writes=[p])
                    if oit % 2 == 0:
                        S.op("dve", lambda e: e.tensor_copy(o_[:, 0:n], p[:, 0:n]), reads=[p], writes=[o_])
                    else:
                        S.op("act", lambda e: e.activation(o_[:, 0:n], p[:, 0:n], AF.Copy), reads=[p], writes=[o_])
                    cb = g * 4 + j
                    S.dma("sp", pre.h.ap()[cb * 128:(cb + 1) * 128, t0:t0 + n], o_[:, 0:n], reads=[o_], writes=[pre])
        S.barrier()


def zlayout(TL, K):
    P = K // 2
    NT = CTXN + NR * TL
    return P, P, 3 * P + CTXN, NT + 4 * P


def load_padded(S, Z, pre, row0, nrows, TL, K):
    P, c0, l0, tot = zlayout(TL, K)
    NT = CTXN + NR * TL
    S.dma("sp", Z[0:nrows, c0:c0 + CTXN], pre.h.ap()[row0:row0 + nrows, 0:CTXN], reads=[pre], writes=[Z])
    S.dma("sp", Z[0:nrows, l0:l0 + NT - CTXN], pre.h.ap()[row0:row0 + nrows, CTXN:NT], reads=[pre], writes=[Z])


def conv_seq(S, Z, pT, wc, bias, acc, TL, K, nrows=128, eng="dve"):
    P, c0, l0, tot = zlayout(TL, K)
    NT = CTXN + NR * TL
    for (zs, os_, n) in ((c0, 0, CTXN), (l0, CTXN, NT - CTXN)):
        S.op(eng, lambda e: e.tensor_scalar(acc[0:nrows, os_:os_ + n], Z[0:nrows, zs - P:zs - P + n],
                                            wc[0:nrows, 0:1], bias[0:nrows, 0:1], ALU.mult, ALU.add),
             reads=[Z, pT], writes=[acc])
        for j in range(1, K):
            S.op(eng, lambda e: e.scalar_tensor_tensor(acc[0:nrows, os_:os_ + n], Z[0:nrows, zs - P + j:zs - P + j + n],
                                                       wc[0:nrows, j:j + 1], acc[0:nrows, os_:os_ + n],
                                                       ALU.mult, ALU.add),
                 reads=[Z, pT, acc], writes=[acc])


def store_yout(S, yout, ob, row0, TL, nrows=128):
    yv = yout.h.ap().rearrange("r c t -> c r t")
    S.dma("sp", yv[row0:row0 + nrows, :, 0:TCTX], ob[0:nrows, 0:CTXN].rearrange("p (r t) -> p r t", t=TCTX),
          reads=[ob], writes=[yout])
    S.dma("sp", yv[row0:row0 + nrows, :, TCTX:TCTX + TL], ob[0:nrows, CTXN:CTXN + NR * TL].rearrange("p (r t) -> p r t", t=TL),
          reads=[ob], writes=[yout])


def build_ffnA(TL):
    TT, NT = dims(TL)
    nc, S = new_prog()
    hfull = din(nc, "hfull", [NR, D, TT], BF16)
    w = din(nc, "w", [D, 1024])
    convp = din(nc, "convp", [128, 8, 4])
    yout = dout(nc, "yout", [NR, 512, TT], BF16)
    pre = dscr(nc, "pre", [1024, NT])
    proj_all(S, hfull, w, 1024, pre, TL)
    P, c0, l0, tot = zlayout(TL, 3)
    cp = S.sb("cp", [128, 8, 4], F32)
    S.dma("sp", cp[:, :, :], convp[:, :, :], reads=[convp], writes=[cp])
    Zg = S.sb("Zg", [128, tot], F32)
    Zv = S.sb("Zv", [128, tot], F32)
    ag = S.sb("ag", [128, NT], F32)
    av = S.sb("av", [128, NT], F32)
    ob = [S.sb(f"ob{i}", [128, NT], BF16) for i in range(2)]
    S.op("pool", lambda e: e.memset(Zg[:, :], 0.0), writes=[Zg])
    S.op("pool", lambda e: e.memset(Zv[:, :], 0.0), writes=[Zv])
    for i in range(4):
        load_padded(S, Zg, pre, i * 128, 128, TL, 3)
        load_padded(S, Zv, pre, 512 + i * 128, 128, TL, 3)
        conv_seq(S, Zg, cp, cp[:, i, 0:3], cp[:, i, 3:4], ag, TL, 3)
        conv_seq(S, Zv, cp, cp[:, 4 + i, 0:3], cp[:, 4 + i, 3:4], av, TL, 3)
        S.op("act", lambda e: e.activation(ag[:, :], ag[:, :], AF.Silu), reads=[ag], writes=[ag])
        o_ = ob[i % 2]
        S.op("dve", lambda e: e.tensor_tensor(o_[:, :], ag[:, :], av[:, :], ALU.mult), reads=[ag, av], writes=[o_])
        store_yout(S, yout, o_, i * 128, TL)
    S.barrier()
    return nc


SSD_C = 2432
NEG = -30000.0


def ssd_consts():
    c = np.zeros((128, 6, 128), np.float32)
    i = np.arange(128)
    c[:, 0, :] = np.eye(128)
    c[:, 1, :] = (i[:, None] <= i[None, :])
    c[:, 2, :] = (i[:, None] >= i[None, :])
    c[:, 3, :] = np.where(i[None, :] >= i[:, None], 0.0, NEG)
    c[:, 4, :] = np.where(i[None, :] <= i[:, None], 0.0, NEG)
    c[:, 5, :] = 1.0
    return c


def build_ssdA(TL):
    TT, NT = dims(TL)
    nc, S = new_prog()
    hfull = din(nc, "hfull", [NR, D, TT], BF16)
    w = din(nc, "w", [D, SSD_C])
    convp = din(nc, "convp", [128, 10, 6])
    hpd = din(nc, "hp", [128, 8, 2])
    dtpd = din(nc, "dtp", [32, 2])
    constd = din(nc, "consts", [128, 6, 128])
    yout = dout(nc, "yout", [NR, 1024, TT], BF16)
    pre = dscr(nc, "pre", [SSD_C, NT])
    xs = dscr(nc, "xs", [1024, NT], BF16)
    Bs = dscr(nc, "Bs", [128, NT], BF16)
    Cs = dscr(nc, "Cs", [128, NT], BF16)
    dts = dscr(nc, "dts", [32, NT])
    das = dscr(nc, "das", [32, NT])
    yf = dscr(nc, "yf", [1024, NT])

    proj_all(S, hfull, w, SSD_C, pre, TL)

    P, c0, l0, tot = zlayout(TL, 5)
    with contextlib.ExitStack() as st:
        cp = S.sb("cp", [128, 10, 6], F32, st)
        dtp = S.sb("dtp", [32, 2], F32, st)
        S.dma("sp", cp[:, :, :], convp[:, :, :], reads=[convp], writes=[cp])
        S.dma("sp", dtp[:, :], dtpd[:, :], reads=[dtpd], writes=[dtp])
        Z = S.sb("Z", [128, tot], F32, st)
        acc = S.sb("acc", [128, NT], F32, st)
        ob = S.sb("ob", [128, NT], BF16, st)
        S.op("pool", lambda e: e.memset(Z[:, :], 0.0), writes=[Z])
        for i in range(10):
            load_padded(S, Z, pre, 1024 + i * 128, 128, TL, 5)
            conv_seq(S, Z, cp, cp[:, i, 0:5], cp[:, i, 5:6], acc, TL, 5)
            S.op("act", lambda e: e.activation(ob[:, :], acc[:, :], AF.Silu), reads=[acc], writes=[ob])
            if i < 8:
                S.dma("sp", xs.h.ap()[i * 128:(i + 1) * 128, :], ob[:, :], reads=[ob], writes=[xs])
            else:
                dst = Bs if i == 8 else Cs
                S.dma("sp", dst.h.ap()[:, :], ob[:, :], reads=[ob], writes=[dst])
        dtt = S.sb("dtt", [32, NT], F32, st)
        dat = S.sb("dat", [32, NT], F32, st)
        av = S.sb("av", [32, 1], F32, st)
        S.dma("sp", dtt[:, :], pre.h.ap()[2304:2336, :], reads=[pre], writes=[dtt])
        S.op("act", lambda e: e.activation(dtt[:, :], dtt[:, :], AF.Exp, bias=dtp[:, 0:1]), reads=[dtt, dtp], writes=[dtt])
        S.op("act", lambda e: e.activation(av[:, :], dtp[:, 1:2], AF.Exp), reads=[dtp], writes=[av])
        S.op("act", lambda e: e.activation(dtt[:, :], dtt[:, :], AF.Ln, bias=1.0), reads=[dtt], writes=[dtt])
        S.op("dve", lambda e: e.tensor_scalar(dat[:, :], dtt[:, :], av[:, 0:1], -1.0, ALU.mult, ALU.mult),
             reads=[dtt, av], writes=[dat])
        S.dma("sp", dts.h.ap()[:, :], dtt[:, :], reads=[dtt], writes=[dts])
        S.dma("sp", das.h.ap()[:, :], dat[:, :], reads=[dat], writes=[das])
        S.barrier()

    nchunk_c = CTXN // 128
    nchunk_l = NR * TL // 128
    ctx_chunks = list(range(nchunk_c))
    lat_chunks = list(range(nchunk_c, nchunk_c + nchunk_l))
    with contextlib.ExitStack() as st:
        cst = S.sb("cst", [128, 6, 128], F32, st)
        cstb = S.sb("cstb", [128, 128], BF16, st)
        hp = S.sb("hp", [128, 8, 2], F32, st)
        S.dma("sp", cst[:, :, :], constd[:, :, :], reads=[constd], writes=[cst])
        S.dma("sp", hp[:, :, :], hpd[:, :, :], reads=[hpd], writes=[hp])
        S.op("dve", lambda e: e.tensor_copy(cstb[:, :], cst[:, 0, :]), reads=[cst], writes=[cstb])
        H = S.sb("H", [128, 16, 64], F32, st)
        Hb = S.sb("Hb", [128, 16, 64], BF16, st)
        Ht = S.sb("Ht", [128, 16, 64], F32, st)
        xT = S.sb("xT", [128, 8, 128], BF16, st)
        BT = S.sb("BT", [128, 128], BF16, st)
        CT = S.sb("CT", [128, 128], BF16, st)
        dd = S.sb("dd", [32, 2, 128], F32, st)
        xtok = S.sb("xtok", [128, 16, 64], BF16, st)
        xdt = S.sb("xdt", [128, 16, 64], BF16, st)
        xdtw = S.sb("xdtw", [128, 16, 64], BF16, st)
        Btok = S.sb("Btok", [128, 128], BF16, st)
        dtok = S.sb("dtok", [128, 2, 32], F32, st)
        cum = S.sb("cum", [128, 16], F32, st)
        ncum = S.sb("ncum", [128, 16], F32, st)
        totb = S.sb("totb", [128, 16], F32, st)
        cd = S.sb("cd", [128, 16], F32, st)
        dte = S.sb("dte", [128, 16], F32, st)
        cbT = S.sb("cbT", [128, 128], F32, st)
        Ecum = S.sb("Ecum", [128, 4, 128], F32, st)
        Ck = S.sb("Ck", [128, 4, 128], BF16, st)
        Ek = [S.sb(f"Ek{i}", [128, 128], F32, st) for i in range(2)]
        Mk = [S.sb(f"Mk{i}", [128, 128], BF16, st) for i in range(2)]
        yfs = S.sb("yfs", [128, 8, 128], F32, st)
        zc = S.sb("zc", [128, 8, 128], F32, st)
        ych = S.sb("ych", [128, 8, 128], F32, st)
        ysq = S.sb("ysq", [128, 128], F32, st)
        rs = S.sb("rs", [128, 128], F32, st)
        yob = S.sb("yob", [128, 8, 128], BF16, st)
        p_misc = S.ps("p_misc", stack=st)
        p_xt = S.ps("p_xt", [128, 1024], BF16, st)
        p_s = S.ps("p_s", stack=st)
        p_d = S.ps("p_d", stack=st)
        p_m = S.ps("p_m", stack=st)
        p_y = [S.ps(f"p_y{i}", stack=st) for i in range(2)]

        for d in range(2):
            tri = cst[:, 1 + d, :]
            negm = cst[:, 3 + d, :]
            S.op("pool", lambda e: e.memset(H[:, :, :], 0.0), writes=[H])
            S.op("pool", lambda e: e.memset(Hb[:, :, :], 0.0), writes=[Hb])
            order = (ctx_chunks + lat_chunks) if d == 0 else (ctx_chunks[::-1] + lat_chunks[::-1])
            for c in order:
                t0 = c * 128
                tsl = slice(t0, t0 + 128)
                S.dma("sp", xT[:, :, :], xs.h.ap().rearrange("(b p) t -> p b t", p=128)[:, :, tsl], reads=[xs], writes=[xT])
                S.dma("sp", BT[:, :], Bs.h.ap()[:, tsl], reads=[Bs], writes=[BT])
                S.dma("sp", CT[:, :], Cs.h.ap()[:, tsl], reads=[Cs], writes=[CT])
                S.dma("sp", dd[:, 0, :], dts.h.ap()[:, tsl], reads=[dts], writes=[dd])
                S.dma("sp", dd[:, 1, :], das.h.ap()[:, tsl], reads=[das], writes=[dd])
                if d == 1:
                    S.dma("sp", yfs[:, :, :], yf.h.ap().rearrange("(b p) t -> p b t", p=128)[:, :, tsl], reads=[yf], writes=[yfs])
                    S.dma("sp", zc[:, :, :], pre.h.ap()[0:1024, :].rearrange("(b p) t -> p b t", p=128)[:, :, tsl],
                          reads=[pre], writes=[zc])
                for b in range(8):
                    S.op("pe", lambda e: e.transpose(p_xt[:, b * 128:(b + 1) * 128], xT[:, b, :], cstb[:, :]),
                         reads=[xT, cstb], writes=[p_xt])
                S.op("act", lambda e: e.activation(xtok[:, :, :].rearrange("p k q -> p (k q)"), p_xt[:, :], AF.Copy),
                     reads=[p_xt], writes=[xtok])
                S.op("pe", lambda e: e.matmul(p_misc[:, 384:512], BT[:, :], cstb[:, :], start=True, stop=True),
                     reads=[BT, cstb], writes=[p_misc])
                S.op("dve", lambda e: e.tensor_copy(Btok[:, :], p_misc[:, 384:512]), reads=[p_misc], writes=[Btok])
                for j in range(2):
                    S.op("pe", lambda e: e.transpose(p_misc[:, 32 + 32 * j:64 + 32 * j], dd[:, j, :], cst[0:32, 0, 0:32]),
                         reads=[dd, cst], writes=[p_misc])
                S.op("dve", lambda e: e.tensor_copy(dtok[:, :, :].rearrange("p a b -> p (a b)"), p_misc[:, 32:96]),
                     reads=[p_misc], writes=[dtok])
                dk = 16 * d
                S.op("pe", lambda e: e.matmul(p_misc[:, 0:16], tri, dtok[:, 1, dk:dk + 16], start=True, stop=True),
                     reads=[cst, dtok], writes=[p_misc])
                S.op("pe", lambda e: e.matmul(p_misc[:, 16:32], cst[:, 5, :], dtok[:, 1, dk:dk + 16], start=True, stop=True),
                     reads=[cst, dtok], writes=[p_misc])
                S.op("pe", lambda e: e.matmul(p_misc[:, 128:256], BT[:, :], CT[:, :], start=True, stop=True),
                     reads=[BT, CT], writes=[p_misc])
                S.op("dve", lambda e: e.tensor_copy(cum[:, :], p_misc[:, 0:16]), reads=[p_misc], writes=[cum])
                S.op("dve", lambda e: e.tensor_scalar(ncum[:, :], p_misc[:, 0:16], -1.0, None, ALU.mult),
                     reads=[p_misc], writes=[ncum])
                S.op("dve", lambda e: e.tensor_copy(totb[:, :], p_misc[:, 16:32]), reads=[p_misc], writes=[totb])
                S.op("dve", lambda e: e.tensor_copy(cbT[:, :], p_misc[:, 128:256]), reads=[p_misc], writes=[cbT])
                S.op("act", lambda e: e.activation(cd[:, :], totb[:, :], AF.Exp), reads=[totb], writes=[cd])
                S.op("dve", lambda e: e.tensor_tensor(dte[:, :], totb[:, :], cum[:, :], ALU.subtract),
                     reads=[totb, cum], writes=[dte])
                S.op("act", lambda e: e.activation(dte[:, :], dte[:, :], AF.Exp), reads=[dte], writes=[dte])
                S.op("dve", lambda e: e.tensor_tensor(xdt[:, :, :], xtok[:, :, :],
                                                      dtok[:, 0, dk:dk + 16].unsqueeze(2).to_broadcast([128, 16, 64]), ALU.mult),
                     reads=[xtok, dtok], writes=[xdt])
                S.op("pool", lambda e: e.tensor_tensor(xdtw[:, :, :], xdt[:, :, :],
                                                       dte[:, :].unsqueeze(2).to_broadcast([128, 16, 64]), ALU.mult),
                     reads=[xdt, dte], writes=[xdtw])
                for g4 in range(4):
                    for kk in range(4):
                        k = g4 * 4 + kk
                        S.op("pe", lambda e: e.matmul(p_d[:, kk * 128:(kk + 1) * 128],
                                                      dtok[:, 1, dk + k:dk + k + 1].to_broadcast([128, 128]), tri,
                                                      start=True, stop=True), reads=[dtok, cst], writes=[p_d])
                    S.op("act", lambda e: e.activation(Ecum[:, :, :].rearrange("p a b -> p (a b)"), p_d[:, :], AF.Exp),
                         reads=[p_d], writes=[Ecum])
                    S.op("pool", lambda e: e.tensor_tensor(Ck[:, :, :], Ecum[:, :, :],
                                                           CT[:, :].unsqueeze(1).to_broadcast([128, 4, 128]), ALU.mult),
                         reads=[Ecum, CT], writes=[Ck])
                    for kk in range(4):
                        k = g4 * 4 + kk
                        pm = p_m[:, kk * 128:(kk + 1) * 128]
                        S.op("pe", lambda e: e.matmul(pm, dtok[:, 1, dk + k:dk + k + 1].to_broadcast([128, 128]), tri,
                                                      start=True, stop=False), reads=[dtok, cst], writes=[p_m])
                        S.op("pe", lambda e: e.matmul(pm, cst[:, 0, :], negm, start=False, stop=True),
                             reads=[cst], writes=[p_m])
                        E = Ek[k % 2]
                        M = Mk[k % 2]
                        S.op("act", lambda e: e.activation(E[:, :], pm, AF.Exp, bias=ncum[:, k:k + 1]),
                             reads=[p_m, ncum], writes=[E])
                        S.op("dve", lambda e: e.tensor_tensor(M[:, :], E[:, :], cbT[:, :], ALU.mult),
                             reads=[E, cbT], writes=[M])
                        py = p_y[k // 8]
                        b4 = (k // 2) % 4
                        po = (k % 2) * 64
                        yo = py[po:po + 64, b4 * 128:(b4 + 1) * 128]
                        S.op("pe", lambda e: e.matmul(yo, xdt[:, k, :], M[:, :], start=True, stop=False),
                             reads=[xdt, M], writes=[py])
                        S.op("pe", lambda e: e.matmul(yo, Hb[:, k, :], Ck[:, kk, :], start=False, stop=True),
                             reads=[Hb, Ck], writes=[py])
                S.op("dve", lambda e: e.tensor_tensor(Ht[:, :, :], H[:, :, :],
                                                      cd[:, :].unsqueeze(2).to_broadcast([128, 16, 64]), ALU.mult),
                     reads=[H, cd], writes=[Ht])
                for hh in range(2):
                    S.op("pe", lambda e: e.matmul(p_s[:, :], Btok[:, :],
                                                  xdtw[:, 8 * hh:8 * hh + 8, :].rearrange("p k q -> p (k q)"),
                                                  start=True, stop=True), reads=[Btok, xdtw], writes=[p_s])
                    S.op("dve", lambda e: e.tensor_tensor(H[:, 8 * hh:8 * hh + 8, :].rearrange("p k q -> p (k q)"),
                                                          Ht[:, 8 * hh:8 * hh + 8, :].rearrange("p k q -> p (k q)"),
                                                          p_s[:, :], ALU.add), reads=[Ht, p_s], writes=[H])
                S.op("act", lambda e: e.activation(Hb[:, :, :], H[:, :, :], AF.Copy), reads=[H], writes=[Hb])
                if d == 0:
                    for hh in range(2):
                        S.op("act", lambda e: e.activation(ych[:, 4 * hh:4 * hh + 4, :].rearrange("p a b -> p (a b)"),
                                                           p_y[hh][:, :], AF.Copy), reads=[p_y[hh]], writes=[ych])
                    S.dma("sp", yf.h.ap().rearrange("(b p) t -> p b t", p=128)[:, :, tsl], ych[:, :, :], reads=[ych], writes=[yf])
                else:
                    S.op("act", lambda e: e.activation(zc[:, :, :], zc[:, :, :], AF.Silu), reads=[zc], writes=[zc])
                    for b in range(8):
                        S.op("dve", lambda e: e.scalar_tensor_tensor(ych[:, b, :], xT[:, b, :], hp[:, b, 0:1], yfs[:, b, :],
                                                                     ALU.mult, ALU.add), reads=[xT, hp, yfs], writes=[ych])
                    for hh in range(2):
                        sl = ych[:, 4 * hh:4 * hh + 4, :].rearrange("p a b -> p (a b)")
                        S.op("dve", lambda e: e.tensor_tensor(sl, sl, p_y[hh][:, :], ALU.add), reads=[ych, p_y[hh]], writes=[ych])
                    S.op("pool", lambda e: e.tensor_tensor(ych[:, :, :], ych[:, :, :], zc[:, :, :], ALU.mult),
                         reads=[ych, zc], writes=[ych])
                    for b in range(8):
                        S.op("act", lambda e: e.activation(ysq[:, :], ych[:, b, :], AF.Square), reads=[ych], writes=[ysq])
                        S.op("pe", lambda e: e.matmul(p_misc[:, 256:384], cst[:, 5, :], ysq[:, :], start=(b == 0), stop=(b == 7)),
                             reads=[cst, ysq], writes=[p_misc])
                    S.op("dve", lambda e: e.tensor_scalar(rs[:, :], p_misc[:, 256:384], 1.0 / 1024, EPS, ALU.mult, ALU.add),
                         reads=[p_misc], writes=[rs])
                    S.op("act", lambda e: e.activation(rs[:, :], rs[:, :], AF.Sqrt), reads=[rs], writes=[rs])
                    S.op("dve", lambda e: e.reciprocal(rs[:, :], rs[:, :]), reads=[rs], writes=[rs])
                    for b in range(8):
                        S.op("dve", lambda e: e.scalar_tensor_tensor(yob[:, b, :], ych[:, b, :], hp[:, b, 1:2], rs[:, :],
                                                                     ALU.mult, ALU.mult), reads=[ych, hp, rs], writes=[yob])
                    yv = yout.h.ap().rearrange("r (b p) t -> r p b t", p=128)
                    for (r, o, ts, ln) in seq_ranges(t0, 128, TL):
                        S.dma("sp", yv[r][:, :, o:o + ln], yob[:, :, ts - t0:ts - t0 + ln], reads=[yob], writes=[yout])
        S.barrier()
    return nc


def pvec(v):
    v = np.asarray(v, np.float32)
    return np.ascontiguousarray(v.reshape(-1, 128).T)


def ssd_host(inp, j, g):
    w_in = inp["ssd_w_in"][j]
    DI, GN = 8192, 1024
    cols = np.r_[g * 1024:(g + 1) * 1024, DI + g * 1024:DI + (g + 1) * 1024,
                 2 * DI + g * 128:2 * DI + (g + 1) * 128, 2 * DI + GN + g * 128:2 * DI + GN + (g + 1) * 128,
                 2 * DI + 2 * GN + g * 16:2 * DI + 2 * GN + (g + 1) * 16,
                 2 * DI + 2 * GN + 128 + g * 16:2 * DI + 2 * GN + 128 + (g + 1) * 16]
    w = np.zeros((D, SSD_C), np.float32)
    w[:, :len(cols)] = w_in[:, cols]
    cch = np.r_[g * 1024:(g + 1) * 1024, DI + g * 128:DI + (g + 1) * 128, DI + GN + g * 128:DI + GN + (g + 1) * 128]
    cw = inp["ssd_conv_w"][j][:, cch]
    cb = inp["ssd_conv_b"][j][cch]
    cp = np.concatenate([cw, cb[None]], 0)
    convp = np.ascontiguousarray(cp.T.reshape(10, 128, 6).transpose(1, 0, 2))
    dsk = np.repeat(inp["ssd_d"][j][g * 16:(g + 1) * 16], 64)
    nw = inp["ssd_norm"][j][g * 1024:(g + 1) * 1024]
    hp = np.ascontiguousarray(np.stack([pvec(dsk), pvec(nw)], -1))
    dtb = inp["ssd_dt_bias"][j].reshape(2, 8, 16)[:, g].reshape(32)
    alog = inp["ssd_a_log"][j].reshape(2, 8, 16)[:, g].reshape(32)
    dtp = np.ascontiguousarray(np.stack([dtb, alog], -1).astype(np.float32))
    return {"w": w, "convp": convp, "hp": hp, "dtp": dtp, "consts": ssd_consts()}


def gather_tokens(outs, key):
    return [np.ascontiguousarray(np.concatenate([outs[c][key][r] for c in range(NR)], 0)) for r in range(NR)]


ATT_SCALE = 128 ** -0.5


def rope_tables(nlat):
    f32 = np.float32
    t = np.arange(nlat)
    row = (t // 64).astype(f32)
    col = (t % 64).astype(f32)
    n_freq = 32
    inv_freq = (f32(10000.0) ** (-np.arange(n_freq, dtype=f32) / f32(n_freq))).astype(f32)
    ang_r = (row[:, None] * inv_freq[None, :]).astype(f32)
    ang_c = (col[:, None] * inv_freq[None, :]).astype(f32)
    cos = np.zeros((128, nlat), f32)
    sins = np.zeros((128, nlat), f32)
    for d in range(128):
        ang = ang_r if d < 64 else ang_c
        f = d % 32
        cos[d] = np.cos(ang[:, f])
        s = np.sin(ang[:, f])
        sins[d] = -s if (d % 64) < 32 else s
    perm = np.zeros((128, 128), f32)
    for d in range(128):
        partner = d + 32 if (d % 64) < 32 else d - 32
        perm[partner, d] = 1.0
    return cos, sins, perm


def build_attA(TL, lambda_init, want_ctx=True, dbg=False):
    TT, NT = dims(TL)
    NL = NR * TL
    nc, S = new_prog()
    hfull = din(nc, "hfull", [NR, D, TT], BF16)
    wqk = din(nc, "wqk", [D, 1024])
    wv = din(nc, "wv", [D, 512])
    gains = din(nc, "gains", [128, 2])
    grow = din(nc, "grow", [128, 2, 128])
    lamv = din(nc, "lamv", [128, 4, 128])
    subg = din(nc, "subg", [128, 256])
    cosd = din(nc, "cosd", [128, NL])
    sind = din(nc, "sind", [128, NL])
    permd = din(nc, "perm", [128, 128])
    yout = dout(nc, "yout", [NR, 512, TT], BF16)
    mk = dout if dbg else dscr
    pre = mk(nc, "pre", [1024, NT])
    qk = mk(nc, "qk", [1024, NT], BF16)
    Vs = mk(nc, "Vs", [NT, 512], BF16)
    dbgo = dout(nc, "dbgo", [128, 8]) if dbg else None

    proj_all(S, hfull, wqk, 1024, pre, TL)

    with contextlib.ExitStack() as st:
        wvs = S.sb("wvs", [128, KD, 512], BF16, st)
        S.dma("pool", wvs[:, :, :], wv.h.ap().rearrange("(k p) c -> p k c", p=128), reads=[wv], writes=[wvs])
        ht = [S.sb(f"vh{i}", [128, KD, 512], BF16, st) for i in range(2)]
        vo = [S.sb(f"vo{i}", [128, 512], BF16, st) for i in range(2)]
        pv = [S.ps(f"pv{i}", stack=st) for i in range(2)]
        it = 0
        for ti, (t0, n) in enumerate(seq_tiles(TL)):
            h = ht[ti % 2]
            for (r, o, ts, ln) in seq_ranges(t0, n, TL):
                S.dma("sp", h[:, :, ts - t0:ts - t0 + ln],
                      hfull.h.ap()[r].rearrange("(k p) t -> p k t", p=128)[:, :, o:o + ln], reads=[hfull], writes=[h])
            for sub in range(n // 128):
                p = pv[it % 2]
                o_ = vo[it % 2]
                it += 1
                for k in range(KD):
                    S.op("pe", lambda e: e.matmul(p[:, :], h[:, k, sub * 128:(sub + 1) * 128], wvs[:, k, :],
                                                  start=(k == 0), stop=(k == KD - 1)), reads=[h, wvs], writes=[p])
                S.op("act", lambda e: e.activation(o_[:, :], p[:, :], AF.Copy), reads=[p], writes=[o_])
                S.dma("sp", Vs.h.ap()[t0 + sub * 128:t0 + (sub + 1) * 128, :], o_[:, :], reads=[o_], writes=[Vs])
        S.barrier()

    with contextlib.ExitStack() as st:
        gn = S.sb("gn", [128, 2], F32, st)
        pm = S.sb("pm", [128, 128], F32, st)
        ones = S.sb("ones", [128, 128], F32, st)
        S.dma("sp", gn[:, :], gains[:, :], reads=[gains], writes=[gn])
        S.dma("sp", pm[:, :], permd[:, :], reads=[permd], writes=[pm])
        S.op("pool", lambda e: e.memset(ones[:, :], 1.0), writes=[ones])
        cs = [S.sb(f"cs{i}", [128, 512], F32, st) for i in range(2)]
        sn = [S.sb(f"sn{i}", [128, 512], F32, st) for i in range(2)]
        X = [S.sb(f"X{i}", [128, 512], F32, st) for i in range(2)]
        sq = [S.sb(f"sq{i}", [128, 512], F32, st) for i in range(2)]
        rs = [S.sb(f"rs{i}", [128, 512], F32, st) for i in range(2)]
        Xn = [S.sb(f"Xn{i}", [128, 512], F32, st) for i in range(2)]
        R1 = [S.sb(f"R1{i}", [128, 512], F32, st) for i in range(2)]
        ob = [S.sb(f"ob{i}", [128, 512], BF16, st) for i in range(2)]
        pss = [S.ps(f"pss{i}", stack=st) for i in range(2)]
        ppx = [S.ps(f"ppx{i}", stack=st) for i in range(2)]
        it = 0
        for ti, (t0, n) in enumerate(seq_tiles(TL)):
            lat = t0 >= CTXN
            c_, s_ = cs[ti % 2], sn[ti % 2]
            if lat:
                S.dma("sp", c_[:, 0:n], cosd.h.ap()[:, t0 - CTXN:t0 - CTXN + n], reads=[cosd], writes=[c_])
                S.dma("sp", s_[:, 0:n], sind.h.ap()[:, t0 - CTXN:t0 - CTXN + n], reads=[sind], writes=[s_])
            for blk in range(8):
                i2 = it % 2
                it += 1
                x_, q_, r_, xn_, r1_, o_ = X[i2], sq[i2], rs[i2], Xn[i2], R1[i2], ob[i2]
                S.dma("sp", x_[:, 0:n], pre.h.ap()[blk * 128:(blk + 1) * 128, t0:t0 + n], reads=[pre], writes=[x_])
                S.op("act", lambda e: e.activation(q_[:, 0:n], x_[:, 0:n], AF.Square), reads=[x_], writes=[q_])
                S.op("pe", lambda e: e.matmul(pss[i2][:, 0:n], ones[:, :], q_[:, 0:n], start=True, stop=True),
                     reads=[ones, q_], writes=[pss[i2]])
                S.op("dve", lambda e: e.tensor_scalar(r_[:, 0:n], pss[i2][:, 0:n], 1.0 / 128, EPS, ALU.mult, ALU.add),
                     reads=[pss[i2]], writes=[r_])
                S.op("act", lambda e: e.activation(r_[:, 0:n], r_[:, 0:n], AF.Sqrt), reads=[r_], writes=[r_])
                S.op("dve", lambda e: e.reciprocal(r_[:, 0:n], r_[:, 0:n]), reads=[r_], writes=[r_])
                g_ = gn[:, 0:1] if blk < 4 else gn[:, 1:2]
                if lat:
                    S.op("dve", lambda e: e.scalar_tensor_tensor(xn_[:, 0:n], x_[:, 0:n], g_, r_[:, 0:n], ALU.mult, ALU.mult),
                         reads=[x_, gn, r_], writes=[xn_])
                    S.op("pe", lambda e: e.matmul(ppx[i2][:, 0:n], pm[:, :], xn_[:, 0:n], start=True, stop=True),
                         reads=[pm, xn_], writes=[ppx[i2]])
                    S.op("pool", lambda e: e.tensor_tensor(r1_[:, 0:n], xn_[:, 0:n], c_[:, 0:n], ALU.mult),
                         reads=[xn_, c_], writes=[r1_])
                    S.op("dve", lambda e: e.tensor_tensor(xn_[:, 0:n], ppx[i2][:, 0:n], s_[:, 0:n], ALU.mult),
                         reads=[ppx[i2], s_], writes=[xn_])
                    S.op("dve", lambda e: e.tensor_tensor(o_[:, 0:n], r1_[:, 0:n], xn_[:, 0:n], ALU.add),
                         reads=[r1_, xn_], writes=[o_])
                else:
                    S.op("dve", lambda e: e.scalar_tensor_tensor(o_[:, 0:n], x_[:, 0:n], g_, r_[:, 0:n], ALU.mult, ALU.mult),
                         reads=[x_, gn, r_], writes=[o_])
                S.dma("sp", qk.h.ap()[blk * 128:(blk + 1) * 128, t0:t0 + n], o_[:, 0:n], reads=[o_], writes=[qk])
        S.barrier()

    NKT = NT // 128
    with contextlib.ExitStack() as st:
        identb = S.sb("identb", [128, 128], BF16, st)
        identf = S.sb("identf", [128, 128], F32, st)
        S.op("pool", lambda e: e.memset(identf[:, :], 0.0), writes=[identf])
        grw = S.sb("grw", [128, 2, 128], F32, st)
        lv = S.sb("lv", [128, 4, 128], F32, st)
        sg = S.sb("sg", [128, 256], F32, st)
        S.dma("sp", grw[:, :, :], grow[:, :, :], reads=[grow], writes=[grw])
        S.dma("sp", lv[:, :, :], lamv[:, :, :], reads=[lamv], writes=[lv])
        S.dma("sp", sg[:, :], subg[:, :], reads=[subg], writes=[sg])
        pmf = S.sb("pmf", [128, 128], F32, st)
        S.dma("sp", pmf[:, :], permd[:, :], reads=[permd], writes=[pmf])
        pmisc = S.ps("pmisc", stack=st)
        S.op("pe", lambda e: e.matmul(pmisc[:, 0:128], pmf[:, :], pmf[:, :], start=True, stop=True), reads=[pmf], writes=[pmisc])
        S.op("dve", lambda e: e.tensor_copy(identb[:, :], pmisc[:, 0:128]), reads=[pmisc], writes=[identb])
        sm = S.sb("sm", [128, 8], F32, st)
        tmpv = S.sb("tmpv", [128, 128], F32, st)
        for j in range(2):
            S.op("dve", lambda e: e.tensor_reduce(sm[:, j:j + 1], grw[:, j, :], AX.X, ALU.max, apply_absolute_value=True),
                 reads=[grw], writes=[sm])
        S.op("dve", lambda e: e.scalar_tensor_tensor(sm[:, 2:3], sm[:, 0:1], -math.sqrt(128.0), sm[:, 1:2], ALU.mult, ALU.mult),
             reads=[sm], writes=[sm])
        for j in range(2):
            S.op("dve", lambda e: e.tensor_tensor(tmpv[:, :], lv[:, 2 * j, :], lv[:, 2 * j + 1, :], ALU.mult), reads=[lv], writes=[tmpv])
            S.op("dve", lambda e: e.tensor_reduce(sm[:, 3 + j:4 + j], tmpv[:, :], AX.X, ALU.add), reads=[tmpv], writes=[sm])
        S.op("act", lambda e: e.activation(sm[:, 3:5], sm[:, 3:5], AF.Exp), reads=[sm], writes=[sm])
        S.op("dve", lambda e: e.scalar_tensor_tensor(sm[:, 5:6], sm[:, 4:5], -float(lambda_init), sm[:, 3:4], ALU.add, ALU.subtract),
             reads=[sm], writes=[sm])
        S.op("dve", lambda e: e.tensor_scalar(sg[:, :], sg[:, :], 1.0 - float(lambda_init), None, ALU.mult), reads=[sg], writes=[sg])
        negB = sm[:, 2:3]
        neglam = sm[:, 5:6]
        if dbg:
            S.dma("sp", dbgo[:, :], sm[:, :], reads=[sm], writes=[dbgo])

        kT = S.sb("kT", [128, 2, NT], BF16, st)
        Vh = S.sb("Vh", [128, NKT, 257], BF16, st)
        S.op("pool", lambda e: e.memset(Vh[:, :, 256:257], 1.0), writes=[Vh])
        qT = [S.sb(f"qT{i}", [128, 512], BF16, st) for i in range(2)]
        PT = [S.sb(f"PT{i}", [128, 512], BF16, st) for i in range(3)]
        O0 = S.sb("O0", [128, 4, 256], F32, st)
        O1 = S.sb("O1", [128, 256], F32, st)
        rz = S.sb("rz", [128, 8], F32, st)
        junk = S.sb("junk", [128, 256], F32, st)
        onb = S.sb("onb", [128, 256], BF16, st)
        obT = [S.sb(f"obT{i}", [128, 2, 512], BF16, st) for i in range(2)]
        ps_s = [S.ps(f"ps_s{i}", stack=st) for i in range(2)]
        acc = [S.ps(f"acc{i}", stack=st) for i in range(4)]
        ptr = S.ps("ptr", [128, 1024], BF16, st)
        qit = 0
        pit = 0
        yv = yout.h.ap().rearrange("r (hb p) t -> r p hb t", p=128)
        for h in range(2):
            for j in range(2):
                S.dma("sp", kT[:, j, :], qk.h.ap()[512 + (2 * h + j) * 128:512 + (2 * h + j + 1) * 128, :], reads=[qk], writes=[kT])
            S.dma("sp", Vh[:, :, 0:256], Vs.h.ap().rearrange("(kt p) e -> p kt e", p=128)[:, :, h * 256:(h + 1) * 256],
                  reads=[Vs], writes=[Vh])
            qtiles = [(t0, n) for (t0, n) in seq_tiles(TL) if (t0 >= CTXN or want_ctx)]
            for (t0, n) in qtiles:
                lat = t0 >= CTXN
                kts = list(range(NKT)) if lat else list(range(CTXN // 128))
                nqg = n // 128
                obt = obT[qit % 2]
                for j in range(2):
                    q_ = qT[qit % 2]
                    qit += 1
                    S.dma("sp", q_[:, 0:n], qk.h.ap()[(2 * h + j) * 128:(2 * h + j + 1) * 128, t0:t0 + n], reads=[qk], writes=[q_])
                    for ki, kt in enumerate(kts):
                        ps_ = ps_s[pit % 2]
                        pt_ = PT[pit % 3]
                        pit += 1
                        S.op("pe", lambda e: e.matmul(ps_[:, 0:n], kT[:, j, kt * 128:(kt + 1) * 128], q_[:, 0:n], start=True, stop=True),
                             reads=[kT, q_], writes=[ps_])
                        S.op("act", lambda e: e.activation(pt_[:, 0:n], ps_[:, 0:n], AF.Exp, bias=negB, scale=ATT_SCALE),
                             reads=[ps_, sm], writes=[pt_])
                        for qg in range(nqg):
                            S.op("pe", lambda e: e.matmul(acc[qg][:, 0:257], pt_[:, qg * 128:(qg + 1) * 128], Vh[:, kt, :],
                                                          start=(ki == 0), stop=(ki == len(kts) - 1)),
                                 reads=[pt_, Vh], writes=[acc[qg]])
                    for qg in range(nqg):
                        S.op("dve", lambda e: e.reciprocal(rz[:, qg:qg + 1], acc[qg][:, 256:257]), reads=[acc[qg]], writes=[rz])
                        if j == 0:
                            S.op("dve", lambda e: e.tensor_scalar(O0[:, qg, :], acc[qg][:, 0:256], rz[:, qg:qg + 1], None, ALU.mult),
                                 reads=[acc[qg], rz], writes=[O0])
                        else:
                            S.op("dve", lambda e: e.tensor_scalar(O1[:, :], acc[qg][:, 0:256], rz[:, qg:qg + 1], None, ALU.mult),
                                 reads=[acc[qg], rz], writes=[O1])
                            S.op("dve", lambda e: e.scalar_tensor_tensor(O1[:, :], O1[:, :], neglam, O0[:, qg, :], ALU.mult, ALU.add),
                                 reads=[O1, sm, O0], writes=[O1])
                            S.op("act", lambda e: e.activation(junk[:, :], O1[:, :], AF.Square, accum_out=rz[:, 4 + qg:5 + qg]),
                                 reads=[O1], writes=[junk, rz])
                            S.op("dve", lambda e: e.tensor_scalar(rz[:, 4 + qg:5 + qg], rz[:, 4 + qg:5 + qg], 1.0 / 256, EPS, ALU.mult, ALU.add),
                                 reads=[rz], writes=[rz])
                            S.op("act", lambda e: e.activation(rz[:, 4 + qg:5 + qg], rz[:, 4 + qg:5 + qg], AF.Sqrt), reads=[rz], writes=[rz])
                            S.op("dve", lambda e: e.reciprocal(rz[:, 4 + qg:5 + qg], rz[:, 4 + qg:5 + qg]), reads=[rz], writes=[rz])
                            S.op("dve", lambda e: e.scalar_tensor_tensor(onb[:, :], O1[:, :], rz[:, 4 + qg:5 + qg], sg[:, :], ALU.mult, ALU.mult),
                                 reads=[O1, rz, sg], writes=[onb])
                            for eb in range(2):
                                S.op("pe", lambda e: e.transpose(ptr[:, eb * 512 + qg * 128:eb * 512 + (qg + 1) * 128],
                                                                 onb[:, eb * 128:(eb + 1) * 128], identb[:, :]),
                                     reads=[onb, identb], writes=[ptr])
                S.op("act", lambda e: e.activation(obt[:, :, 0:n], ptr[:, :].rearrange("p (a b) -> p a b", a=2)[:, :, 0:n], AF.Copy),
                     reads=[ptr], writes=[obt])
                for (r, o, ts, ln) in seq_ranges(t0, n, TL):
                    S.dma("sp", yv[r][:, 2 * h:2 * h + 2, o:o + ln], obt[:, :, ts - t0:ts - t0 + ln], reads=[obt], writes=[yout])
        S.barrier()
    return nc


def att_host(inp, j, c, TL, lambda_init):
    cols = np.r_[512 * c:512 * (c + 1)]
    wqk = np.ascontiguousarray(np.concatenate([inp["da_w_q"][j][:, cols], inp["da_w_k"][j][:, cols]], 1))
    wv = np.ascontiguousarray(inp["da_w_v"][j][:, cols])
    gq, gk = inp["da_q_norm"][j], inp["da_k_norm"][j]
    gains = np.ascontiguousarray(np.stack([gq, gk], -1).astype(np.float32))
    grow = np.ascontiguousarray(np.broadcast_to(np.stack([gq, gk], 0)[None], (128, 2, 128)).astype(np.float32))
    lamv = np.stack([inp["da_lam_q1"][j], inp["da_lam_k1"][j], inp["da_lam_q2"][j], inp["da_lam_k2"][j]], 0)
    lamv = np.ascontiguousarray(np.broadcast_to(lamv[None], (128, 4, 128)).astype(np.float32))
    subg = np.ascontiguousarray(np.broadcast_to(inp["da_sub_norm"][j][None], (128, 256)).astype(np.float32))
    cos, sins, perm = rope_tables(NR * TL)
    return {"wqk": wqk, "wv": wv, "gains": gains, "grow": grow, "lamv": lamv, "subg": subg,
            "cosd": cos, "sind": sins, "perm": perm}


S5W = 128
TWO_PI = 2.0 * math.pi


I32 = mybir.dt.int32
CW1 = 6.28125
CW2 = TWO_PI - 6.28125


def emit_sin(S, outT, out_ap, y, yap, ki, kiap, kf, kfap):
    S.op("dve", lambda e: e.tensor_scalar(kfap, yap, 1.0 / TWO_PI, None, ALU.mult), reads=[y], writes=[kf])
    S.op("dve", lambda e: e.tensor_copy(kiap, kfap), reads=[kf], writes=[ki])
    S.op("dve", lambda e: e.tensor_copy(kfap, kiap), reads=[ki], writes=[kf])
    S.op("dve", lambda e: e.scalar_tensor_tensor(yap, kfap, -CW1, yap, ALU.mult, ALU.add), reads=[kf, y], writes=[y])
    S.op("dve", lambda e: e.scalar_tensor_tensor(yap, kfap, -CW2, yap, ALU.mult, ALU.add), reads=[kf, y], writes=[y])
    S.op("dve", lambda e: e.tensor_scalar(kfap, yap, math.pi, None, ALU.is_gt), reads=[y], writes=[kf])
    S.op("dve", lambda e: e.scalar_tensor_tensor(yap, kfap, -TWO_PI, yap, ALU.mult, ALU.add), reads=[kf, y], writes=[y])
    S.op("dve", lambda e: e.tensor_scalar(kfap, yap, -math.pi, None, ALU.is_lt), reads=[y], writes=[kf])
    S.op("dve", lambda e: e.scalar_tensor_tensor(yap, kfap, TWO_PI, yap, ALU.mult, ALU.add), reads=[kf, y], writes=[y])
    S.op("dve", lambda e: e.tensor_scalar(yap, yap, -math.pi, math.pi, ALU.max, ALU.min), reads=[y], writes=[y])
    S.op("act", lambda e: e.activation(out_ap, yap, AF.Sin), reads=[y], writes=[outT])


def build_s5A(TL, dbg=False):
    TT, NT = dims(TL)
    W = S5W
    NWIN = NT // W
    nc, S = new_prog()
    hs5 = din(nc, "hs5", [NR, 512, TT], BF16)
    lamd = din(nc, "lam", [128, 2, 2, 16])
    stpd = din(nc, "stp", [128, 2, 16])
    bblkd = din(nc, "bblk", [128, 2, 16, 128])
    cblkd = din(nc, "cblk", [128, 2, 2, 16, 128])
    dskd = din(nc, "dsk", [128, 4])
    jrowd = din(nc, "jrow", [128, W])
    yout = dout(nc, "yout", [NR, 512, TT], BF16)
    yfd = dscr(nc, "yfd", [512, NT])
    dbgo = dout(nc, "dbgo", [128, 8, 16]) if dbg else None

    u = S.sb("u", [128, 4, NT], BF16)
    lam = S.sb("lam", [128, 2, 2, 16], F32)
    stp = S.sb("stp", [128, 2, 16], F32)
    bblkb = S.sb("bblkb", [128, 2, 16, 128], BF16)
    cblkb = S.sb("cblkb", [128, 2, 2, 16, 128], BF16)
    dsk = S.sb("dsk", [128, 4], F32)
    jrow = S.sb("jrow", [128, W], F32)
    for (dst, src) in ((lam, lamd), (stp, stpd), (dsk, dskd), (jrow, jrowd)):
        S.dma("sp", dst[tuple(slice(None) for _ in dst.h.shape)], src.h.ap(), reads=[src], writes=[dst])
    S.dma("pool", bblkb[:, :, :, :], bblkd.h.ap(), reads=[bblkd], writes=[bblkb])
    for d in range(2):
        S.dma("pool", cblkb[:, d, :, :, :], cblkd.h.ap()[:, d], reads=[cblkd], writes=[cblkb])
    uv = hs5.h.ap().rearrange("r (f p) t -> r p f t", p=128)
    for r in range(NR):
        S.dma("sp", u[:, :, TCTX * r:TCTX * (r + 1)], uv[r][:, :, 0:TCTX], reads=[hs5], writes=[u])
        S.dma("sp", u[:, :, CTXN + TL * r:CTXN + TL * (r + 1)], uv[r][:, :, TCTX:TT], reads=[hs5], writes=[u])
    for d in range(2):
        S.op("dve", lambda e: e.tensor_scalar(cblkb[:, d, 1, :, :], cblkb[:, d, 1, :, :], -1.0, None, ALU.mult),
             reads=[cblkb], writes=[cblkb])

    pki = S.sb("pki", [128, 16], I32)
    pp = S.sb("pp", [128, 2, 12, 16], F32)
    for d in range(2):
        P_ = lambda i: pp[:, d, i, :]
        lr, li = lam[:, 0, d, :], lam[:, 1, d, :]
        S.op("act", lambda e: e.activation(P_(0), stp[:, d, :], AF.Exp), reads=[stp], writes=[pp])
        S.op("dve", lambda e: e.tensor_tensor(P_(10), lr, P_(0), ALU.mult), reads=[lam, pp], writes=[pp])
        S.op("act", lambda e: e.activation(P_(1), P_(10), AF.Exp), reads=[pp], writes=[pp])
        S.op("dve", lambda e: e.tensor_tensor(P_(2), li, P_(0), ALU.mult), reads=[lam, pp], writes=[pp])
        S.op("dve", lambda e: e.tensor_copy(P_(10), P_(2)), reads=[pp], writes=[pp])
        emit_sin(S, pp, P_(4), pp, P_(10), pki, pki[:, :], pp, P_(11))
        S.op("dve", lambda e: e.tensor_scalar(P_(10), P_(2), 0.5 * math.pi, None, ALU.add), reads=[pp], writes=[pp])
        emit_sin(S, pp, P_(3), pp, P_(10), pki, pki[:, :], pp, P_(11))
        S.op("dve", lambda e: e.tensor_tensor(P_(5), P_(1), P_(3), ALU.mult), reads=[pp], writes=[pp])
        S.op("dve", lambda e: e.tensor_scalar(P_(5), P_(5), -1.0, None, ALU.add), reads=[pp], writes=[pp])
        S.op("dve", lambda e: e.tensor_tensor(P_(6), P_(1), P_(4), ALU.mult), reads=[pp], writes=[pp])
        S.op("dve", lambda e: e.tensor_tensor(P_(7), lr, lr, ALU.mult), reads=[lam], writes=[pp])
        S.op("dve", lambda e: e.tensor_tensor(P_(10), li, li, ALU.mult), reads=[lam], writes=[pp])
        S.op("dve", lambda e: e.tensor_tensor(P_(7), P_(7), P_(10), ALU.add), reads=[pp], writes=[pp])
        S.op("dve", lambda e: e.reciprocal(P_(7), P_(7)), reads=[pp], writes=[pp])
        S.op("dve", lambda e: e.tensor_tensor(P_(8), P_(5), lr, ALU.mult), reads=[pp, lam], writes=[pp])
        S.op("dve", lambda e: e.tensor_tensor(P_(10), P_(6), li, ALU.mult), reads=[pp, lam], writes=[pp])
        S.op("dve", lambda e: e.tensor_tensor(P_(8), P_(8), P_(10), ALU.add), reads=[pp], writes=[pp])
        S.op("dve", lambda e: e.tensor_tensor(P_(8), P_(8), P_(7), ALU.mult), reads=[pp], writes=[pp])
        S.op("dve", lambda e: e.tensor_tensor(P_(9), P_(6), lr, ALU.mult), reads=[pp, lam], writes=[pp])
        S.op("dve", lambda e: e.tensor_tensor(P_(10), P_(5), li, ALU.mult), reads=[pp, lam], writes=[pp])
        S.op("dve", lambda e: e.tensor_tensor(P_(9), P_(9), P_(10), ALU.subtract), reads=[pp], writes=[pp])
        S.op("dve", lambda e: e.tensor_tensor(P_(9), P_(9), P_(7), ALU.mult), reads=[pp], writes=[pp])
    if dbg:
        S.dma("sp", dbgo[:, :, :], pp[:, 0, 0:8, :], reads=[pp], writes=[dbgo])

    cosT = S.sb("cosT", [128, 16, W], F32)
    sinT = S.sb("sinT", [128, 16, W], F32)
    TrT = S.sb("TrT", [128, 16, W], F32)
    TiT = S.sb("TiT", [128, 16, W], F32)
    rB = S.sb("rB", [128, 16, W], F32)
    ang = S.sb("ang", [128, W], F32)
    angi = S.sb("angi", [128, W], I32)
    car = S.sb("car", [128, 2, 16], F32)
    t1 = [S.sb(f"t1{i}", [128, W], F32) for i in range(1)]
    G4 = 4 * W
    ta = [S.sb(f"ta{i}", [128, G4], F32) for i in range(2)]
    tb = [S.sb(f"tb{i}", [128, G4], F32) for i in range(2)]
    tc_ = [S.sb(f"tc{i}", [128, G4], F32) for i in range(2)]
    td = [S.sb(f"td{i}", [128, G4], F32) for i in range(2)]
    Xr = [S.sb(f"Xr{i}", [128, 4, W], F32) for i in range(2)]
    Xi = [S.sb(f"Xi{i}", [128, 4, W], F32) for i in range(2)]
    sr = [S.sb(f"sr{i}", [128, 4, W], F32) for i in range(2)]
    si = [S.sb(f"si{i}", [128, 4, W], F32) for i in range(2)]
    srb = [S.sb(f"srb{i}", [128, 4, W], BF16) for i in range(2)]
    sib = [S.sb(f"sib{i}", [128, 4, W], BF16) for i in range(2)]
    yfs = S.sb("yfs", [128, 4, W], F32)
    ych = S.sb("ych", [128, 4, W], F32)
    g1 = S.sb("g1", [128, 4, W], F32)
    yob = S.sb("yob", [128, 4, W], BF16)
    pre_ = [S.ps(f"pre{i}") for i in range(2)]
    pim_ = [S.ps(f"pim{i}") for i in range(2)]
    py = [S.ps(f"py{i}") for i in range(2)]
    yfv = yfd.h.ap().rearrange("(f p) t -> p f t", p=128)
    yv = yout.h.ap().rearrange("r (f p) t -> r p f t", p=128)
    it = 0
    for d in range(2):
        for sb in range(16):
            th = pp[:, d, 2, sb:sb + 1]
            S.op("dve", lambda e: e.tensor_scalar(ang[:, :], jrow[:, :], th, None, ALU.mult), reads=[jrow, pp], writes=[ang])
            emit_sin(S, sinT, sinT[:, sb, :], ang, ang[:, :], angi, angi[:, :], t1[0], t1[0][:, :])
            S.op("dve", lambda e: e.tensor_scalar(ang[:, :], jrow[:, :], th, 0.5 * math.pi, ALU.mult, ALU.add), reads=[jrow, pp], writes=[ang])
            emit_sin(S, cosT, cosT[:, sb, :], ang, ang[:, :], angi, angi[:, :], t1[0], t1[0][:, :])
            kr, ki = pp[:, d, 8, sb:sb + 1], pp[:, d, 9, sb:sb + 1]
            S.op("dve", lambda e: e.tensor_scalar(TrT[:, sb, :], sinT[:, sb, :], ki, None, ALU.mult), reads=[sinT, pp], writes=[TrT])
            S.op("dve", lambda e: e.scalar_tensor_tensor(TrT[:, sb, :], cosT[:, sb, :], kr, TrT[:, sb, :], ALU.mult, ALU.add),
                 reads=[cosT, pp, TrT], writes=[TrT])
            S.op("dve", lambda e: e.tensor_scalar(TiT[:, sb, :], sinT[:, sb, :], kr, -1.0, ALU.mult, ALU.mult), reads=[sinT, pp], writes=[TiT])
            S.op("dve", lambda e: e.scalar_tensor_tensor(TiT[:, sb, :], cosT[:, sb, :], ki, TiT[:, sb, :], ALU.mult, ALU.add),
                 reads=[cosT, pp, TiT], writes=[TiT])
            S.op("pool", lambda e: e.memset(rB[:, sb, :], 0.0), writes=[rB])
            S.op("dve", lambda e: e.tensor_scalar(rB[:, sb, :], rB[:, sb, :], pp[:, d, 1, sb:sb + 1], None, ALU.add), reads=[rB, pp], writes=[rB])
        S.op("pool", lambda e: e.memset(car[:, :, :], 0.0), writes=[car])
        nwc = CTXN // W
        wins = list(range(NWIN)) if d == 0 else (list(range(nwc))[::-1] + list(range(nwc, NWIN))[::-1])
        for wi in wins:
            t0 = wi * W
            if d == 1:
                S.dma("sp", yfs[:, :, :], yfv[:, :, t0:t0 + W], reads=[yfd], writes=[yfs])
            pyt = py[wi % 2]
            for fb in range(4):
                i2 = it % 2
                it += 1
                sbs = slice(4 * fb, 4 * fb + 4)
                if d == 0:
                    urhs = u[:, fb, t0:t0 + W]
                else:
                    urhs = u[:, fb, t0:t0 + W][:, ::-1]
                pr, pi_ = pre_[i2], pim_[i2]
                for s4 in range(4):
                    sb = 4 * fb + s4
                    S.op("pe", lambda e: e.matmul(pr[:, s4 * W:(s4 + 1) * W], bblkb[:, 0, sb, :], urhs, start=True, stop=True),
                         reads=[bblkb, u], writes=[pr])
                    S.op("pe", lambda e: e.matmul(pi_[:, s4 * W:(s4 + 1) * W], bblkb[:, 1, sb, :], urhs, start=True, stop=True),
                         reads=[bblkb, u], writes=[pi_])
                a, b, c_, d_ = ta[i2], tb[i2], tc_[i2], td[i2]
                xr_, xi_ = Xr[i2], Xi[i2]
                Trv = TrT[:, sbs, :].rearrange("p a w -> p (a w)")
                Tiv = TiT[:, sbs, :].rearrange("p a w -> p (a w)")
                cov = cosT[:, sbs, :].rearrange("p a w -> p (a w)")
                siv = sinT[:, sbs, :].rearrange("p a w -> p (a w)")
                xrv = xr_[:, :, :].rearrange("p a w -> p (a w)")
                xiv = xi_[:, :, :].rearrange("p a w -> p (a w)")
                S.op("dve", lambda e: e.tensor_tensor(a[:, :], pr[:, :], Trv, ALU.mult), reads=[pr, TrT], writes=[a])
                S.op("dve", lambda e: e.tensor_tensor(b[:, :], pi_[:, :], Tiv, ALU.mult), reads=[pi_, TiT], writes=[b])
                S.op("pool", lambda e: e.tensor_tensor(xrv, a[:, :], b[:, :], ALU.subtract), reads=[a, b], writes=[xr_])
                S.op("dve", lambda e: e.tensor_tensor(c_[:, :], pr[:, :], Tiv, ALU.mult), reads=[pr, TiT], writes=[c_])
                S.op("dve", lambda e: e.tensor_tensor(d_[:, :], pi_[:, :], Trv, ALU.mult), reads=[pi_, TrT], writes=[d_])
                S.op("pool", lambda e: e.tensor_tensor(xiv, c_[:, :], d_[:, :], ALU.add), reads=[c_, d_], writes=[xi_])
                for s4 in range(4):
                    sb = 4 * fb + s4
                    S.op("dve", lambda e: e.tensor_tensor_scan(xr_[:, s4, :], rB[:, sb, :], xr_[:, s4, :], car[:, 0, sb:sb + 1], ALU.mult, ALU.add),
                         reads=[rB, xr_, car], writes=[xr_])
                    S.op("dve", lambda e: e.tensor_tensor_scan(xi_[:, s4, :], rB[:, sb, :], xi_[:, s4, :], car[:, 1, sb:sb + 1], ALU.mult, ALU.add),
                         reads=[rB, xi_, car], writes=[xi_])
                s_r, s_i = sr[i2], si[i2]
                srv = s_r[:, :, :].rearrange("p a w -> p (a w)")
                siv2 = s_i[:, :, :].rearrange("p a w -> p (a w)")
                S.op("pool", lambda e: e.tensor_tensor(a[:, :], xrv, cov, ALU.mult), reads=[xr_, cosT], writes=[a])
                S.op("pool", lambda e: e.tensor_tensor(b[:, :], xiv, siv, ALU.mult), reads=[xi_, sinT], writes=[b])
                S.op("dve", lambda e: e.tensor_tensor(srv, a[:, :], b[:, :], ALU.subtract), reads=[a, b], writes=[s_r])
                S.op("pool", lambda e: e.tensor_tensor(c_[:, :], xrv, siv, ALU.mult), reads=[xr_, sinT], writes=[c_])
                S.op("pool", lambda e: e.tensor_tensor(d_[:, :], xiv, cov, ALU.mult), reads=[xi_, cosT], writes=[d_])
                S.op("dve", lambda e: e.tensor_tensor(siv2, c_[:, :], d_[:, :], ALU.add), reads=[c_, d_], writes=[s_i])
                S.op("act", lambda e: e.activation(srb[i2][:, :, :], s_r[:, :, :], AF.Copy), reads=[s_r], writes=[srb[i2]])
                S.op("act", lambda e: e.activation(sib[i2][:, :, :], s_i[:, :, :], AF.Copy), reads=[s_i], writes=[sib[i2]])
                S.op("act", lambda e: e.activation(car[:, 0, sbs], s_r[:, :, W - 1], AF.Copy), reads=[s_r], writes=[car])
                S.op("act", lambda e: e.activation(car[:, 1, sbs], s_i[:, :, W - 1], AF.Copy), reads=[s_i], writes=[car])
                yo = pyt[:, fb * W:(fb + 1) * W]
                for s4 in range(4):
                    sb = 4 * fb + s4
                    S.op("pe", lambda e: e.matmul(yo, cblkb[:, d, 0, sb, :], srb[i2][:, s4, :], start=(s4 == 0), stop=False),
                         reads=[cblkb, srb[i2]], writes=[pyt])
                    S.op("pe", lambda e: e.matmul(yo, cblkb[:, d, 1, sb, :], sib[i2][:, s4, :], start=False, stop=(s4 == 3)),
                         reads=[cblkb, sib[i2]], writes=[pyt])
            if d == 0:
                S.op("act", lambda e: e.activation(ych[:, :, :].rearrange("p f w -> p (f w)"), pyt[:, :], AF.Copy), reads=[pyt], writes=[ych])
                S.dma("sp", yfv[:, :, t0:t0 + W], ych[:, :, :], reads=[ych], writes=[yfd])
            else:
                for fb in range(4):
                    S.op("dve", lambda e: e.tensor_tensor(ych[:, fb, :], pyt[:, fb * W:(fb + 1) * W][:, ::-1], yfs[:, fb, :], ALU.add),
                         reads=[pyt, yfs], writes=[ych])
                    S.op("dve", lambda e: e.scalar_tensor_tensor(ych[:, fb, :], u[:, fb, t0:t0 + W], dsk[:, fb:fb + 1], ych[:, fb, :],
                                                                 ALU.mult, ALU.add), reads=[u, dsk, ych], writes=[ych])
                S.op("act", lambda e: e.activation(g1[:, :, :], ych[:, :, :], AF.Square), reads=[ych], writes=[g1])
                S.op("dve", lambda e: e.tensor_scalar(g1[:, :, :], g1[:, :, :], 0.044715, 1.0, ALU.mult, ALU.add), reads=[g1], writes=[g1])
                S.op("dve", lambda e: e.tensor_tensor(g1[:, :, :], g1[:, :, :], ych[:, :, :], ALU.mult), reads=[g1, ych], writes=[g1])
                S.op("act", lambda e: e.activation(g1[:, :, :], g1[:, :, :], AF.Sigmoid, scale=1.5957691216057308), reads=[g1], writes=[g1])
                S.op("dve", lambda e: e.tensor_tensor(yob[:, :, :], g1[:, :, :], ych[:, :, :], ALU.mult), reads=[g1, ych], writes=[yob])
                for (r, o, ts, ln) in seq_ranges(t0, W, TL):
                    S.dma("sp", yv[r][:, :, o:o + ln], yob[:, :, ts - t0:ts - t0 + ln], reads=[yob], writes=[yout])
    S.barrier()
    return nc


def s5_host(inp, j, b, TL):
    gs = slice(32 * b, 32 * (b + 1))
    def st_layout(a):
        return np.ascontiguousarray(a.reshape(16, 2, 64).transpose(1, 2, 0).reshape(128, 16))
    lam = np.zeros((128, 2, 2, 16), np.float32)
    stp = np.zeros((128, 2, 16), np.float32)
    for d in range(2):
        lam[:, 0, d] = st_layout(inp["s5_lam_re"][j][d, gs])
        lam[:, 1, d] = st_layout(inp["s5_lam_im"][j][d, gs])
        stp[:, d] = st_layout(np.broadcast_to(inp["s5_log_step"][j][d, gs][:, None], (32, 64)))
    bblk = np.zeros((128, 2, 16, 128), np.float32)
    for ri, key in enumerate(("s5_b_re", "s5_b_im")):
        B = inp[key][j][gs]
        for sb in range(16):
            for gg in range(2):
                g = 2 * sb + gg
                rows = 32 * (sb % 4) + 16 * gg
                bblk[rows:rows + 16, ri, sb, 64 * gg:64 * gg + 64] = B[g].T
    cblk = np.zeros((128, 2, 2, 16, 128), np.float32)
    for ri, key in enumerate(("s5_c_re", "s5_c_im")):
        for d in range(2):
            C = inp[key][j][d, gs]
            for sb in range(16):
                for gg in range(2):
                    g = 2 * sb + gg
                    co = 32 * (sb % 4) + 16 * gg
                    cblk[64 * gg:64 * gg + 64, d, ri, sb, co:co + 16] = C[g].T
    dsk = pvec(inp["s5_d"][j][512 * b:512 * (b + 1)])
    jrow = np.ascontiguousarray(np.broadcast_to(np.arange(1, S5W + 1, dtype=np.float32)[None], (128, S5W)))
    return {"lam": lam, "stp": stp, "bblk": bblk, "cblk": cblk, "dsk": dsk, "jrow": jrow}


_PROGS = {}
TIMES = []


def _prog(key, fn):
    if key not in _PROGS:
        _PROGS[key] = fn()
    return _PROGS[key]


def _run(nc, maps):
    import time
    t0 = time.time()
    res = run_bass_kernel_spmd(nc, maps, core_ids=list(range(NR)))
    TIMES.append(time.time() - t0)
    return res.results


def ffn_host(inp, i, c):
    cols = np.r_[512 * c:512 * c + 512, D + 512 * c:D + 512 * c + 512]
    wc = np.ascontiguousarray(inp["ffn_up"][i][:, cols])
    cp = np.concatenate([inp["ffn_conv_w"][i][:, cols], inp["ffn_conv_b"][i][None, cols]], 0)
    cp = np.ascontiguousarray(cp.T.reshape(8, 128, 4).transpose(1, 0, 2))
    return {"w": wc, "convp": cp}


def make_vec(g, gain, scale, shift, ba=None, bb=None):
    v = np.zeros((128, NVEC, KD), np.float32)
    if g is not None:
        v[:, 0], v[:, 1] = pvec(g[:, 0]), pvec(g[:, 1])
    if gain is not None:
        v[:, 2] = pvec(gain)
        v[:, 3], v[:, 4] = pvec(scale[:, 0]), pvec(scale[:, 1])
        v[:, 5], v[:, 6] = pvec(shift[:, 0]), pvec(shift[:, 1])
    if ba is not None:
        v[:, 7], v[:, 8] = pvec(ba), pvec(bb)
    return v


def forward(inp, TL, depth=4, x=None, ctx=None):
    TT, NT = dims(TL)
    x = inp["x"][0] if x is None else x
    ctx = inp["ctx"][0] if ctx is None else ctx
    cvec = np.stack([inp["c_ctx"], inp["c"][0]], -1).astype(np.float32)
    cvec = np.ascontiguousarray(cvec.reshape(KD, 128, 2).transpose(1, 0, 2))
    ncm = _prog(("mod", depth), lambda: build_mod(depth))
    maps = []
    for c in range(NR):
        cs = slice(MODC * c, MODC * (c + 1))
        modb = inp["mod_b"][:depth, cs]
        maps.append({"cvec": cvec, "modw": np.ascontiguousarray(inp["mod_w"][:depth, :, cs]),
                     "modb": np.ascontiguousarray(modb.reshape(depth, MODB, 128).transpose(2, 0, 1))})
    res = _run(ncm, maps)
    mod = np.zeros((depth, 6 * D, 2), np.float32)
    for c in range(NR):
        o = res[c]["modo"]
        mod[:, MODC * c:MODC * (c + 1), :] = o.transpose(1, 2, 0, 3).reshape(depth, MODC, 2)
    mods = mod.reshape(depth, 6, D, 2)

    xs = [np.ascontiguousarray(np.concatenate([ctx[TCTX * r:TCTX * (r + 1)], x[TL * r:TL * (r + 1)]], 0).T) for r in range(NR)]
    ncn = _prog(("B", 0, False, TL, False, True), lambda: build_B(128, False, TL, gemm=False, do_norm=True))
    vec = make_vec(None, inp["norm_mix"][0], mods[0, 1], mods[0, 0])
    res = _run(ncn, [{"xres": xs[r], "vec": vec} for r in range(NR)])
    hs = [res[r]["hnext"] for r in range(NR)]

    for i in range(depth):
        kind, j = i % 3, i // 3
        hfull = np.ascontiguousarray(np.stack(hs, 0))
        want_ctx = i < depth - 1
        if kind == 0:
            nca = _prog(("ssdA", TL), lambda: build_ssdA(TL))
            outs = _run(nca, [dict(ssd_host(inp, j, c), hfull=hfull) for c in range(NR)])
            Kb, glu, wB = 8192, False, inp["ssd_w_out"][j]
            ba = bb = None
        elif kind == 1:
            nca = _prog(("s5A", TL), lambda: build_s5A(TL))
            outs = _run(nca, [dict(s5_host(inp, j, c, TL), hs5=np.ascontiguousarray(hfull[:, 512 * c:512 * (c + 1), :]))
                              for c in range(NR)])
            Kb, glu, wB = 4096, True, inp["s5_glu_w"][j]
            ba, bb = inp["s5_glu_b"][j][:D], inp["s5_glu_b"][j][D:]
        else:
            li = 0.8 - 0.6 * math.exp(-0.3 * i)
            nca = _prog(("attA", TL, i, want_ctx), lambda: build_attA(TL, li, want_ctx))
            outs = _run(nca, [dict(att_host(inp, j, c, TL, li), hfull=hfull) for c in range(NR)])
            Kb, glu, wB = 4096, False, inp["da_w_o"][j]
            ba = bb = None
        yin = gather_tokens(outs, "yout")
        ncb = _prog(("B", Kb, glu, TL, True, True), lambda: build_B(Kb, glu, TL))
        vec = make_vec(mods[i, 2], inp["norm_ffn"][i], mods[i, 4], mods[i, 3], ba, bb)
        wB = np.ascontiguousarray(wB)
        res = _run(ncb, [{"xres": xs[r], "vec": vec, "yin": yin[r], "w": wB} for r in range(NR)])
        xs = [res[r]["xnew"] for r in range(NR)]
        hs = [res[r]["hnext"] for r in range(NR)]
        hfull = np.ascontiguousarray(np.stack(hs, 0))
        ncf = _prog(("ffnA", TL), lambda: build_ffnA(TL))
        outs = _run(ncf, [dict(ffn_host(inp, i, c), hfull=hfull) for c in range(NR)])
        yin = gather_tokens(outs, "yout")
        last = i == depth - 1
        ncb = _prog(("B", 4096, False, TL, True, not last), lambda: build_B(4096, False, TL, do_norm=not last))
        if last:
            vec = make_vec(mods[i, 5], None, None, None)
        else:
            vec = make_vec(mods[i, 5], inp["norm_mix"][i + 1], mods[i + 1, 1], mods[i + 1, 0])
        wB = np.ascontiguousarray(inp["ffn_down"][i])
        res = _run(ncb, [{"xres": xs[r], "vec": vec, "yin": yin[r], "w": wB} for r in range(NR)])
        xs = [res[r]["xnew"] for r in range(NR)]
        if not last:
            hs = [res[r]["hnext"] for r in range(NR)]
    out = np.concatenate([xs[r][:, TCTX:].T for r in range(NR)], 0)
    ctx_out = np.concatenate([xs[r][:, :TCTX].T for r in range(NR)], 0)
    return np.ascontiguousarray(out[None]).astype(np.float32), ctx_out


def kernel(**inputs):
    inp = {k: np.asarray(v) for k, v in inputs.items()}
    out, _ = forward(inp, TLAT)
    import sys
    print("launch wall times (s):", [round(t, 1) for t in TIMES], file=sys.stderr)
    return out
```

```python
import contextlib
import math
import numpy as np
import ml_dtypes
import concourse.bass as bass
import concourse.mybir as mybir
from concourse.bass_utils import run_bass_kernel_spmd

F32 = mybir.dt.float32
BF16 = mybir.dt.bfloat16
AF = mybir.ActivationFunctionType
ALU = mybir.AluOpType
AX = mybir.AxisListType

NR = 8
D = 4096
KD = D // 128
TCTX = 32
CTXN = NR * TCTX
EPS = 1e-6
TLAT = 1024


def dims(TL):
    TT = TCTX + TL
    NT = CTXN + NR * TL
    return TT, NT


class Buf:
    __slots__ = ("name", "w", "r")

    def __init__(self, name=""):
        self.name = name
        self.w = None
        self.r = {}


class T:
    def __init__(self, h, name=""):
        self.h = h
        self.b = Buf(name)

    def __getitem__(self, idx):
        return self.h[idx]


class Sched:
    SEM_ROT = 30000

    def __init__(self, nc, n_dma_sems=16):
        self.nc = nc
        self.eng = {"pe": nc.tensor, "act": nc.scalar, "dve": nc.vector,
                    "pool": nc.gpsimd, "sp": nc.sync}
        self.sem, self.cnt, self.semgen = {}, {}, {}
        for k in self.eng:
            self.semgen[k] = 0
            self.sem[k] = nc.alloc_semaphore(f"s_{k}_0")
            self.cnt[k] = 0
        self.seen = {k: {} for k in self.eng}
        self.dsem = [nc.alloc_semaphore(f"d_{i}") for i in range(n_dma_sems)]
        self.dcnt = [0] * n_dma_sems
        self.dnext = 0
        self.n_inst = 0
        self.n_wait = 0
        self.stack = contextlib.ExitStack()

    def _wait(self, e, dep):
        if dep is None:
            return
        key, val = dep
        if key[0] == "e" and key[1] == e and e in ("pe", "sp"):
            return
        if self.seen[e].get(key, 0) >= val:
            return
        self.seen[e][key] = val
        self.eng[e].wait_ge(key[2], val)
        self.n_wait += 1

    def _deps(self, e, reads, writes):
        for b in reads:
            self._wait(e, b.w)
        for b in writes:
            self._wait(e, b.w)
            for k, v in b.r.items():
                self._wait(e, (k, v))

    def _mark(self, me, reads, writes):
        k, v = me
        for b in reads:
            if b.r.get(k, 0) < v:
                b.r[k] = v
        for b in writes:
            b.w = me
            b.r = {}

    def op(self, e, fn, reads=(), writes=()):
        reads = [t.b if isinstance(t, T) else t for t in reads]
        writes = [t.b if isinstance(t, T) else t for t in writes]
        self._deps(e, reads, writes)
        if self.cnt[e] >= self.SEM_ROT:
            self.semgen[e] += 1
            self.sem[e] = self.nc.alloc_semaphore(f"s_{e}_{self.semgen[e]}")
            self.cnt[e] = 0
        inst = fn(self.eng[e])
        self.cnt[e] += 1
        inst.then_inc(self.sem[e], 1)
        me = (("e", e, self.sem[e]), self.cnt[e])
        self._mark(me, reads, writes)
        self.n_inst += 1
        return inst

    def dma(self, q, out, in_, reads=(), writes=(), **kw):
        reads = [t.b if isinstance(t, T) else t for t in reads]
        writes = [t.b if isinstance(t, T) else t for t in writes]
        i = self.dnext
        self.dnext = (self.dnext + 1) % len(self.dsem)
        key = ("d", i, self.dsem[i])
        if self.dcnt[i] > 0:
            self._wait(q, (key, self.dcnt[i]))
        self._deps(q, reads, writes)
        inst = self.eng[q].dma_start(out=out, in_=in_, **kw)
        self.dcnt[i] += 16
        inst.then_inc(self.dsem[i], 16)
        me = (key, self.dcnt[i])
        self._mark(me, reads, writes)
        self.n_inst += 1
        return inst

    def coll(self, kind, in_ap, out_ap, reads=(), writes=(), q="pool"):
        reads = [t.b if isinstance(t, T) else t for t in reads]
        writes = [t.b if isinstance(t, T) else t for t in writes]
        i = self.dnext
        self.dnext = (self.dnext + 1) % len(self.dsem)
        key = ("d", i, self.dsem[i])
        if self.dcnt[i] > 0:
            self._wait(q, (key, self.dcnt[i]))
        self._deps(q, reads, writes)
        inst = self.nc.gpsimd.collective_compute(kind, ALU.bypass, [list(range(NR))], [in_ap], [out_ap])
        self.dcnt[i] += 16
        inst.then_inc(self.dsem[i], 16)
        me = (key, self.dcnt[i])
        self._mark(me, reads, writes)
        self.n_inst += 1
        return inst

    def barrier(self):
        for e in self.eng:
            for e2 in self.eng:
                if e2 != e and self.cnt[e2] > 0:
                    self._wait(e, (("e", e2, self.sem[e2]), self.cnt[e2]))
            for i, s in enumerate(self.dsem):
                if self.dcnt[i] > 0:
                    self._wait(e, (("d", i, s), self.dcnt[i]))

    def sb(self, name, shape, dt, stack=None):
        st = stack if stack is not None else self.stack
        self.uid = getattr(self, "uid", 0) + 1
        h = st.enter_context(self.nc.sbuf_tensor(f"sb{self.uid}_{name}", list(shape), dt))
        return T(h, name)

    def ps(self, name, shape=(128, 512), dt=F32, stack=None):
        st = stack if stack is not None else self.stack
        self.uid = getattr(self, "uid", 0) + 1
        h = st.enter_context(self.nc.psum_tensor(f"ps{self.uid}_{name}", list(shape), dt))
        return T(h, name)


def new_prog():
    nc = bass.Bass("TRN2", target_bir_lowering=False)
    return nc, Sched(nc)


def din(nc, name, shape, dt=F32):
    return T(nc.dram_tensor(name, list(shape), dt, kind="ExternalInput"), name)


def dout(nc, name, shape, dt=F32):
    return T(nc.dram_tensor(name, list(shape), dt, kind="ExternalOutput"), name)


def dscr(nc, name, shape, dt=F32):
    return T(nc.dram_tensor(name, list(shape), dt, kind="Internal"), name)


def tok_tiles(n, maxn=512):
    k = (n + maxn - 1) // maxn
    assert n % k == 0, (n, k)
    return k, n // k


MODC = 6 * D // NR
MODB = MODC // 128


def build_mod(nlayers=4):
    nc, S = new_prog()
    cvec = din(nc, "cvec", [128, KD, 2])
    modw = din(nc, "modw", [nlayers, D, MODC])
    modb = din(nc, "modb", [128, nlayers, MODB])
    out = dout(nc, "modo", [128, nlayers, MODB, 2])
    cs = S.sb("cs", [128, KD, 2], F32)
    cb = S.sb("cb", [128, KD, 2], BF16)
    bs = S.sb("bs", [128, nlayers, MODB], F32)
    os_ = S.sb("os", [128, nlayers, MODB, 2], F32)
    wt = [S.sb(f"wt{i}", [128, KD, 512], BF16) for i in range(2)]
    pt = [S.ps(f"pt{i}") for i in range(2)]
    S.dma("sp", cs[:, :, :], cvec[:, :, :], reads=[cvec], writes=[cs])
    S.dma("sp", bs[:, :, :], modb[:, :, :], reads=[modb], writes=[bs])
    S.op("act", lambda e: e.activation(cb[:, :, :], cs[:, :, :], AF.Silu), reads=[cs], writes=[cb])
    it = 0
    for l in range(nlayers):
        wv = modw.h.ap()[l].rearrange("(k p) c -> p k c", p=128)
        for g in range(MODB // 4):
            w = wt[it % 2]
            p = pt[it % 2]
            it += 1
            S.dma("pool", w[:, :, :], wv[:, :, g * 512:(g + 1) * 512], reads=[modw], writes=[w])
            for j in range(4):
                for k in range(KD):
                    S.op("pe", lambda e: e.matmul(p[:, 2 * j:2 * j + 2], w[:, k, j * 128:(j + 1) * 128],
                                                  cb[:, k, :], start=(k == 0), stop=(k == KD - 1)),
                         reads=[w, cb], writes=[p])
            for j in range(4):
                b = g * 4 + j
                S.op("dve", lambda e: e.tensor_scalar(os_[:, l, b, :], p[:, 2 * j:2 * j + 2],
                                                      bs[:, l, b:b + 1], None, ALU.add),
                     reads=[p, bs], writes=[os_])
    S.dma("sp", out[:, :, :, :], os_[:, :, :, :], reads=[os_], writes=[out])
    S.barrier()
    return nc


NVEC = 9


def emit_norm_mod(S, xn, ss_ps, vec, hout, tsl, ntok, c0n, st):
    rstd = S.sb("rstd", [128, ntok], F32, st)
    s1 = S.sb("s1", [128, 2, KD], F32, st)
    hb = [S.sb(f"hb{i}", [128, ntok], BF16, st) for i in range(2)]
    tmp = [S.sb(f"ntmp{i}", [128, ntok], F32, st) for i in range(2)]
    S.op("dve", lambda e: e.tensor_scalar(rstd[:, :], ss_ps[:, 0:ntok], 1.0 / D, EPS, ALU.mult, ALU.add),
         reads=[ss_ps], writes=[rstd])
    S.op("act", lambda e: e.activation(rstd[:, :], rstd[:, :], AF.Sqrt), reads=[rstd], writes=[rstd])
    S.op("dve", lambda e: e.reciprocal(rstd[:, :], rstd[:, :]), reads=[rstd], writes=[rstd])
    for j in range(2):
        S.op("dve", lambda e: e.scalar_tensor_tensor(s1[:, j, :], vec[:, 3 + j, :], 1.0, vec[:, 2, :],
                                                     ALU.add, ALU.mult), reads=[vec], writes=[s1])
    for ob in range(KD):
        t = tmp[ob % 2]
        h = hb[ob % 2]
        S.op("pool", lambda e: e.tensor_tensor(t[:, :], xn[:, ob, :], rstd[:, :], ALU.mult),
             reads=[xn, rstd], writes=[t])
        if c0n > 0:
            S.op("dve", lambda e: e.tensor_scalar(h[:, 0:c0n], t[:, 0:c0n], s1[:, 0, ob:ob + 1],
                                                  vec[:, 5, ob:ob + 1], ALU.mult, ALU.add),
                 reads=[t, s1, vec], writes=[h])
        S.op("dve", lambda e: e.tensor_scalar(h[:, c0n:ntok], t[:, c0n:ntok], s1[:, 1, ob:ob + 1],
                                              vec[:, 6, ob:ob + 1], ALU.mult, ALU.add),
             reads=[t, s1, vec], writes=[h])
        S.dma("sp", hout.h.ap()[ob * 128:(ob + 1) * 128, tsl], h[:, :], reads=[h], writes=[hout])


def build_B(K, glu, TL, gemm=True, do_norm=True):
    TT, NT = dims(TL)
    nc, S = new_prog()
    KC = K // 128
    Cout = 2 * D if glu else D
    xres = din(nc, "xres", [D, TT])
    vecd = din(nc, "vec", [128, NVEC, KD])
    if gemm:
        yin = din(nc, "yin", [K, TT], BF16)
        w = din(nc, "w", [K, Cout])
    xnew = dout(nc, "xnew", [D, TT])
    hnext = dout(nc, "hnext", [D, TT], BF16) if do_norm else None
    ntile, ntok = tok_tiles(TT, 512)
    vec = S.sb("vec", [128, NVEC, KD], F32)
    ones = S.sb("ones", [128, 128], F32)
    S.dma("sp", vec[:, :, :], vecd[:, :, :], reads=[vecd], writes=[vec])
    S.op("pool", lambda e: e.memset(ones[:, :], 1.0), writes=[ones])
    ss = [S.ps(f"ss{i}") for i in range(ntile)]
    with contextlib.ExitStack() as st:
        if gemm:
            yt = S.sb("yt", [128, KC, TT], BF16, st)
            S.dma("sp", yt[:, :, :], yin.h.ap().rearrange("(k p) t -> p k t", p=128), reads=[yin], writes=[yt])
            ngrp = 2 if glu else 1
            wt = [S.sb(f"wt{i}", [128, KC, ngrp, 128], BF16, st) for i in range(2)]
            pa = [S.ps(f"pa{i}", stack=st) for i in range(2)]
            pb = [S.ps(f"pb{i}", stack=st) for i in range(2)] if glu else None
            sg = [S.sb(f"sg{i}", [128, ntok], F32, st) for i in range(2)] if glu else None
            za = [S.sb(f"za{i}", [128, ntok], F32, st) for i in range(2)] if glu else None
            wv = w.h.ap().rearrange("(k p) c -> p k c", p=128)
        xr = [S.sb(f"xr{i}", [128, ntok], F32, st) for i in range(2)]
        xo = [S.sb(f"xo{i}", [128, ntok], F32, st) for i in range(2)]
        sq = [S.sb(f"sq{i}", [128, ntok], F32, st) for i in range(2)]
        it = 0
        for ob in range(KD):
            if gemm:
                wtile = wt[ob % 2]
                S.dma("pool", wtile[:, :, 0, :], wv[:, :, ob * 128:(ob + 1) * 128], reads=[w], writes=[wtile])
                if glu:
                    S.dma("pool", wtile[:, :, 1, :], wv[:, :, D + ob * 128:D + (ob + 1) * 128], reads=[w], writes=[wtile])
            for tt in range(ntile):
                tsl = slice(tt * ntok, (tt + 1) * ntok)
                c0n = TCTX if tt == 0 else 0
                i2 = it % 2
                it += 1
                x_, xn_ = xr[i2], xo[i2]
                S.dma("sp", x_[:, :], xres.h.ap()[ob * 128:(ob + 1) * 128, tsl], reads=[xres], writes=[x_])
                if gemm:
                    p = pa[i2]
                    for k in range(KC):
                        S.op("pe", lambda e: e.matmul(p[:, 0:ntok], wtile[:, k, 0, :], yt[:, k, tsl],
                                                      start=(k == 0), stop=(k == KC - 1)), reads=[wtile, yt], writes=[p])
                    src = p
                    if glu:
                        q = pb[i2]
                        for k in range(KC):
                            S.op("pe", lambda e: e.matmul(q[:, 0:ntok], wtile[:, k, 1, :], yt[:, k, tsl],
                                                          start=(k == 0), stop=(k == KC - 1)), reads=[wtile, yt], writes=[q])
                        s_, z_ = sg[i2], za[i2]
                        S.op("act", lambda e: e.activation(s_[:, :], q[:, 0:ntok], AF.Sigmoid, bias=vec[:, 8, ob:ob + 1]),
                             reads=[q, vec], writes=[s_])
                        S.op("dve", lambda e: e.scalar_tensor_tensor(z_[:, :], p[:, 0:ntok], vec[:, 7, ob:ob + 1], s_[:, :],
                                                                     ALU.add, ALU.mult), reads=[p, vec, s_], writes=[z_])
                        src = z_
                    if c0n > 0:
                        S.op("dve", lambda e: e.scalar_tensor_tensor(xn_[:, 0:c0n], src[:, 0:c0n], vec[:, 0, ob:ob + 1], x_[:, 0:c0n],
                                                                     ALU.mult, ALU.add), reads=[src, vec, x_], writes=[xn_])
                    S.op("dve", lambda e: e.scalar_tensor_tensor(xn_[:, c0n:ntok], src[:, c0n:ntok], vec[:, 1, ob:ob + 1],
                                                                 x_[:, c0n:ntok], ALU.mult, ALU.add), reads=[src, vec, x_], writes=[xn_])
                else:
                    xn_ = x_
                S.dma("sp", xnew.h.ap()[ob * 128:(ob + 1) * 128, tsl], xn_[:, :], reads=[xn_], writes=[xnew])
                if do_norm:
                    s2 = sq[i2]
                    S.op("act", lambda e: e.activation(s2[:, :], xn_[:, :], AF.Square), reads=[xn_], writes=[s2])
                    S.op("pe", lambda e: e.matmul(ss[tt][:, 0:ntok], ones[:, :], s2[:, :], start=(ob == 0), stop=(ob == KD - 1)),
                         reads=[ones, s2], writes=[ss[tt]])
        S.barrier()
    if do_norm:
        for tt in range(ntile):
            tsl = slice(tt * ntok, (tt + 1) * ntok)
            c0n = TCTX if tt == 0 else 0
            with contextlib.ExitStack() as st:
                xn = S.sb("xn", [128, KD, ntok], F32, st)
                S.dma("sp", xn[:, :, :], xnew.h.ap().rearrange("(k p) t -> p k t", p=128)[:, :, tsl], reads=[xnew], writes=[xn])
                emit_norm_mod(S, xn, ss[tt], vec, hnext, tsl, ntok, c0n, st)
                S.barrier()
    S.barrier()
    return nc


def seq_ranges(t0, n, TL):
    out = []
    t = t0
    end = t0 + n
    while t < end:
        if t < CTXN:
            r, o = divmod(t, TCTX)
            ln = min(TCTX - o, end - t)
            out.append((r, o, t, ln))
        else:
            r, o = divmod(t - CTXN, TL)
            ln = min(TL - o, end - t)
            out.append((r, TCTX + o, t, ln))
        t += ln
    return out


def seq_tiles(TL, maxn=512):
    NT = CTXN + NR * TL
    tiles = [(0, CTXN)]
    t = CTXN
    while t < NT:
        n = min(maxn, NT - t)
        tiles.append((t, n))
        t += n
    return tiles


def proj_all(S, hfull, w, C, pre, TL, evac_hook=None):
    NB = C // 128
    tiles = seq_tiles(TL)
    with contextlib.ExitStack() as st:
        wt = [S.sb(f"pw{i}", [128, KD, 512], BF16, st) for i in range(2)]
        ht = [S.sb(f"ph{i}", [128, KD, 512], BF16, st) for i in range(2)]
        og = [S.sb(f"po{i}", [128, 512], F32, st) for i in range(4)]
        pp = [S.ps(f"pp{i}", stack=st) for i in range(4)]
        wv = w.h.ap().rearrange("(k p) c -> p k c", p=128)
        hit = 0
        oit = 0
        for g in range((NB + 3) // 4):
            nb = min(4, NB - g * 4)
            wtile = wt[g % 2]
            S.dma("pool", wtile[:, :, 0:nb * 128], wv[:, :, g * 512:g * 512 + nb * 128], reads=[w], writes=[wtile])
            for (t0, n) in tiles:
                h = ht[hit % 2]
                hit += 1
                for (r, o, ts, ln) in seq_ranges(t0, n, TL):
                    S.dma("sp", h[:, :, ts - t0:ts - t0 + ln],
                          hfull.h.ap()[r].rearrange("(k p) t -> p k t", p=128)[:, :, o:o + ln],
                          reads=[hfull], writes=[h])
                for j in range(nb):
                    p = pp[oit % 4]
                    o_ = og[oit % 4]
                    oit += 1
                    for k in range(KD):
                        S.op("pe", lambda e: e.matmul(p[:, 0:n], wtile[:, k, j * 128:(j + 1) * 128], h[:, k, 0:n],
                                                      start=(k == 0), stop=(k == KD - 1)),
                             reads=[wtile, h], writes=[p])
                    if oit % 2 == 0:
                        S.op("dve", lambda e: e.tensor_copy(o_[:, 0:n], p[:, 0:n]), reads=[p], writes=[o_])
                    else:
                        S.op("act", lambda e: e.activation(o_[:, 0:n], p[:, 0:n], AF.Copy), reads=[p], writes=[o_])
                    cb = g * 4 + j
                    S.dma("sp", pre.h.ap()[cb * 128:(cb + 1) * 128, t0:t0 + n], o_[:, 0:n], reads=[o_], writes=[pre])
        S.barrier()


def zlayout(TL, K):
    P = K // 2
    NT = CTXN + NR * TL
    return P, P, 3 * P + CTXN, NT + 4 * P


def load_padded(S, Z, pre, row0, nrows, TL, K):
    P, c0, l0, tot = zlayout(TL, K)
    NT = CTXN + NR * TL
    S.dma("sp", Z[0:nrows, c0:c0 + CTXN], pre.h.ap()[row0:row0 + nrows, 0:CTXN], reads=[pre], writes=[Z])
    S.dma("sp", Z[0:nrows, l0:l0 + NT - CTXN], pre.h.ap()[row0:row0 + nrows, CTXN:NT], reads=[pre], writes=[Z])


def conv_seq(S, Z, pT, wc, bias, acc, TL, K, nrows=128, eng="dve"):
    P, c0, l0, tot = zlayout(TL, K)
    NT = CTXN + NR * TL
    for (zs, os_, n) in ((c0, 0, CTXN), (l0, CTXN, NT - CTXN)):
        S.op(eng, lambda e: e.tensor_scalar(acc[0:nrows, os_:os_ + n], Z[0:nrows, zs - P:zs - P + n],
                                            wc[0:nrows, 0:1], bias[0:nrows, 0:1], ALU.mult, ALU.add),
             reads=[Z, pT], writes=[acc])
        for j in range(1, K):
            S.op(eng, lambda e: e.scalar_tensor_tensor(acc[0:nrows, os_:os_ + n], Z[0:nrows, zs - P + j:zs - P + j + n],
                                                       wc[0:nrows, j:j + 1], acc[0:nrows, os_:os_ + n],
                                                       ALU.mult, ALU.add),
                 reads=[Z, pT, acc], writes=[acc])


def store_yout(S, yout, ob, row0, TL, nrows=128):
    yv = yout.h.ap().rearrange("r c t -> c r t")
    S.dma("sp", yv[row0:row0 + nrows, :, 0:TCTX], ob[0:nrows, 0:CTXN].rearrange("p (r t) -> p r t", t=TCTX),
          reads=[ob], writes=[yout])
    S.dma("sp", yv[row0:row0 + nrows, :, TCTX:TCTX + TL], ob[0:nrows, CTXN:CTXN + NR * TL].rearrange("p (r t) -> p r t", t=TL),
          reads=[ob], writes=[yout])


def build_ffnA(TL):
    TT, NT = dims(TL)
    nc, S = new_prog()
    hfull = din(nc, "hfull", [NR, D, TT], BF16)
    w = din(nc, "w", [D, 1024])
    convp = din(nc, "convp", [128, 8, 4])
    yout = dout(nc, "yout", [NR, 512, TT], BF16)
    pre = dscr(nc, "pre", [1024, NT])
    proj_all(S, hfull, w, 1024, pre, TL)
    P, c0, l0, tot = zlayout(TL, 3)
    cp = S.sb("cp", [128, 8, 4], F32)
    S.dma("sp", cp[:, :, :], convp[:, :, :], reads=[convp], writes=[cp])
    Zg = S.sb("Zg", [128, tot], F32)
    Zv = S.sb("Zv", [128, tot], F32)
    ag = S.sb("ag", [128, NT], F32)
    av = S.sb("av", [128, NT], F32)
    ob = [S.sb(f"ob{i}", [128, NT], BF16) for i in range(2)]
    S.op("pool", lambda e: e.memset(Zg[:, :], 0.0), writes=[Zg])
    S.op("pool", lambda e: e.memset(Zv[:, :], 0.0), writes=[Zv])
    for i in range(4):
        load_padded(S, Zg, pre, i * 128, 128, TL, 3)
        load_padded(S, Zv, pre, 512 + i * 128, 128, TL, 3)
        conv_seq(S, Zg, cp, cp[:, i, 0:3], cp[:, i, 3:4], ag, TL, 3)
        conv_seq(S, Zv, cp, cp[:, 4 + i, 0:3], cp[:, 4 + i, 3:4], av, TL, 3)
        S.op("act", lambda e: e.activation(ag[:, :], ag[:, :], AF.Silu), reads=[ag], writes=[ag])
        o_ = ob[i % 2]
        S.op("dve", lambda e: e.tensor_tensor(o_[:, :], ag[:, :], av[:, :], ALU.mult), reads=[ag, av], writes=[o_])
        store_yout(S, yout, o_, i * 128, TL)
    S.barrier()
    return nc


SSD_C = 2432
NEG = -30000.0


def ssd_consts():
    c = np.zeros((128, 6, 128), np.float32)
    i = np.arange(128)
    c[:, 0, :] = np.eye(128)
    c[:, 1, :] = (i[:, None] <= i[None, :])
    c[:, 2, :] = (i[:, None] >= i[None, :])
    c[:, 3, :] = np.where(i[None, :] >= i[:, None], 0.0, NEG)
    c[:, 4, :] = np.where(i[None, :] <= i[:, None], 0.0, NEG)
    c[:, 5, :] = 1.0
    return c


def build_ssdA(TL):
    TT, NT = dims(TL)
    nc, S = new_prog()
    hfull = din(nc, "hfull", [NR, D, TT], BF16)
    w = din(nc, "w", [D, SSD_C])
    convp = din(nc, "convp", [128, 10, 6])
    hpd = din(nc, "hp", [128, 8, 2])
    dtpd = din(nc, "dtp", [32, 2])
    constd = din(nc, "consts", [128, 6, 128])
    yout = dout(nc, "yout", [NR, 1024, TT], BF16)
    pre = dscr(nc, "pre", [SSD_C, NT])
    xs = dscr(nc, "xs", [1024, NT], BF16)
    Bs = dscr(nc, "Bs", [128, NT], BF16)
    Cs = dscr(nc, "Cs", [128, NT], BF16)
    dts = dscr(nc, "dts", [32, NT])
    das = dscr(nc, "das", [32, NT])
    yf = dscr(nc, "yf", [1024, NT])

    proj_all(S, hfull, w, SSD_C, pre, TL)

    P, c0, l0, tot = zlayout(TL, 5)
    with contextlib.ExitStack() as st:
        cp = S.sb("cp", [128, 10, 6], F32, st)
        dtp = S.sb("dtp", [32, 2], F32, st)
        S.dma("sp", cp[:, :, :], convp[:, :, :], reads=[convp], writes=[cp])
        S.dma("sp", dtp[:, :], dtpd[:, :], reads=[dtpd], writes=[dtp])
        Z = S.sb("Z", [128, tot], F32, st)
        acc = S.sb("acc", [128, NT], F32, st)
        ob = S.sb("ob", [128, NT], BF16, st)
        S.op("pool", lambda e: e.memset(Z[:, :], 0.0), writes=[Z])
        for i in range(10):
            load_padded(S, Z, pre, 1024 + i * 128, 128, TL, 5)
            conv_seq(S, Z, cp, cp[:, i, 0:5], cp[:, i, 5:6], acc, TL, 5)
            S.op("act", lambda e: e.activation(ob[:, :], acc[:, :], AF.Silu), reads=[acc], writes=[ob])
            if i < 8:
                S.dma("sp", xs.h.ap()[i * 128:(i + 1) * 128, :], ob[:, :], reads=[ob], writes=[xs])
            else:
                dst = Bs if i == 8 else Cs
                S.dma("sp", dst.h.ap()[:, :], ob[:, :], reads=[ob], writes=[dst])
        dtt = S.sb("dtt", [32, NT], F32, st)
        dat = S.sb("dat", [32, NT], F32, st)
        av = S.sb("av", [32, 1], F32, st)
        S.dma("sp", dtt[:, :], pre.h.ap()[2304:2336, :], reads=[pre], writes=[dtt])
        S.op("act", lambda e: e.activation(dtt[:, :], dtt[:, :], AF.Exp, bias=dtp[:, 0:1]), reads=[dtt, dtp], writes=[dtt])
        S.op("act", lambda e: e.activation(av[:, :], dtp[:, 1:2], AF.Exp), reads=[dtp], writes=[av])
        S.op("act", lambda e: e.activation(dtt[:, :], dtt[:, :], AF.Ln, bias=1.0), reads=[dtt], writes=[dtt])
        S.op("dve", lambda e: e.tensor_scalar(dat[:, :], dtt[:, :], av[:, 0:1], -1.0, ALU.mult, ALU.mult),
             reads=[dtt, av], writes=[dat])
        S.dma("sp", dts.h.ap()[:, :], dtt[:, :], reads=[dtt], writes=[dts])
        S.dma("sp", das.h.ap()[:, :], dat[:, :], reads=[dat], writes=[das])
        S.barrier()

    nchunk_c = CTXN // 128
    nchunk_l = NR * TL // 128
    ctx_chunks = list(range(nchunk_c))
    lat_chunks = list(range(nchunk_c, nchunk_c + nchunk_l))
    with contextlib.ExitStack() as st:
        cst = S.sb("cst", [128, 6, 128], F32, st)
        cstb = S.sb("cstb", [128, 128], BF16, st)
        hp = S.sb("hp", [128, 8, 2], F32, st)
        S.dma("sp", cst[:, :, :], constd[:, :, :], reads=[constd], writes=[cst])
        S.dma("sp", hp[:, :, :], hpd[:, :, :], reads=[hpd], writes=[hp])
        S.op("dve", lambda e: e.tensor_copy(cstb[:, :], cst[:, 0, :]), reads=[cst], writes=[cstb])
        H = S.sb("H", [128, 16, 64], F32, st)
        Hb = S.sb("Hb", [128, 16, 64], BF16, st)
        Ht = S.sb("Ht", [128, 16, 64], F32, st)
        xT = S.sb("xT", [128, 8, 128], BF16, st)
        BT = S.sb("BT", [128, 128], BF16, st)
        CT = S.sb("CT", [128, 128], BF16, st)
        dd = S.sb("dd", [32, 2, 128], F32, st)
        xtok = S.sb("xtok", [128, 16, 64], BF16, st)
        xdt = S.sb("xdt", [128, 16, 64], BF16, st)
        xdtw = S.sb("xdtw", [128, 16, 64], BF16, st)
        Btok = S.sb("Btok", [128, 128], BF16, st)
        dtok = S.sb("dtok", [128, 2, 32], F32, st)
        cum = S.sb("cum", [128, 16], F32, st)
        ncum = S.sb("ncum", [128, 16], F32, st)
        totb = S.sb("totb", [128, 16], F32, st)
        cd = S.sb("cd", [128, 16], F32, st)
        dte = S.sb("dte", [128, 16], F32, st)
        cbT = S.sb("cbT", [128, 128], F32, st)
        Ecum = S.sb("Ecum", [128, 4, 128], F32, st)
        Ck = S.sb("Ck", [128, 4, 128], BF16, st)
        Ek = [S.sb(f"Ek{i}", [128, 128], F32, st) for i in range(2)]
        Mk = [S.sb(f"Mk{i}", [128, 128], BF16, st) for i in range(2)]
        yfs = S.sb("yfs", [128, 8, 128], F32, st)
        zc = S.sb("zc", [128, 8, 128], F32, st)
        ych = S.sb("ych", [128, 8, 128], F32, st)
        ysq = S.sb("ysq", [128, 128], F32, st)
        rs = S.sb("rs", [128, 128], F32, st)
        yob = S.sb("yob", [128, 8, 128], BF16, st)
        p_misc = S.ps("p_misc", stack=st)
        p_xt = S.ps("p_xt", [128, 1024], BF16, st)
        p_s = S.ps("p_s", stack=st)
        p_d = S.ps("p_d", stack=st)
        p_m = S.ps("p_m", stack=st)
        p_y = [S.ps(f"p_y{i}", stack=st) for i in range(2)]

        for d in range(2):
            tri = cst[:, 1 + d, :]
            negm = cst[:, 3 + d, :]
            S.op("pool", lambda e: e.memset(H[:, :, :], 0.0), writes=[H])
            S.op("pool", lambda e: e.memset(Hb[:, :, :], 0.0), writes=[Hb])
            order = (ctx_chunks + lat_chunks) if d == 0 else (ctx_chunks[::-1] + lat_chunks[::-1])
            for c in order:
                t0 = c * 128
                tsl = slice(t0, t0 + 128)
                S.dma("sp", xT[:, :, :], xs.h.ap().rearrange("(b p) t -> p b t", p=128)[:, :, tsl], reads=[xs], writes=[xT])
                S.dma("sp", BT[:, :], Bs.h.ap()[:, tsl], reads=[Bs], writes=[BT])
                S.dma("sp", CT[:, :], Cs.h.ap()[:, tsl], reads=[Cs], writes=[CT])
                S.dma("sp", dd[:, 0, :], dts.h.ap()[:, tsl], reads=[dts], writes=[dd])
                S.dma("sp", dd[:, 1, :], das.h.ap()[:, tsl], reads=[das], writes=[dd])
                if d == 1:
                    S.dma("sp", yfs[:, :, :], yf.h.ap().rearrange("(b p) t -> p b t", p=128)[:, :, tsl], reads=[yf], writes=[yfs])
                    S.dma("sp", zc[:, :, :], pre.h.ap()[0:1024, :].rearrange("(b p) t -> p b t", p=128)[:, :, tsl],
                          reads=[pre], writes=[zc])
                for b in range(8):
                    S.op("pe", lambda e: e.transpose(p_xt[:, b * 128:(b + 1) * 128], xT[:, b, :], cstb[:, :]),
                         reads=[xT, cstb], writes=[p_xt])
                S.op("act", lambda e: e.activation(xtok[:, :, :].rearrange("p k q -> p (k q)"), p_xt[:, :], AF.Copy),
                     reads=[p_xt], writes=[xtok])
                S.op("pe", lambda e: e.matmul(p_misc[:, 384:512], BT[:, :], cstb[:, :], start=True, stop=True),
                     reads=[BT, cstb], writes=[p_misc])
                S.op("dve", lambda e: e.tensor_copy(Btok[:, :], p_misc[:, 384:512]), reads=[p_misc], writes=[Btok])
                for j in range(2):
                    S.op("pe", lambda e: e.transpose(p_misc[:, 32 + 32 * j:64 + 32 * j], dd[:, j, :], cst[0:32, 0, 0:32]),
                         reads=[dd, cst], writes=[p_misc])
                S.op("dve", lambda e: e.tensor_copy(dtok[:, :, :].rearrange("p a b -> p (a b)"), p_misc[:, 32:96]),
                     reads=[p_misc], writes=[dtok])
                dk = 16 * d
                S.op("pe", lambda e: e.matmul(p_misc[:, 0:16], tri, dtok[:, 1, dk:dk + 16], start=True, stop=True),
                     reads=[cst, dtok], writes=[p_misc])
                S.op("pe", lambda e: e.matmul(p_misc[:, 16:32], cst[:, 5, :], dtok[:, 1, dk:dk + 16], start=True, stop=True),
                     reads=[cst, dtok], writes=[p_misc])
                S.op("pe", lambda e: e.matmul(p_misc[:, 128:256], BT[:, :], CT[:, :], start=True, stop=True),
                     reads=[BT, CT], writes=[p_misc])
                S.op("dve", lambda e: e.tensor_copy(cum[:, :], p_misc[:, 0:16]), reads=[p_misc], writes=[cum])
                S.op("dve", lambda e: e.tensor_scalar(ncum[:, :], p_misc[:, 0:16], -1.0, None, ALU.mult),
                     reads=[p_misc], writes=[ncum])
                S.op("dve", lambda e: e.tensor_copy(totb[:, :], p_misc[:, 16:32]), reads=[p_misc], writes=[totb])
                S.op("dve", lambda e: e.tensor_copy(cbT[:, :], p_misc[:, 128:256]), reads=[p_misc], writes=[cbT])
                S.op("act", lambda e: e.activation(cd[:, :], totb[:, :], AF.Exp), reads=[totb], writes=[cd])
                S.op("dve", lambda e: e.tensor_tensor(dte[:, :], totb[:, :], cum[:, :], ALU.subtract),
                     reads=[totb, cum], writes=[dte])
                S.op("act", lambda e: e.activation(dte[:, :], dte[:, :], AF.Exp), reads=[dte], writes=[dte])
                S.op("dve", lambda e: e.tensor_tensor(xdt[:, :, :], xtok[:, :, :],
                                                      dtok[:, 0, dk:dk + 16].unsqueeze(2).to_broadcast([128, 16, 64]), ALU.mult),
                     reads=[xtok, dtok], writes=[xdt])
                S.op("pool", lambda e: e.tensor_tensor(xdtw[:, :, :], xdt[:, :, :],
                                                       dte[:, :].unsqueeze(2).to_broadcast([128, 16, 64]), ALU.mult),
                     reads=[xdt, dte], writes=[xdtw])
                for g4 in range(4):
                    for kk in range(4):
                        k = g4 * 4 + kk
                        S.op("pe", lambda e: e.matmul(p_d[:, kk * 128:(kk + 1) * 128],
                                                      dtok[:, 1, dk + k:dk + k + 1].to_broadcast([128, 128]), tri,
                                                      start=True, stop=True), reads=[dtok, cst], writes=[p_d])
                    S.op("act", lambda e: e.activation(Ecum[:, :, :].rearrange("p a b -> p (a b)"), p_d[:, :], AF.Exp),
                         reads=[p_d], writes=[Ecum])
                    S.op("pool", lambda e: e.tensor_tensor(Ck[:, :, :], Ecum[:, :, :],
                                                           CT[:, :].unsqueeze(1).to_broadcast([128, 4, 128]), ALU.mult),
                         reads=[Ecum, CT], writes=[Ck])
                    for kk in range(4):
                        k = g4 * 4 + kk
                        pm = p_m[:, kk * 128:(kk + 1) * 128]
                        S.op("pe", lambda e: e.matmul(pm, dtok[:, 1, dk + k:dk + k + 1].to_broadcast([128, 128]), tri,
                                                      start=True, stop=False), reads=[dtok, cst], writes=[p_m])
                        S.op("pe", lambda e: e.matmul(pm, cst[:, 0, :], negm, start=False, stop=True),
                             reads=[cst], writes=[p_m])
                        E = Ek[k % 2]
                        M = Mk[k % 2]
                        S.op("act", lambda e: e.activation(E[:, :], pm, AF.Exp, bias=ncum[:, k:k + 1]),
                             reads=[p_m, ncum], writes=[E])
                        S.op("dve", lambda e: e.tensor_tensor(M[:, :], E[:, :], cbT[:, :], ALU.mult),
                             reads=[E, cbT], writes=[M])
                        py = p_y[k // 8]
                        b4 = (k // 2) % 4
                        po = (k % 2) * 64
                        yo = py[po:po + 64, b4 * 128:(b4 + 1) * 128]
                        S.op("pe", lambda e: e.matmul(yo, xdt[:, k, :], M[:, :], start=True, stop=False),
                             reads=[xdt, M], writes=[py])
                        S.op("pe", lambda e: e.matmul(yo, Hb[:, k, :], Ck[:, kk, :], start=False, stop=True),
                             reads=[Hb, Ck], writes=[py])
                S.op("dve", lambda e: e.tensor_tensor(Ht[:, :, :], H[:, :, :],
                                                      cd[:, :].unsqueeze(2).to_broadcast([128, 16, 64]), ALU.mult),
                     reads=[H, cd], writes=[Ht])
                for hh in range(2):
                    S.op("pe", lambda e: e.matmul(p_s[:, :], Btok[:, :],
                                                  xdtw[:, 8 * hh:8 * hh + 8, :].rearrange("p k q -> p (k q)"),
                                                  start=True, stop=True), reads=[Btok, xdtw], writes=[p_s])
                    S.op("dve", lambda e: e.tensor_tensor(H[:, 8 * hh:8 * hh + 8, :].rearrange("p k q -> p (k q)"),
                                                          Ht[:, 8 * hh:8 * hh + 8, :].rearrange("p k q -> p (k q)"),
                                                          p_s[:, :], ALU.add), reads=[Ht, p_s], writes=[H])
                S.op("act", lambda e: e.activation(Hb[:, :, :], H[:, :, :], AF.Copy), reads=[H], writes=[Hb])
                if d == 0:
                    for hh in range(2):
                        S.op("act", lambda e: e.activation(ych[:, 4 * hh:4 * hh + 4, :].rearrange("p a b -> p (a b)"),
                                                           p_y[hh][:, :], AF.Copy), reads=[p_y[hh]], writes=[ych])
                    S.dma("sp", yf.h.ap().rearrange("(b p) t -> p b t", p=128)[:, :, tsl], ych[:, :, :], reads=[ych], writes=[yf])
                else:
                    S.op("act", lambda e: e.activation(zc[:, :, :], zc[:, :, :], AF.Silu), reads=[zc], writes=[zc])
                    for b in range(8):
                        S.op("dve", lambda e: e.scalar_tensor_tensor(ych[:, b, :], xT[:, b, :], hp[:, b, 0:1], yfs[:, b, :],
                                                                     ALU.mult, ALU.add), reads=[xT, hp, yfs], writes=[ych])
                    for hh in range(2):
                        sl = ych[:, 4 * hh:4 * hh + 4, :].rearrange("p a b -> p (a b)")
                        S.op("dve", lambda e: e.tensor_tensor(sl, sl, p_y[hh][:, :], ALU.add), reads=[ych, p_y[hh]], writes=[ych])
                    S.op("pool", lambda e: e.tensor_tensor(ych[:, :, :], ych[:, :, :], zc[:, :, :], ALU.mult),
                         reads=[ych, zc], writes=[ych])
                    for b in range(8):
                        S.op("act", lambda e: e.activation(ysq[:, :], ych[:, b, :], AF.Square), reads=[ych], writes=[ysq])
                        S.op("pe", lambda e: e.matmul(p_misc[:, 256:384], cst[:, 5, :], ysq[:, :], start=(b == 0), stop=(b == 7)),
                             reads=[cst, ysq], writes=[p_misc])
                    S.op("dve", lambda e: e.tensor_scalar(rs[:, :], p_misc[:, 256:384], 1.0 / 1024, EPS, ALU.mult, ALU.add),
                         reads=[p_misc], writes=[rs])
                    S.op("act", lambda e: e.activation(rs[:, :], rs[:, :], AF.Sqrt), reads=[rs], writes=[rs])
                    S.op("dve", lambda e: e.reciprocal(rs[:, :], rs[:, :]), reads=[rs], writes=[rs])
                    for b in range(8):
                        S.op("dve", lambda e: e.scalar_tensor_tensor(yob[:, b, :], ych[:, b, :], hp[:, b, 1:2], rs[:, :],
                                                                     ALU.mult, ALU.mult), reads=[ych, hp, rs], writes=[yob])
                    yv = yout.h.ap().rearrange("r (b p) t -> r p b t", p=128)
                    for (r, o, ts, ln) in seq_ranges(t0, 128, TL):
                        S.dma("sp", yv[r][:, :, o:o + ln], yob[:, :, ts - t0:ts - t0 + ln], reads=[yob], writes=[yout])
        S.barrier()
    return nc


def pvec(v):
    v = np.asarray(v, np.float32)
    return np.ascontiguousarray(v.reshape(-1, 128).T)


def ssd_host(inp, j, g):
    w_in = inp["ssd_w_in"][j]
    DI, GN = 8192, 1024
    cols = np.r_[g * 1024:(g + 1) * 1024, DI + g * 1024:DI + (g + 1) * 1024,
                 2 * DI + g * 128:2 * DI + (g + 1) * 128, 2 * DI + GN + g * 128:2 * DI + GN + (g + 1) * 128,
                 2 * DI + 2 * GN + g * 16:2 * DI + 2 * GN + (g + 1) * 16,
                 2 * DI + 2 * GN + 128 + g * 16:2 * DI + 2 * GN + 128 + (g + 1) * 16]
    w = np.zeros((D, SSD_C), np.float32)
    w[:, :len(cols)] = w_in[:, cols]
    cch = np.r_[g * 1024:(g + 1) * 1024, DI + g * 128:DI + (g + 1) * 128, DI + GN + g * 128:DI + GN + (g + 1) * 128]
    cw = inp["ssd_conv_w"][j][:, cch]
    cb = inp["ssd_conv_b"][j][cch]
    cp = np.concatenate([cw, cb[None]], 0)
    convp = np.ascontiguousarray(cp.T.reshape(10, 128, 6).transpose(1, 0, 2))
    dsk = np.repeat(inp["ssd_d"][j][g * 16:(g + 1) * 16], 64)
    nw = inp["ssd_norm"][j][g * 1024:(g + 1) * 1024]
    hp = np.ascontiguousarray(np.stack([pvec(dsk), pvec(nw)], -1))
    dtb = inp["ssd_dt_bias"][j].reshape(2, 8, 16)[:, g].reshape(32)
    alog = inp["ssd_a_log"][j].reshape(2, 8, 16)[:, g].reshape(32)
    dtp = np.ascontiguousarray(np.stack([dtb, alog], -1).astype(np.float32))
    return {"w": w, "convp": convp, "hp": hp, "dtp": dtp, "consts": ssd_consts()}


def gather_tokens(outs, key):
    return [np.ascontiguousarray(np.concatenate([outs[c][key][r] for c in range(NR)], 0)) for r in range(NR)]


ATT_SCALE = 128 ** -0.5


def rope_tables(nlat):
    f32 = np.float32
    t = np.arange(nlat)
    row = (t // 64).astype(f32)
    col = (t % 64).astype(f32)
    n_freq = 32
    inv_freq = (f32(10000.0) ** (-np.arange(n_freq, dtype=f32) / f32(n_freq))).astype(f32)
    ang_r = (row[:, None] * inv_freq[None, :]).astype(f32)
    ang_c = (col[:, None] * inv_freq[None, :]).astype(f32)
    cos = np.zeros((128, nlat), f32)
    sins = np.zeros((128, nlat), f32)
    for d in range(128):
        ang = ang_r if d < 64 else ang_c
        f = d % 32
        cos[d] = np.cos(ang[:, f])
        s = np.sin(ang[:, f])
        sins[d] = -s if (d % 64) < 32 else s
    perm = np.zeros((128, 128), f32)
    for d in range(128):
        partner = d + 32 if (d % 64) < 32 else d - 32
        perm[partner, d] = 1.0
    return cos, sins, perm


def build_attA(TL, lambda_init, want_ctx=True, dbg=False):
    TT, NT = dims(TL)
    NL = NR * TL
    nc, S = new_prog()
    hfull = din(nc, "hfull", [NR, D, TT], BF16)
    wqk = din(nc, "wqk", [D, 1024])
    wv = din(nc, "wv", [D, 512])
    gains = din(nc, "gains", [128, 2])
    grow = din(nc, "grow", [128, 2, 128])
    lamv = din(nc, "lamv", [128, 4, 128])
    subg = din(nc, "subg", [128, 256])
    cosd = din(nc, "cosd", [128, NL])
    sind = din(nc, "sind", [128, NL])
    permd = din(nc, "perm", [128, 128])
    yout = dout(nc, "yout", [NR, 512, TT], BF16)
    mk = dout if dbg else dscr
    pre = mk(nc, "pre", [1024, NT])
    qk = mk(nc, "qk", [1024, NT], BF16)
    Vs = mk(nc, "Vs", [NT, 512], BF16)
    dbgo = dout(nc, "dbgo", [128, 8]) if dbg else None

    proj_all(S, hfull, wqk, 1024, pre, TL)

    with contextlib.ExitStack() as st:
        wvs = S.sb("wvs", [128, KD, 512], BF16, st)
        S.dma("pool", wvs[:, :, :], wv.h.ap().rearrange("(k p) c -> p k c", p=128), reads=[wv], writes=[wvs])
        ht = [S.sb(f"vh{i}", [128, KD, 512], BF16, st) for i in range(2)]
        vo = [S.sb(f"vo{i}", [128, 512], BF16, st) for i in range(2)]
        pv = [S.ps(f"pv{i}", stack=st) for i in range(2)]
        it = 0
        for ti, (t0, n) in enumerate(seq_tiles(TL)):
            h = ht[ti % 2]
            for (r, o, ts, ln) in seq_ranges(t0, n, TL):
                S.dma("sp", h[:, :, ts - t0:ts - t0 + ln],
                      hfull.h.ap()[r].rearrange("(k p) t -> p k t", p=128)[:, :, o:o + ln], reads=[hfull], writes=[h])
            for sub in range(n // 128):
                p = pv[it % 2]
                o_ = vo[it % 2]
                it += 1
                for k in range(KD):
                    S.op("pe", lambda e: e.matmul(p[:, :], h[:, k, sub * 128:(sub + 1) * 128], wvs[:, k, :],
                                                  start=(k == 0), stop=(k == KD - 1)), reads=[h, wvs], writes=[p])
                S.op("act", lambda e: e.activation(o_[:, :], p[:, :], AF.Copy), reads=[p], writes=[o_])
                S.dma("sp", Vs.h.ap()[t0 + sub * 128:t0 + (sub + 1) * 128, :], o_[:, :], reads=[o_], writes=[Vs])
        S.barrier()

    with contextlib.ExitStack() as st:
        gn = S.sb("gn", [128, 2], F32, st)
        pm = S.sb("pm", [128, 128], F32, st)
        ones = S.sb("ones", [128, 128], F32, st)
        S.dma("sp", gn[:, :], gains[:, :], reads=[gains], writes=[gn])
        S.dma("sp", pm[:, :], permd[:, :], reads=[permd], writes=[pm])
        S.op("pool", lambda e: e.memset(ones[:, :], 1.0), writes=[ones])
        cs = [S.sb(f"cs{i}", [128, 512], F32, st) for i in range(2)]
        sn = [S.sb(f"sn{i}", [128, 512], F32, st) for i in range(2)]
        X = [S.sb(f"X{i}", [128, 512], F32, st) for i in range(2)]
        sq = [S.sb(f"sq{i}", [128, 512], F32, st) for i in range(2)]
        rs = [S.sb(f"rs{i}", [128, 512], F32, st) for i in range(2)]
        Xn = [S.sb(f"Xn{i}", [128, 512], F32, st) for i in range(2)]
        R1 = [S.sb(f"R1{i}", [128, 512], F32, st) for i in range(2)]
        ob = [S.sb(f"ob{i}", [128, 512], BF16, st) for i in range(2)]
        pss = [S.ps(f"pss{i}", stack=st) for i in range(2)]
        ppx = [S.ps(f"ppx{i}", stack=st) for i in range(2)]
        it = 0
        for ti, (t0, n) in enumerate(seq_tiles(TL)):
            lat = t0 >= CTXN
            c_, s_ = cs[ti % 2], sn[ti % 2]
            if lat:
                S.dma("sp", c_[:, 0:n], cosd.h.ap()[:, t0 - CTXN:t0 - CTXN + n], reads=[cosd], writes=[c_])
                S.dma("sp", s_[:, 0:n], sind.h.ap()[:, t0 - CTXN:t0 - CTXN + n], reads=[sind], writes=[s_])
            for blk in range(8):
                i2 = it % 2
                it += 1
                x_, q_, r_, xn_, r1_, o_ = X[i2], sq[i2], rs[i2], Xn[i2], R1[i2], ob[i2]
                S.dma("sp", x_[:, 0:n], pre.h.ap()[blk * 128:(blk + 1) * 128, t0:t0 + n], reads=[pre], writes=[x_])
                S.op("act", lambda e: e.activation(q_[:, 0:n], x_[:, 0:n], AF.Square), reads=[x_], writes=[q_])
                S.op("pe", lambda e: e.matmul(pss[i2][:, 0:n], ones[:, :], q_[:, 0:n], start=True, stop=True),
                     reads=[ones, q_], writes=[pss[i2]])
                S.op("dve", lambda e: e.tensor_scalar(r_[:, 0:n], pss[i2][:, 0:n], 1.0 / 128, EPS, ALU.mult, ALU.add),
                     reads=[pss[i2]], writes=[r_])
                S.op("act", lambda e: e.activation(r_[:, 0:n], r_[:, 0:n], AF.Sqrt), reads=[r_], writes=[r_])
                S.op("dve", lambda e: e.reciprocal(r_[:, 0:n], r_[:, 0:n]), reads=[r_], writes=[r_])
                g_ = gn[:, 0:1] if blk < 4 else gn[:, 1:2]
                if lat:
                    S.op("dve", lambda e: e.scalar_tensor_tensor(xn_[:, 0:n], x_[:, 0:n], g_, r_[:, 0:n], ALU.mult, ALU.mult),
                         reads=[x_, gn, r_], writes=[xn_])
                    S.op("pe", lambda e: e.matmul(ppx[i2][:, 0:n], pm[:, :], xn_[:, 0:n], start=True, stop=True),
                         reads=[pm, xn_], writes=[ppx[i2]])
                    S.op("pool", lambda e: e.tensor_tensor(r1_[:, 0:n], xn_[:, 0:n], c_[:, 0:n], ALU.mult),
                         reads=[xn_, c_], writes=[r1_])
                    S.op("dve", lambda e: e.tensor_tensor(xn_[:, 0:n], ppx[i2][:, 0:n], s_[:, 0:n], ALU.mult),
                         reads=[ppx[i2], s_], writes=[xn_])
                    S.op("dve", lambda e: e.tensor_tensor(o_[:, 0:n], r1_[:, 0:n], xn_[:, 0:n], ALU.add),
                         reads=[r1_, xn_], writes=[o_])
                else:
                    S.op("dve", lambda e: e.scalar_tensor_tensor(o_[:, 0:n], x_[:, 0:n], g_, r_[:, 0:n], ALU.mult, ALU.mult),
                         reads=[x_, gn, r_], writes=[o_])
                S.dma("sp", qk.h.ap()[blk * 128:(blk + 1) * 128, t0:t0 + n], o_[:, 0:n], reads=[o_], writes=[qk])
        S.barrier()

    NKT = NT // 128
    with contextlib.ExitStack() as st:
        identb = S.sb("identb", [128, 128], BF16, st)
        identf = S.sb("identf", [128, 128], F32, st)
        S.op("pool", lambda e: e.memset(identf[:, :], 0.0), writes=[identf])
        grw = S.sb("grw", [128, 2, 128], F32, st)
        lv = S.sb("lv", [128, 4, 128], F32, st)
        sg = S.sb("sg", [128, 256], F32, st)
        S.dma("sp", grw[:, :, :], grow[:, :, :], reads=[grow], writes=[grw])
        S.dma("sp", lv[:, :, :], lamv[:, :, :], reads=[lamv], writes=[lv])
        S.dma("sp", sg[:, :], subg[:, :], reads=[subg], writes=[sg])
        pmf = S.sb("pmf", [128, 128], F32, st)
        S.dma("sp", pmf[:, :], permd[:, :], reads=[permd], writes=[pmf])
        pmisc = S.ps("pmisc", stack=st)
        S.op("pe", lambda e: e.matmul(pmisc[:, 0:128], pmf[:, :], pmf[:, :], start=True, stop=True), reads=[pmf], writes=[pmisc])
        S.op("dve", lambda e: e.tensor_copy(identb[:, :], pmisc[:, 0:128]), reads=[pmisc], writes=[identb])
        sm = S.sb("sm", [128, 8], F32, st)
        tmpv = S.sb("tmpv", [128, 128], F32, st)
        for j in range(2):
            S.op("dve", lambda e: e.tensor_reduce(sm[:, j:j + 1], grw[:, j, :], AX.X, ALU.max, apply_absolute_value=True),
                 reads=[grw], writes=[sm])
        S.op("dve", lambda e: e.scalar_tensor_tensor(sm[:, 2:3], sm[:, 0:1], -math.sqrt(128.0), sm[:, 1:2], ALU.mult, ALU.mult),
             reads=[sm], writes=[sm])
        for j in range(2):
            S.op("dve", lambda e: e.tensor_tensor(tmpv[:, :], lv[:, 2 * j, :], lv[:, 2 * j + 1, :], ALU.mult), reads=[lv], writes=[tmpv])
            S.op("dve", lambda e: e.tensor_reduce(sm[:, 3 + j:4 + j], tmpv[:, :], AX.X, ALU.add), reads=[tmpv], writes=[sm])
        S.op("act", lambda e: e.activation(sm[:, 3:5], sm[:, 3:5], AF.Exp), reads=[sm], writes=[sm])
        S.op("dve", lambda e: e.scalar_tensor_tensor(sm[:, 5:6], sm[:, 4:5], -float(lambda_init), sm[:, 3:4], ALU.add, ALU.subtract),
             reads=[sm], writes=[sm])
        S.op("dve", lambda e: e.tensor_scalar(sg[:, :], sg[:, :], 1.0 - float(lambda_init), None, ALU.mult), reads=[sg], writes=[sg])
        negB = sm[:, 2:3]
        neglam = sm[:, 5:6]
        if dbg:
            S.dma("sp", dbgo[:, :], sm[:, :], reads=[sm], writes=[dbgo])

        kT = S.sb("kT", [128, 2, NT], BF16, st)
        Vh = S.sb("Vh", [128, NKT, 257], BF16, st)
        S.op("pool", lambda e: e.memset(Vh[:, :, 256:257], 1.0), writes=[Vh])
        qT = [S.sb(f"qT{i}", [128, 512], BF16, st) for i in range(2)]
        PT = [S.sb(f"PT{i}", [128, 512], BF16, st) for i in range(3)]
        O0 = S.sb("O0", [128, 4, 256], F32, st)
        O1 = S.sb("O1", [128, 256], F32, st)
        rz = S.sb("rz", [128, 8], F32, st)
        junk = S.sb("junk", [128, 256], F32, st)
        onb = S.sb("onb", [128, 256], BF16, st)
        obT = [S.sb(f"obT{i}", [128, 2, 512], BF16, st) for i in range(2)]
        ps_s = [S.ps(f"ps_s{i}", stack=st) for i in range(2)]
        acc = [S.ps(f"acc{i}", stack=st) for i in range(4)]
        ptr = S.ps("ptr", [128, 1024], BF16, st)
        qit = 0
        pit = 0
        yv = yout.h.ap().rearrange("r (hb p) t -> r p hb t", p=128)
        for h in range(2):
            for j in range(2):
                S.dma("sp", kT[:, j, :], qk.h.ap()[512 + (2 * h + j) * 128:512 + (2 * h + j + 1) * 128, :], reads=[qk], writes=[kT])
            S.dma("sp", Vh[:, :, 0:256], Vs.h.ap().rearrange("(kt p) e -> p kt e", p=128)[:, :, h * 256:(h + 1) * 256],
                  reads=[Vs], writes=[Vh])
            qtiles = [(t0, n) for (t0, n) in seq_tiles(TL) if (t0 >= CTXN or want_ctx)]
            for (t0, n) in qtiles:
                lat = t0 >= CTXN
                kts = list(range(NKT)) if lat else list(range(CTXN // 128))
                nqg = n // 128
                obt = obT[qit % 2]
                for j in range(2):
                    q_ = qT[qit % 2]
                    qit += 1
                    S.dma("sp", q_[:, 0:n], qk.h.ap()[(2 * h + j) * 128:(2 * h + j + 1) * 128, t0:t0 + n], reads=[qk], writes=[q_])
                    for ki, kt in enumerate(kts):
                        ps_ = ps_s[pit % 2]
                        pt_ = PT[pit % 3]
                        pit += 1
                        S.op("pe", lambda e: e.matmul(ps_[:, 0:n], kT[:, j, kt * 128:(kt + 1) * 128], q_[:, 0:n], start=True, stop=True),
                             reads=[kT, q_], writes=[ps_])
                        S.op("act", lambda e: e.activation(pt_[:, 0:n], ps_[:, 0:n], AF.Exp, bias=negB, scale=ATT_SCALE),
                             reads=[ps_, sm], writes=[pt_])
                        for qg in range(nqg):
                            S.op("pe", lambda e: e.matmul(acc[qg][:, 0:257], pt_[:, qg * 128:(qg + 1) * 128], Vh[:, kt, :],
                                                          start=(ki == 0), stop=(ki == len(kts) - 1)),
                                 reads=[pt_, Vh], writes=[acc[qg]])
                    for qg in range(nqg):
                        S.op("dve", lambda e: e.reciprocal(rz[:, qg:qg + 1], acc[qg][:, 256:257]), reads=[acc[qg]], writes=[rz])
                        if j == 0:
                            S.op("dve", lambda e: e.tensor_scalar(O0[:, qg, :], acc[qg][:, 0:256], rz[:, qg:qg + 1], None, ALU.mult),
                                 reads=[acc[qg], rz], writes=[O0])
                        else:
                            S.op("dve", lambda e: e.tensor_scalar(O1[:, :], acc[qg][:, 0:256], rz[:, qg:qg + 1], None, ALU.mult),
                                 reads=[acc[qg], rz], writes=[O1])
                            S.op("dve", lambda e: e.scalar_tensor_tensor(O1[:, :], O1[:, :], neglam, O0[:, qg, :], ALU.mult, ALU.add),
                                 reads=[O1, sm, O0], writes=[O1])
                            S.op("act", lambda e: e.activation(junk[:, :], O1[:, :], AF.Square, accum_out=rz[:, 4 + qg:5 + qg]),
                                 reads=[O1], writes=[junk, rz])
                            S.op("dve", lambda e: e.tensor_scalar(rz[:, 4 + qg:5 + qg], rz[:, 4 + qg:5 + qg], 1.0 / 256, EPS, ALU.mult, ALU.add),
                                 reads=[rz], writes=[rz])
                            S.op("act", lambda e: e.activation(rz[:, 4 + qg:5 + qg], rz[:, 4 + qg:5 + qg], AF.Sqrt), reads=[rz], writes=[rz])
                            S.op("dve", lambda e: e.reciprocal(rz[:, 4 + qg:5 + qg], rz[:, 4 + qg:5 + qg]), reads=[rz], writes=[rz])
                            S.op("dve", lambda e: e.scalar_tensor_tensor(onb[:, :], O1[:, :], rz[:, 4 + qg:5 + qg], sg[:, :], ALU.mult, ALU.mult),
                                 reads=[O1, rz, sg], writes=[onb])
                            for eb in range(2):
                                S.op("pe", lambda e: e.transpose(ptr[:, eb * 512 + qg * 128:eb * 512 + (qg + 1) * 128],
                                                                 onb[:, eb * 128:(eb + 1) * 128], identb[:, :]),
                                     reads=[onb, identb], writes=[ptr])
                S.op("act", lambda e: e.activation(obt[:, :, 0:n], ptr[:, :].rearrange("p (a b) -> p a b", a=2)[:, :, 0:n], AF.Copy),
                     reads=[ptr], writes=[obt])
                for (r, o, ts, ln) in seq_ranges(t0, n, TL):
                    S.dma("sp", yv[r][:, 2 * h:2 * h + 2, o:o + ln], obt[:, :, ts - t0:ts - t0 + ln], reads=[obt], writes=[yout])
        S.barrier()
    return nc


def att_host(inp, j, c, TL, lambda_init):
    cols = np.r_[512 * c:512 * (c + 1)]
    wqk = np.ascontiguousarray(np.concatenate([inp["da_w_q"][j][:, cols], inp["da_w_k"][j][:, cols]], 1))
    wv = np.ascontiguousarray(inp["da_w_v"][j][:, cols])
    gq, gk = inp["da_q_norm"][j], inp["da_k_norm"][j]
    gains = np.ascontiguousarray(np.stack([gq, gk], -1).astype(np.float32))
    grow = np.ascontiguousarray(np.broadcast_to(np.stack([gq, gk], 0)[None], (128, 2, 128)).astype(np.float32))
    lamv = np.stack([inp["da_lam_q1"][j], inp["da_lam_k1"][j], inp["da_lam_q2"][j], inp["da_lam_k2"][j]], 0)
    lamv = np.ascontiguousarray(np.broadcast_to(lamv[None], (128, 4, 128)).astype(np.float32))
    subg = np.ascontiguousarray(np.broadcast_to(inp["da_sub_norm"][j][None], (128, 256)).astype(np.float32))
    cos, sins, perm = rope_tables(NR * TL)
    return {"wqk": wqk, "wv": wv, "gains": gains, "grow": grow, "lamv": lamv, "subg": subg,
            "cosd": cos, "sind": sins, "perm": perm}


S5W = 128
TWO_PI = 2.0 * math.pi


I32 = mybir.dt.int32
CW1 = 6.28125
CW2 = TWO_PI - 6.28125


def emit_sin(S, outT, out_ap, y, yap, ki, kiap, kf, kfap):
    S.op("dve", lambda e: e.tensor_scalar(kfap, yap, 1.0 / TWO_PI, None, ALU.mult), reads=[y], writes=[kf])
    S.op("dve", lambda e: e.tensor_copy(kiap, kfap), reads=[kf], writes=[ki])
    S.op("dve", lambda e: e.tensor_copy(kfap, kiap), reads=[ki], writes=[kf])
    S.op("dve", lambda e: e.scalar_tensor_tensor(yap, kfap, -CW1, yap, ALU.mult, ALU.add), reads=[kf, y], writes=[y])
    S.op("dve", lambda e: e.scalar_tensor_tensor(yap, kfap, -CW2, yap, ALU.mult, ALU.add), reads=[kf, y], writes=[y])
    S.op("dve", lambda e: e.tensor_scalar(kfap, yap, math.pi, None, ALU.is_gt), reads=[y], writes=[kf])
    S.op("dve", lambda e: e.scalar_tensor_tensor(yap, kfap, -TWO_PI, yap, ALU.mult, ALU.add), reads=[kf, y], writes=[y])
    S.op("dve", lambda e: e.tensor_scalar(kfap, yap, -math.pi, None, ALU.is_lt), reads=[y], writes=[kf])
    S.op("dve", lambda e: e.scalar_tensor_tensor(yap, kfap, TWO_PI, yap, ALU.mult, ALU.add), reads=[kf, y], writes=[y])
    S.op("dve", lambda e: e.tensor_scalar(yap, yap, -math.pi, math.pi, ALU.max, ALU.min), reads=[y], writes=[y])
    S.op("act", lambda e: e.activation(out_ap, yap, AF.Sin), reads=[y], writes=[outT])


def build_s5A(TL, dbg=False):
    TT, NT = dims(TL)
    W = S5W
    NWIN = NT // W
    nc, S = new_prog()
    hs5 = din(nc, "hs5", [NR, 512, TT], BF16)
    lamd = din(nc, "lam", [128, 2, 2, 16])
    stpd = din(nc, "stp", [128, 2, 16])
    bblkd = din(nc, "bblk", [128, 2, 16, 128])
    cblkd = din(nc, "cblk", [128, 2, 2, 16, 128])
    dskd = din(nc, "dsk", [128, 4])
    jrowd = din(nc, "jrow", [128, W])
    yout = dout(nc, "yout", [NR, 512, TT], BF16)
    yfd = dscr(nc, "yfd", [512, NT])
    dbgo = dout(nc, "dbgo", [128, 8, 16]) if dbg else None

    u = S.sb("u", [128, 4, NT], BF16)
    lam = S.sb("lam", [128, 2, 2, 16], F32)
    stp = S.sb("stp", [128, 2, 16], F32)
    bblkb = S.sb("bblkb", [128, 2, 16, 128], BF16)
    cblkb = S.sb("cblkb", [128, 2, 2, 16, 128], BF16)
    dsk = S.sb("dsk", [128, 4], F32)
    jrow = S.sb("jrow", [128, W], F32)
    for (dst, src) in ((lam, lamd), (stp, stpd), (dsk, dskd), (jrow, jrowd)):
        S.dma("sp", dst[tuple(slice(None) for _ in dst.h.shape)], src.h.ap(), reads=[src], writes=[dst])
    S.dma("pool", bblkb[:, :, :, :], bblkd.h.ap(), reads=[bblkd], writes=[bblkb])
    for d in range(2):
        S.dma("pool", cblkb[:, d, :, :, :], cblkd.h.ap()[:, d], reads=[cblkd], writes=[cblkb])
    uv = hs5.h.ap().rearrange("r (f p) t -> r p f t", p=128)
    for r in range(NR):
        S.dma("sp", u[:, :, TCTX * r:TCTX * (r + 1)], uv[r][:, :, 0:TCTX], reads=[hs5], writes=[u])
        S.dma("sp", u[:, :, CTXN + TL * r:CTXN + TL * (r + 1)], uv[r][:, :, TCTX:TT], reads=[hs5], writes=[u])
    for d in range(2):
        S.op("dve", lambda e: e.tensor_scalar(cblkb[:, d, 1, :, :], cblkb[:, d, 1, :, :], -1.0, None, ALU.mult),
             reads=[cblkb], writes=[cblkb])

    pki = S.sb("pki", [128, 16], I32)
    pp = S.sb("pp", [128, 2, 12, 16], F32)
    for d in range(2):
        P_ = lambda i: pp[:, d, i, :]
        lr, li = lam[:, 0, d, :], lam[:, 1, d, :]
        S.op("act", lambda e: e.activation(P_(0), stp[:, d, :], AF.Exp), reads=[stp], writes=[pp])
        S.op("dve", lambda e: e.tensor_tensor(P_(10), lr, P_(0), ALU.mult), reads=[lam, pp], writes=[pp])
        S.op("act", lambda e: e.activation(P_(1), P_(10), AF.Exp), reads=[pp], writes=[pp])
        S.op("dve", lambda e: e.tensor_tensor(P_(2), li, P_(0), ALU.mult), reads=[lam, pp], writes=[pp])
        S.op("dve", lambda e: e.tensor_copy(P_(10), P_(2)), reads=[pp], writes=[pp])
        emit_sin(S, pp, P_(4), pp, P_(10), pki, pki[:, :], pp, P_(11))
        S.op("dve", lambda e: e.tensor_scalar(P_(10), P_(2), 0.5 * math.pi, None, ALU.add), reads=[pp], writes=[pp])
        emit_sin(S, pp, P_(3), pp, P_(10), pki, pki[:, :], pp, P_(11))
        S.op("dve", lambda e: e.tensor_tensor(P_(5), P_(1), P_(3), ALU.mult), reads=[pp], writes=[pp])
        S.op("dve", lambda e: e.tensor_scalar(P_(5), P_(5), -1.0, None, ALU.add), reads=[pp], writes=[pp])
        S.op("dve", lambda e: e.tensor_tensor(P_(6), P_(1), P_(4), ALU.mult), reads=[pp], writes=[pp])
        S.op("dve", lambda e: e.tensor_tensor(P_(7), lr, lr, ALU.mult), reads=[lam], writes=[pp])
        S.op("dve", lambda e: e.tensor_tensor(P_(10), li, li, ALU.mult), reads=[lam], writes=[pp])
        S.op("dve", lambda e: e.tensor_tensor(P_(7), P_(7), P_(10), ALU.add), reads=[pp], writes=[pp])
        S.op("dve", lambda e: e.reciprocal(P_(7), P_(7)), reads=[pp], writes=[pp])
        S.op("dve", lambda e: e.tensor_tensor(P_(8), P_(5), lr, ALU.mult), reads=[pp, lam], writes=[pp])
        S.op("dve", lambda e: e.tensor_tensor(P_(10), P_(6), li, ALU.mult), reads=[pp, lam], writes=[pp])
        S.op("dve", lambda e: e.tensor_tensor(P_(8), P_(8), P_(10), ALU.add), reads=[pp], writes=[pp])
        S.op("dve", lambda e: e.tensor_tensor(P_(8), P_(8), P_(7), ALU.mult), reads=[pp], writes=[pp])
        S.op("dve", lambda e: e.tensor_tensor(P_(9), P_(6), lr, ALU.mult), reads=[pp, lam], writes=[pp])
        S.op("dve", lambda e: e.tensor_tensor(P_(10), P_(5), li, ALU.mult), reads=[pp, lam], writes=[pp])
        S.op("dve", lambda e: e.tensor_tensor(P_(9), P_(9), P_(10), ALU.subtract), reads=[pp], writes=[pp])
        S.op("dve", lambda e: e.tensor_tensor(P_(9), P_(9), P_(7), ALU.mult), reads=[pp], writes=[pp])
    if dbg:
        S.dma("sp", dbgo[:, :, :], pp[:, 0, 0:8, :], reads=[pp], writes=[dbgo])

    cosT = S.sb("cosT", [128, 16, W], F32)
    sinT = S.sb("sinT", [128, 16, W], F32)
    TrT = S.sb("TrT", [128, 16, W], F32)
    TiT = S.sb("TiT", [128, 16, W], F32)
    rB = S.sb("rB", [128, 16, W], F32)
    ang = S.sb("ang", [128, W], F32)
    angi = S.sb("angi", [128, W], I32)
    car = S.sb("car", [128, 2, 16], F32)
    t1 = [S.sb(f"t1{i}", [128, W], F32) for i in range(1)]
    G4 = 4 * W
    ta = [S.sb(f"ta{i}", [128, G4], F32) for i in range(2)]
    tb = [S.sb(f"tb{i}", [128, G4], F32) for i in range(2)]
    tc_ = [S.sb(f"tc{i}", [128, G4], F32) for i in range(2)]
    td = [S.sb(f"td{i}", [128, G4], F32) for i in range(2)]
    Xr = [S.sb(f"Xr{i}", [128, 4, W], F32) for i in range(2)]
    Xi = [S.sb(f"Xi{i}", [128, 4, W], F32) for i in range(2)]
    sr = [S.sb(f"sr{i}", [128, 4, W], F32) for i in range(2)]
    si = [S.sb(f"si{i}", [128, 4, W], F32) for i in range(2)]
    srb = [S.sb(f"srb{i}", [128, 4, W], BF16) for i in range(2)]
    sib = [S.sb(f"sib{i}", [128, 4, W], BF16) for i in range(2)]
    yfs = S.sb("yfs", [128, 4, W], F32)
    ych = S.sb("ych", [128, 4, W], F32)
    g1 = S.sb("g1", [128, 4, W], F32)
    yob = S.sb("yob", [128, 4, W], BF16)
    pre_ = [S.ps(f"pre{i}") for i in range(2)]
    pim_ = [S.ps(f"pim{i}") for i in range(2)]
    py = [S.ps(f"py{i}") for i in range(2)]
    yfv = yfd.h.ap().rearrange("(f p) t -> p f t", p=128)
    yv = yout.h.ap().rearrange("r (f p) t -> r p f t", p=128)
    it = 0
    for d in range(2):
        for sb in range(16):
            th = pp[:, d, 2, sb:sb + 1]
            S.op("dve", lambda e: e.tensor_scalar(ang[:, :], jrow[:, :], th, None, ALU.mult), reads=[jrow, pp], writes=[ang])
            emit_sin(S, sinT, sinT[:, sb, :], ang, ang[:, :], angi, angi[:, :], t1[0], t1[0][:, :])
            S.op("dve", lambda e: e.tensor_scalar(ang[:, :], jrow[:, :], th, 0.5 * math.pi, ALU.mult, ALU.add), reads=[jrow, pp], writes=[ang])
            emit_sin(S, cosT, cosT[:, sb, :], ang, ang[:, :], angi, angi[:, :], t1[0], t1[0][:, :])
            kr, ki = pp[:, d, 8, sb:sb + 1], pp[:, d, 9, sb:sb + 1]
            S.op("dve", lambda e: e.tensor_scalar(TrT[:, sb, :], sinT[:, sb, :], ki, None, ALU.mult), reads=[sinT, pp], writes=[TrT])
            S.op("dve", lambda e: e.scalar_tensor_tensor(TrT[:, sb, :], cosT[:, sb, :], kr, TrT[:, sb, :], ALU.mult, ALU.add),
                 reads=[cosT, pp, TrT], writes=[TrT])
            S.op("dve", lambda e: e.tensor_scalar(TiT[:, sb, :], sinT[:, sb, :], kr, -1.0, ALU.mult, ALU.mult), reads=[sinT, pp], writes=[TiT])
            S.op("dve", lambda e: e.scalar_tensor_tensor(TiT[:, sb, :], cosT[:, sb, :], ki, TiT[:, sb, :], ALU.mult, ALU.add),
                 reads=[cosT, pp, TiT], writes=[TiT])
            S.op("pool", lambda e: e.memset(rB[:, sb, :], 0.0), writes=[rB])
            S.op("dve", lambda e: e.tensor_scalar(rB[:, sb, :], rB[:, sb, :], pp[:, d, 1, sb:sb + 1], None, ALU.add), reads=[rB, pp], writes=[rB])
        S.op("pool", lambda e: e.memset(car[:, :, :], 0.0), writes=[car])
        nwc = CTXN // W
        wins = list(range(NWIN)) if d == 0 else (list(range(nwc))[::-1] + list(range(nwc, NWIN))[::-1])
        for wi in wins:
            t0 = wi * W
            if d == 1:
                S.dma("sp", yfs[:, :, :], yfv[:, :, t0:t0 + W], reads=[yfd], writes=[yfs])
            pyt = py[wi % 2]
            for fb in range(4):
                i2 = it % 2
                it += 1
                sbs = slice(4 * fb, 4 * fb + 4)
                if d == 0:
                    urhs = u[:, fb, t0:t0 + W]
                else:
                    urhs = u[:, fb, t0:t0 + W][:, ::-1]
                pr, pi_ = pre_[i2], pim_[i2]
                for s4 in range(4):
                    sb = 4 * fb + s4
                    S.op("pe", lambda e: e.matmul(pr[:, s4 * W:(s4 + 1) * W], bblkb[:, 0, sb, :], urhs, start=True, stop=True),
                         reads=[bblkb, u], writes=[pr])
                    S.op("pe", lambda e: e.matmul(pi_[:, s4 * W:(s4 + 1) * W], bblkb[:, 1, sb, :], urhs, start=True, stop=True),
                         reads=[bblkb, u], writes=[pi_])
                a, b, c_, d_ = ta[i2], tb[i2], tc_[i2], td[i2]
                xr_, xi_ = Xr[i2], Xi[i2]
                Trv = TrT[:, sbs, :].rearrange("p a w -> p (a w)")
                Tiv = TiT[:, sbs, :].rearrange("p a w -> p (a w)")
                cov = cosT[:, sbs, :].rearrange("p a w -> p (a w)")
                siv = sinT[:, sbs, :].rearrange("p a w -> p (a w)")
                xrv = xr_[:, :, :].rearrange("p a w -> p (a w)")
                xiv = xi_[:, :, :].rearrange("p a w -> p (a w)")
                S.op("dve", lambda e: e.tensor_tensor(a[:, :], pr[:, :], Trv, ALU.mult), reads=[pr, TrT], writes=[a])
                S.op("dve", lambda e: e.tensor_tensor(b[:, :], pi_[:, :], Tiv, ALU.mult), reads=[pi_, TiT], writes=[b])
                S.op("pool", lambda e: e.tensor_tensor(xrv, a[:, :], b[:, :], ALU.subtract), reads=[a, b], writes=[xr_])
                S.op("dve", lambda e: e.tensor_tensor(c_[:, :], pr[:, :], Tiv, ALU.mult), reads=[pr, TiT], writes=[c_])
                S.op("dve", lambda e: e.tensor_tensor(d_[:, :], pi_[:, :], Trv, ALU.mult), reads=[pi_, TrT], writes=[d_])
                S.op("pool", lambda e: e.tensor_tensor(xiv, c_[:, :], d_[:, :], ALU.add), reads=[c_, d_], writes=[xi_])
                for s4 in range(4):
                    sb = 4 * fb + s4
                    S.op("dve", lambda e: e.tensor_tensor_scan(xr_[:, s4, :], rB[:, sb, :], xr_[:, s4, :], car[:, 0, sb:sb + 1], ALU.mult, ALU.add),
                         reads=[rB, xr_, car], writes=[xr_])
                    S.op("dve", lambda e: e.tensor_tensor_scan(xi_[:, s4, :], rB[:, sb, :], xi_[:, s4, :], car[:, 1, sb:sb + 1], ALU.mult, ALU.add),
                         reads=[rB, xi_, car], writes=[xi_])
                s_r, s_i = sr[i2], si[i2]
                srv = s_r[:, :, :].rearrange("p a w -> p (a w)")
                siv2 = s_i[:, :, :].rearrange("p a w -> p (a w)")
                S.op("pool", lambda e: e.tensor_tensor(a[:, :], xrv, cov, ALU.mult), reads=[xr_, cosT], writes=[a])
                S.op("pool", lambda e: e.tensor_tensor(b[:, :], xiv, siv, ALU.mult), reads=[xi_, sinT], writes=[b])
                S.op("dve", lambda e: e.tensor_tensor(srv, a[:, :], b[:, :], ALU.subtract), reads=[a, b], writes=[s_r])
                S.op("pool", lambda e: e.tensor_tensor(c_[:, :], xrv, siv, ALU.mult), reads=[xr_, sinT], writes=[c_])
                S.op("pool", lambda e: e.tensor_tensor(d_[:, :], xiv, cov, ALU.mult), reads=[xi_, cosT], writes=[d_])
                S.op("dve", lambda e: e.tensor_tensor(siv2, c_[:, :], d_[:, :], ALU.add), reads=[c_, d_], writes=[s_i])
                S.op("act", lambda e: e.activation(srb[i2][:, :, :], s_r[:, :, :], AF.Copy), reads=[s_r], writes=[srb[i2]])
                S.op("act", lambda e: e.activation(sib[i2][:, :, :], s_i[:, :, :], AF.Copy), reads=[s_i], writes=[sib[i2]])
                S.op("act", lambda e: e.activation(car[:, 0, sbs], s_r[:, :, W - 1], AF.Copy), reads=[s_r], writes=[car])
                S.op("act", lambda e: e.activation(car[:, 1, sbs], s_i[:, :, W - 1], AF.Copy), reads=[s_i], writes=[car])
                yo = pyt[:, fb * W:(fb + 1) * W]
                for s4 in range(4):
                    sb = 4 * fb + s4
                    S.op("pe", lambda e: e.matmul(yo, cblkb[:, d, 0, sb, :], srb[i2][:, s4, :], start=(s4 == 0), stop=False),
                         reads=[cblkb, srb[i2]], writes=[pyt])
                    S.op("pe", lambda e: e.matmul(yo, cblkb[:, d, 1, sb, :], sib[i2][:, s4, :], start=False, stop=(s4 == 3)),
                         reads=[cblkb, sib[i2]], writes=[pyt])
            if d == 0:
                S.op("act", lambda e: e.activation(ych[:, :, :].rearrange("p f w -> p (f w)"), pyt[:, :], AF.Copy), reads=[pyt], writes=[ych])
                S.dma("sp", yfv[:, :, t0:t0 + W], ych[:, :, :], reads=[ych], writes=[yfd])
            else:
                for fb in range(4):
                    S.op("dve", lambda e: e.tensor_tensor(ych[:, fb, :], pyt[:, fb * W:(fb + 1) * W][:, ::-1], yfs[:, fb, :], ALU.add),
                         reads=[pyt, yfs], writes=[ych])
                    S.op("dve", lambda e: e.scalar_tensor_tensor(ych[:, fb, :], u[:, fb, t0:t0 + W], dsk[:, fb:fb + 1], ych[:, fb, :],
                                                                 ALU.mult, ALU.add), reads=[u, dsk, ych], writes=[ych])
                S.op("act", lambda e: e.activation(g1[:, :, :], ych[:, :, :], AF.Square), reads=[ych], writes=[g1])
                S.op("dve", lambda e: e.tensor_scalar(g1[:, :, :], g1[:, :, :], 0.044715, 1.0, ALU.mult, ALU.add), reads=[g1], writes=[g1])
                S.op("dve", lambda e: e.tensor_tensor(g1[:, :, :], g1[:, :, :], ych[:, :, :], ALU.mult), reads=[g1, ych], writes=[g1])
                S.op("act", lambda e: e.activation(g1[:, :, :], g1[:, :, :], AF.Sigmoid, scale=1.5957691216057308), reads=[g1], writes=[g1])
                S.op("dve", lambda e: e.tensor_tensor(yob[:, :, :], g1[:, :, :], ych[:, :, :], ALU.mult), reads=[g1, ych], writes=[yob])
                for (r, o, ts, ln) in seq_ranges(t0, W, TL):
                    S.dma("sp", yv[r][:, :, o:o + ln], yob[:, :, ts - t0:ts - t0 + ln], reads=[yob], writes=[yout])
    S.barrier()
    return nc


def s5_host(inp, j, b, TL):
    gs = slice(32 * b, 32 * (b + 1))
    def st_layout(a):
        return np.ascontiguousarray(a.reshape(16, 2, 64).transpose(1, 2, 0).reshape(128, 16))
    lam = np.zeros((128, 2, 2, 16), np.float32)
    stp = np.zeros((128, 2, 16), np.float32)
    for d in range(2):
        lam[:, 0, d] = st_layout(inp["s5_lam_re"][j][d, gs])
        lam[:, 1, d] = st_layout(inp["s5_lam_im"][j][d, gs])
        stp[:, d] = st_layout(np.broadcast_to(inp["s5_log_step"][j][d, gs][:, None], (32, 64)))
    bblk = np.zeros((128, 2, 16, 128), np.float32)
    for ri, key in enumerate(("s5_b_re", "s5_b_im")):
        B = inp[key][j][gs]
        for sb in range(16):
            for gg in range(2):
                g = 2 * sb + gg
                rows = 32 * (sb % 4) + 16 * gg
                bblk[rows:rows + 16, ri, sb, 64 * gg:64 * gg + 64] = B[g].T
    cblk = np.zeros((128, 2, 2, 16, 128), np.float32)
    for ri, key in enumerate(("s5_c_re", "s5_c_im")):
        for d in range(2):
            C = inp[key][j][d, gs]
            for sb in range(16):
                for gg in range(2):
                    g = 2 * sb + gg
                    co = 32 * (sb % 4) + 16 * gg
                    cblk[64 * gg:64 * gg + 64, d, ri, sb, co:co + 16] = C[g].T
    dsk = pvec(inp["s5_d"][j][512 * b:512 * (b + 1)])
    jrow = np.ascontiguousarray(np.broadcast_to(np.arange(1, S5W + 1, dtype=np.float32)[None], (128, S5W)))
    return {"lam": lam, "stp": stp, "bblk": bblk, "cblk": cblk, "dsk": dsk, "jrow": jrow}


_PROGS = {}
TIMES = []


def _prog(key, fn):
    if key not in _PROGS:
        _PROGS[key] = fn()
    return _PROGS[key]


def _run(nc, maps):
    import time
    t0 = time.time()
    res = run_bass_kernel_spmd(nc, maps, core_ids=list(range(NR)))
    TIMES.append(time.time() - t0)
    return res.results


def ffn_host(inp, i, c):
    cols = np.r_[512 * c:512 * c + 512, D + 512 * c:D + 512 * c + 512]
    wc = np.ascontiguousarray(inp["ffn_up"][i][:, cols])
    cp = np.concatenate([inp["ffn_conv_w"][i][:, cols], inp["ffn_conv_b"][i][None, cols]], 0)
    cp = np.ascontiguousarray(cp.T.reshape(8, 128, 4).transpose(1, 0, 2))
    return {"w": wc, "convp": cp}


def make_vec(g, gain, scale, shift, ba=None, bb=None):
    v = np.zeros((128, NVEC, KD), np.float32)
    if g is not None:
        v[:, 0], v[:, 1] = pvec(g[:, 0]), pvec(g[:, 1])
    if gain is not None:
        v[:, 2] = pvec(gain)
        v[:, 3], v[:, 4] = pvec(scale[:, 0]), pvec(scale[:, 1])
        v[:, 5], v[:, 6] = pvec(shift[:, 0]), pvec(shift[:, 1])
    if ba is not None:
        v[:, 7], v[:, 8] = pvec(ba), pvec(bb)
    return v


def forward(inp, TL, depth=4, x=None, ctx=None):
    TT, NT = dims(TL)
    x = inp["x"][0] if x is None else x
    ctx = inp["ctx"][0] if ctx is None else ctx
    cvec = np.stack([inp["c_ctx"], inp["c"][0]], -1).astype(np.float32)
    cvec = np.ascontiguousarray(cvec.reshape(KD, 128, 2).transpose(1, 0, 2))
    ncm = _prog(("mod", depth), lambda: build_mod(depth))
    maps = []
    for c in range(NR):
        cs = slice(MODC * c, MODC * (c + 1))
        modb = inp["mod_b"][:depth, cs]
        maps.append({"cvec": cvec, "modw": np.ascontiguousarray(inp["mod_w"][:depth, :, cs]),
                     "modb": np.ascontiguousarray(modb.reshape(depth, MODB, 128).transpose(2, 0, 1))})
    res = _run(ncm, maps)
    mod = np.zeros((depth, 6 * D, 2), np.float32)
    for c in range(NR):
        o = res[c]["modo"]
        mod[:, MODC * c:MODC * (c + 1), :] = o.transpose(1, 2, 0, 3).reshape(depth, MODC, 2)
    mods = mod.reshape(depth, 6, D, 2)

    xs = [np.ascontiguousarray(np.concatenate([ctx[TCTX * r:TCTX * (r + 1)], x[TL * r:TL * (r + 1)]], 0).T) for r in range(NR)]
    ncn = _prog(("B", 0, False, TL, False, True), lambda: build_B(128, False, TL, gemm=False, do_norm=True))
    vec = make_vec(None, inp["norm_mix"][0], mods[0, 1], mods[0, 0])
    res = _run(ncn, [{"xres": xs[r], "vec": vec} for r in range(NR)])
    hs = [res[r]["hnext"] for r in range(NR)]

    for i in range(depth):
        kind, j = i % 3, i // 3
        hfull = np.ascontiguousarray(np.stack(hs, 0))
        want_ctx = i < depth - 1
        if kind == 0:
            nca = _prog(("ssdA", TL), lambda: build_ssdA(TL))
            outs = _run(nca, [dict(ssd_host(inp, j, c), hfull=hfull) for c in range(NR)])
            Kb, glu, wB = 8192, False, inp["ssd_w_out"][j]
            ba = bb = None
        elif kind == 1:
            nca = _prog(("s5A", TL), lambda: build_s5A(TL))
            outs = _run(nca, [dict(s5_host(inp, j, c, TL), hs5=np.ascontiguousarray(hfull[:, 512 * c:512 * (c + 1), :]))
                              for c in range(NR)])
            Kb, glu, wB = 4096, True, inp["s5_glu_w"][j]
            ba, bb = inp["s5_glu_b"][j][:D], inp["s5_glu_b"][j][D:]
        else:
            li = 0.8 - 0.6 * math.exp(-0.3 * i)
            nca = _prog(("attA", TL, i, want_ctx), lambda: build_attA(TL, li, want_ctx))
            outs = _run(nca, [dict(att_host(inp, j, c, TL, li), hfull=hfull) for c in range(NR)])
            Kb, glu, wB = 4096, False, inp["da_w_o"][j]
            ba = bb = None
        yin = gather_tokens(outs, "yout")
        ncb = _prog(("B", Kb, glu, TL, True, True), lambda: build_B(Kb, glu, TL))
        vec = make_vec(mods[i, 2], inp["norm_ffn"][i], mods[i, 4], mods[i, 3], ba, bb)
        wB = np.ascontiguousarray(wB)
        res = _run(ncb, [{"xres": xs[r], "vec": vec, "yin": yin[r], "w": wB} for r in range(NR)])
        xs = [res[r]["xnew"] for r in range(NR)]
        hs = [res[r]["hnext"] for r in range(NR)]
        hfull = np.ascontiguousarray(np.stack(hs, 0))
        ncf = _prog(("ffnA", TL), lambda: build_ffnA(TL))
        outs = _run(ncf, [dict(ffn_host(inp, i, c), hfull=hfull) for c in range(NR)])
        yin = gather_tokens(outs, "yout")
        last = i == depth - 1
        ncb = _prog(("B", 4096, False, TL, True, not last), lambda: build_B(4096, False, TL, do_norm=not last))
        if last:
            vec = make_vec(mods[i, 5], None, None, None)
        else:
            vec = make_vec(mods[i, 5], inp["norm_mix"][i + 1], mods[i + 1, 1], mods[i + 1, 0])
        wB = np.ascontiguousarray(inp["ffn_down"][i])
        res = _run(ncb, [{"xres": xs[r], "vec": vec, "yin": yin[r], "w": wB} for r in range(NR)])
        xs = [res[r]["xnew"] for r in range(NR)]
        if not last:
            hs = [res[r]["hnext"] for r in range(NR)]
    out = np.concatenate([xs[r][:, TCTX:].T for r in range(NR)], 0)
    ctx_out = np.concatenate([xs[r][:, :TCTX].T for r in range(NR)], 0)
    return np.ascontiguousarray(out[None]).astype(np.float32), ctx_out


def kernel(**inputs):
    inp = {k: np.asarray(v) for k, v in inputs.items()}
    out, _ = forward(inp, TLAT)
    import sys
    print("launch wall times (s):", [round(t, 1) for t in TIMES], file=sys.stderr)
    return out
```
